# Optimizing a Trainium2 kernel written in Bass

```python
import math
import jax, jax.numpy as jnp
from jax import lax
import numpy as np

D_MODEL = 2048
BATCH = 4
SEQ = 4096
DEPTH = 1
DEC_BATCH = 16
DEC_SEQ = 2048
PAST_LEN = 128

SSM_WIDTH = 1024
SSM_GROUP = 16
SSM_GROUPS = SSM_WIDTH // SSM_GROUP
SSM_STATE = 64
ATT_HEADS = 4
ATT_HEAD_DIM = 128
ATT_V_DIM = 2 * ATT_HEAD_DIM
ATT_QK_WIDTH = ATT_HEADS * 2 * ATT_HEAD_DIM
ATT_WIDTH = ATT_HEADS * ATT_V_DIM
Q_BLOCK = 128
NUM_BUCKETS = 32
MAX_DISTANCE = 128
PLE_DIM = 256
IN_WIDTH = 2 * SSM_WIDTH + 2 * ATT_QK_WIDTH + 2 * ATT_WIDTH + 2 * D_MODEL
EPS = 1e-6

kernel_name = "hybrid_s5_diffattn_encoder"


def rmsnorm(x, g):
    xf = x.astype(jnp.float32)
    y = xf * lax.rsqrt(jnp.mean(xf * xf, axis=-1, keepdims=True) + EPS) * g.astype(jnp.float32)
    return y.astype(x.dtype)


def rel_bucket(rel):
    half = NUM_BUCKETS // 2
    max_exact = half // 2
    ret = (rel > 0).astype(jnp.int32) * half
    n = jnp.abs(rel)
    nf = jnp.maximum(n, 1).astype(jnp.float32)
    large = max_exact + (jnp.log(nf / max_exact) / math.log(MAX_DISTANCE / max_exact)
                         * (half - max_exact)).astype(jnp.int32)
    large = jnp.minimum(large, half - 1)
    return ret + jnp.where(n < max_exact, n, large)


def _scan_combine(e1, e2):
    a1r, a1i, b1r, b1i = e1
    a2r, a2i, b2r, b2i = e2
    ar = a1r * a2r - a1i * a2i
    ai = a1r * a2i + a1i * a2r
    br = a2r * b1r - a2i * b1i + b2r
    bi = a2r * b1i + a2i * b1r + b2i
    return ar, ai, br, bi


def s5_direction(u, lam_re, lam_im, log_dt, b_re, b_im, c_re, c_im, reverse):
    lam_re = lam_re.astype(jnp.float32)
    lam_im = lam_im.astype(jnp.float32)
    dt = jnp.exp(log_dt.astype(jnp.float32))[:, None]
    mag = jnp.exp(lam_re * dt)
    ab_re = mag * jnp.cos(lam_im * dt)
    ab_im = mag * jnp.sin(lam_im * dt)
    den = lam_re * lam_re + lam_im * lam_im
    nr = ab_re - 1.0
    ni = ab_im
    k_re = (nr * lam_re + ni * lam_im) / den
    k_im = (ni * lam_re - nr * lam_im) / den
    b_re = b_re.astype(jnp.float32)
    b_im = b_im.astype(jnp.float32)
    bb_re = k_re[..., None] * b_re - k_im[..., None] * b_im
    bb_im = k_re[..., None] * b_im + k_im[..., None] * b_re
    if reverse:
        u = jnp.flip(u, axis=1)
    br = jnp.einsum('blgc,gnc->lbgn', u, bb_re)
    bi = jnp.einsum('blgc,gnc->lbgn', u, bb_im)
    l = u.shape[1]
    ar = jnp.broadcast_to(ab_re, (l, 1) + ab_re.shape)
    ai = jnp.broadcast_to(ab_im, (l, 1) + ab_im.shape)
    _, _, hr, hi = lax.associative_scan(_scan_combine, (ar, ai, br, bi), axis=0)
    y = (jnp.einsum('lbgn,gcn->blgc', hr, c_re.astype(jnp.float32))
         - jnp.einsum('lbgn,gcn->blgc', hi, c_im.astype(jnp.float32)))
    if reverse:
        y = jnp.flip(y, axis=1)
    return y


def diff_attention(q, k, v, rel_bias, lam):
    b, l = q.shape[0], q.shape[1]
    nb = l // Q_BLOCK
    scale = ATT_HEAD_DIM ** -0.5
    qb = q.reshape(b, nb, Q_BLOCK, ATT_HEADS, 2, ATT_HEAD_DIM).swapaxes(0, 1)
    starts = jnp.arange(nb, dtype=jnp.int32) * Q_BLOCK
    kpos = jnp.arange(l, dtype=jnp.int32)
    vf = v.astype(jnp.float32)

    def block(args):
        qi, start = args
        qpos = start + jnp.arange(Q_BLOCK, dtype=jnp.int32)
        bias = rel_bias[rel_bucket(kpos[None, :] - qpos[:, None])].astype(jnp.float32)
        bias = jnp.transpose(bias, (2, 0, 1))
        s = jnp.einsum('bqhcd,bkhcd->bhcqk', qi, k).astype(jnp.float32) * scale + bias[None, :, None]
        pr = jax.nn.softmax(s, axis=-1)
        w = pr[:, :, 0] - lam * pr[:, :, 1]
        return jnp.einsum('bhqk,bkhe->bqhe', w, vf)

    o = lax.map(block, (qb, starts))
    return o.swapaxes(0, 1).reshape(b, l, ATT_HEADS, ATT_V_DIM)


def trunk(x, p, rel_bias, norm_g, w_in, ssm_lambda_re, ssm_lambda_im, ssm_log_dt,
          ssm_b_re, ssm_b_im, ssm_c_re, ssm_c_im, ssm_d, glu_w, glu_b,
          lam_q1, lam_k1, lam_q2, lam_k2, subln_g, w_branch_s, w_branch_a, w_out,
          ple_norm_g, ple_gate_w, ple_proj_w, final_g):
    b, l = x.shape[0], x.shape[1]
    h = x
    sizes = (SSM_WIDTH, SSM_WIDTH, ATT_QK_WIDTH, ATT_QK_WIDTH, ATT_WIDTH, ATT_WIDTH, D_MODEL, D_MODEL)
    cuts = [int(c) for c in np.cumsum(sizes)[:-1]]
    for i in range(DEPTH):
        hn = rmsnorm(h, norm_g[i])
        proj = hn @ w_in[i]
        s_x, s_z, q, k, v, a_z, g_s, g_a = jnp.split(proj, cuts, axis=-1)

        u = s_x.astype(jnp.float32).reshape(b, l, SSM_GROUPS, SSM_GROUP)
        y = (s5_direction(u, ssm_lambda_re[i, 0], ssm_lambda_im[i, 0], ssm_log_dt[i, 0],
                          ssm_b_re[i, 0], ssm_b_im[i, 0], ssm_c_re[i, 0], ssm_c_im[i, 0], False)
             + s5_direction(u, ssm_lambda_re[i, 1], ssm_lambda_im[i, 1], ssm_log_dt[i, 1],
                            ssm_b_re[i, 1], ssm_b_im[i, 1], ssm_c_re[i, 1], ssm_c_im[i, 1], True)
             + ssm_d[i].astype(jnp.float32).reshape(SSM_GROUPS, SSM_GROUP) * u)
        y = jax.nn.gelu(y.reshape(b, l, SSM_WIDTH))
        y = y * jax.nn.sigmoid(y @ glu_w[i].astype(jnp.float32) + glu_b[i].astype(jnp.float32))
        y = (y * jax.nn.silu(s_z.astype(jnp.float32))).astype(x.dtype)
        y_s = y @ w_branch_s[i]

        lam_init = 0.8 - 0.6 * math.exp(-0.3 * i)
        lam = (jnp.exp(jnp.sum(lam_q1[i].astype(jnp.float32) * lam_k1[i].astype(jnp.float32)))
               - jnp.exp(jnp.sum(lam_q2[i].astype(jnp.float32) * lam_k2[i].astype(jnp.float32)))
               + lam_init)
        qh = q.reshape(b, l, ATT_HEADS, 2, ATT_HEAD_DIM)
        kh = k.reshape(b, l, ATT_HEADS, 2, ATT_HEAD_DIM)
        vh = v.reshape(b, l, ATT_HEADS, ATT_V_DIM)
        o = diff_attention(qh, kh, vh, rel_bias, lam)
        o = rmsnorm(o, subln_g[i]) * (1.0 - lam_init)
        o = (o.reshape(b, l, ATT_WIDTH) * jax.nn.silu(a_z.astype(jnp.float32))).astype(x.dtype)
        y_a = o @ w_branch_a[i]

        merged = jax.nn.sigmoid(g_s) * y_s + jax.nn.sigmoid(g_a) * y_a
        h = h + (merged @ w_out[i]).astype(h.dtype)

        gate = jax.nn.sigmoid(rmsnorm(h, ple_norm_g[i]) @ ple_gate_w[i])
        h = h + (gate * (p[i] @ ple_proj_w[i])).astype(h.dtype)
    return rmsnorm(h, final_g)


def setup_inputs(seed: int = 0) -> dict:
    key = jax.random.key(seed)
    ks = jax.random.split(key, 32)
    f32 = jnp.float32
    nrm = lambda k, s, sc: jax.random.normal(k, s, f32) * sc
    n_idx = jnp.arange(SSM_STATE, dtype=f32)
    lam_re = -0.5 + nrm(ks[8], (DEPTH, 2, SSM_GROUPS, SSM_STATE), 0.01)
    lam_im = math.pi * n_idx + nrm(ks[9], (DEPTH, 2, SSM_GROUPS, SSM_STATE), 0.01)
    log_dt = jax.random.uniform(ks[10], (DEPTH, 2, SSM_GROUPS), f32, math.log(1e-3), math.log(1e-1))
    return {
        "x_prompt": nrm(ks[0], (BATCH, SEQ, D_MODEL), 1.0),
        "x_sample": nrm(ks[1], (DEC_BATCH, DEC_SEQ, D_MODEL), 1.0),
        "p_prompt": nrm(ks[2], (DEPTH, BATCH, SEQ, PLE_DIM), 1.0),
        "p_sample": nrm(ks[3], (DEPTH, DEC_BATCH, DEC_SEQ, PLE_DIM), 1.0),
        "rel_bias": nrm(ks[4], (NUM_BUCKETS, ATT_HEADS), 0.5),
        "norm_g": 1.0 + nrm(ks[5], (DEPTH, D_MODEL), 0.02),
        "w_in": nrm(ks[6], (DEPTH, D_MODEL, IN_WIDTH), D_MODEL ** -0.5),
        "ssm_lambda_re": lam_re,
        "ssm_lambda_im": lam_im,
        "ssm_log_dt": log_dt,
        "ssm_b_re": nrm(ks[11], (DEPTH, 2, SSM_GROUPS, SSM_STATE, SSM_GROUP), (2 * SSM_GROUP) ** -0.5),
        "ssm_b_im": nrm(ks[12], (DEPTH, 2, SSM_GROUPS, SSM_STATE, SSM_GROUP), (2 * SSM_GROUP) ** -0.5),
        "ssm_c_re": nrm(ks[13], (DEPTH, 2, SSM_GROUPS, SSM_GROUP, SSM_STATE), (2 * SSM_STATE) ** -0.5),
        "ssm_c_im": nrm(ks[14], (DEPTH, 2, SSM_GROUPS, SSM_GROUP, SSM_STATE), (2 * SSM_STATE) ** -0.5),
        "ssm_d": nrm(ks[15], (DEPTH, SSM_WIDTH), 1.0),
        "glu_w": nrm(ks[16], (DEPTH, SSM_WIDTH, SSM_WIDTH), SSM_WIDTH ** -0.5),
        "glu_b": nrm(ks[17], (DEPTH, SSM_WIDTH), 0.02),
        "lam_q1": nrm(ks[18], (DEPTH, ATT_HEAD_DIM), 0.1),
        "lam_k1": nrm(ks[19], (DEPTH, ATT_HEAD_DIM), 0.1),
        "lam_q2": nrm(ks[20], (DEPTH, ATT_HEAD_DIM), 0.1),
        "lam_k2": nrm(ks[21], (DEPTH, ATT_HEAD_DIM), 0.1),
        "subln_g": 1.0 + nrm(ks[22], (DEPTH, ATT_V_DIM), 0.02),
        "w_branch_s": nrm(ks[23], (DEPTH, SSM_WIDTH, D_MODEL), SSM_WIDTH ** -0.5),
        "w_branch_a": nrm(ks[24], (DEPTH, ATT_WIDTH, D_MODEL), ATT_WIDTH ** -0.5),
        "w_out": nrm(ks[25], (DEPTH, D_MODEL, D_MODEL), D_MODEL ** -0.5),
        "ple_norm_g": 1.0 + nrm(ks[26], (DEPTH, D_MODEL), 0.02),
        "ple_gate_w": nrm(ks[27], (DEPTH, D_MODEL, D_MODEL), D_MODEL ** -0.5),
        "ple_proj_w": nrm(ks[28], (DEPTH, PLE_DIM, D_MODEL), PLE_DIM ** -0.5),
        "final_g": 1.0 + nrm(ks[29], (D_MODEL,), 0.02),
    }


def reference(x_prompt, x_sample, p_prompt, p_sample, rel_bias, norm_g, w_in,
              ssm_lambda_re, ssm_lambda_im, ssm_log_dt, ssm_b_re, ssm_b_im, ssm_c_re, ssm_c_im,
              ssm_d, glu_w, glu_b, lam_q1, lam_k1, lam_q2, lam_k2, subln_g,
              w_branch_s, w_branch_a, w_out, ple_norm_g, ple_gate_w, ple_proj_w, final_g):
    weights = (rel_bias, norm_g, w_in, ssm_lambda_re, ssm_lambda_im, ssm_log_dt,
               ssm_b_re, ssm_b_im, ssm_c_re, ssm_c_im, ssm_d, glu_w, glu_b,
               lam_q1, lam_k1, lam_q2, lam_k2, subln_g, w_branch_s, w_branch_a, w_out,
               ple_norm_g, ple_gate_w, ple_proj_w, final_g)
    y_prompt = trunk(x_prompt, p_prompt, *weights)
    y_sample = trunk(x_sample, p_sample, *weights)
    return (y_prompt, y_sample)
```

```python
import math
from contextlib import ExitStack
import numpy as np
import concourse.bass as bass
import concourse.mybir as mybir
from concourse.bass_utils import run_bass_kernel_spmd

F32 = mybir.dt.float32
BF16 = mybir.dt.bfloat16
AF = mybir.ActivationFunctionType
ALU = mybir.AluOpType
AX = mybir.AxisListType

D = 2048
SEG = 2048
NSEG = 3
TT = 512
NQ = 10240
EPS = 1e-6
LAM_INIT = 0.8 - 0.6 * math.exp(-0.3 * 0)

DBG_S5_IDENT = False
DBG_ATT_IDENT = False
STOP_AFTER = None
ATT_LIM = None
ATT_PER_SWEEP = 1
P1_NSEG = NSEG
P1_NT = SEG // TT
P1_PLAN = None


class Res:
    __slots__ = ("name", "w", "r", "sem", "cnt")

    def __init__(self, name):
        self.name = name
        self.w = {}
        self.r = {}
        self.sem = None
        self.cnt = 0


class Sched:
    def __init__(self, nc, es):
        self.nc = nc
        self.es = es
        self.e = {"pe": nc.tensor, "act": nc.scalar, "dve": nc.vector, "pool": nc.gpsimd, "sp": nc.sync}
        self.sem = {k: es.enter_context(nc.semaphore("c_" + k)) for k in ("pe", "act", "dve", "pool")}
        self.cnt = {k: 0 for k in self.sem}
        self.waited = {k: {} for k in self.e}
        self.dres = []
        self.nres = 0

    def res(self, name):
        self.nres += 1
        return Res("%s_%d" % (name, self.nres))

    def _deps(self, eng, reads, writes, nosame=False):
        need = {}
        for r in reads:
            for k, v in r.w.items():
                if need.get(k, 0) < v:
                    need[k] = v
        for w in writes:
            for k, v in w.w.items():
                if need.get(k, 0) < v:
                    need[k] = v
            for k, v in w.r.items():
                if need.get(k, 0) < v:
                    need[k] = v
        wd = self.waited[eng]
        for k, v in need.items():
            if eng == "pe" and k is self.sem["pe"]:
                continue
            if nosame and eng in self.sem and k is self.sem[eng]:
                continue
            if wd.get(k, 0) < v:
                wd[k] = v
                self.e[eng].wait_ge(k, v)

    def op(self, eng, fn, reads=(), writes=(), nosame=False):
        self._deps(eng, reads, writes, nosame)
        self.cnt[eng] += 1
        c = self.cnt[eng]
        sem = self.sem[eng]
        fn(self.e[eng]).then_inc(sem, 1)
        for r in reads:
            r.r[sem] = c
        for w in writes:
            w.w[sem] = c
            w.r = {}

    def dma(self, q, out, in_, reads, writes, semres):
        self._deps(q, reads, writes)
        if semres.sem is None:
            semres.sem = self.es.enter_context(self.nc.semaphore("d_" + semres.name))
            self.dres.append(semres)
        semres.cnt += 16
        v = semres.cnt
        sem = semres.sem
        self.e[q].dma_start(out=out, in_=in_).then_inc(sem, 16)
        for r in reads:
            r.r[sem] = v
        for w in writes:
            w.w[sem] = v
            w.r = {}

    def barrier(self):
        for eng in self.e:
            wd = self.waited[eng]
            for k in self.sem:
                v = self.cnt[k]
                if v > 0 and wd.get(self.sem[k], 0) < v:
                    wd[self.sem[k]] = v
                    self.e[eng].wait_ge(self.sem[k], v)
            for dr in self.dres:
                if dr.cnt > 0 and wd.get(dr.sem, 0) < dr.cnt:
                    wd[dr.sem] = dr.cnt
                    self.e[eng].wait_ge(dr.sem, dr.cnt)


def build_nc():
    nc = bass.Bass("TRN2", target_bir_lowering=False)
    es = ExitStack()
    S = Sched(nc, es)

    def din(name, shape, dt=F32):
        return nc.dram_tensor(name, list(shape), dt, kind="ExternalInput").ap()

    def dscr(name, shape, dt=BF16):
        return nc.dram_tensor("scr_" + name, list(shape), dt).ap()

    x_in = din("x", [NSEG, SEG, D])
    p_in = din("p", [NSEG, SEG, 256])
    flags_in = din("flags", [128, 2])
    ident_in = din("ident", [128, 128])
    w_in_f = din("w_in", [D, NQ])
    glu_w_f = din("glu_w", [1024, 1024])
    wbs_f = din("w_branch_s", [1024, D])
    wba_f = din("w_branch_a", [1024, D])
    wo_f = din("w_out", [D, D])
    wpg_f = din("ple_gate_w", [D, D])
    wpp_f = din("ple_proj_w", [256, D])
    normg_in = din("norm_g", [128, 16])
    pleg_in = din("ple_norm_g", [128, 16])
    glub_in = din("glu_b", [128, 8])
    subg_in = din("subln_g", [128, 2])
    fing_in = din("final_g", [128, D])
    lamv_in = din("lamv", [128, 4, 128])
    relb_in = din("rel_bias", [32, 4])
    onehot_in = din("onehot", [32, 1536])
    s5_lr_in = din("s5_lr", [128, 128])
    s5_li_in = din("s5_li", [128, 128])
    s5_ldt_in = din("s5_ldt", [128, 128])
    s5_ba_in = din("s5_ba", [128, 128, 16])
    s5_bb_in = din("s5_bb", [128, 128, 16])
    s5_ca_in = din("s5_ca", [128, 128, 16])
    s5_cb_in = din("s5_cb", [128, 128, 16])
    s5_kx_in = din("s5_kx", [128, 128, 8])
    s5_ky_in = din("s5_ky", [128, 128, 8])
    s5_dcol_in = din("s5_dcol", [128, 64])
    s5_maskl_in = din("s5_maskl", [128, 128])
    s5_masku_in = din("s5_masku", [128, 128])
    s5_sel_in = din("s5_sel", [128, 64, 128])
    y_out = nc.dram_tensor("y", [NSEG, SEG, D], F32, kind="ExternalOutput").ap()

    wb_in = dscr("wb_in", [D, NQ])
    wb_glu = dscr("wb_glu", [1024, 1024])
    wb_s = dscr("wb_s", [1024, D])
    wb_a = dscr("wb_a", [1024, D])
    wb_o = dscr("wb_o", [D, D])
    wb_pg = dscr("wb_pg", [D, D])
    wb_pp = dscr("wb_pp", [256, D])
    qT_d = dscr("qT", [NSEG, 8, 128, SEG])
    kT_d = dscr("kT", [NSEG, 8, 128, SEG])
    v_d = dscr("v", [NSEG, SEG, 1024])
    u_d = dscr("u", [NSEG, SEG, 1024])
    szT_d = dscr("szT", [NSEG, 8, 128, SEG])
    azT_d = dscr("azT", [NSEG, 8, 128, SEG])
    gsT_d = dscr("gsT", [NSEG, 16, 128, SEG])
    gaT_d = dscr("gaT", [NSEG, 16, 128, SEG])
    ygT_d = dscr("ygT", [NSEG, 8, 128, SEG])
    ozT_d = dscr("ozT", [NSEG, 8, 128, SEG])
    R_scr = {n: S.res(n) for n in ("qT", "kT", "v", "u", "szT", "azT", "gsT", "gaT", "ygT", "ozT")}

    uniq = [0]

    def sb(stack, name, shape, dt):
        uniq[0] += 1
        return stack.enter_context(nc.sbuf_tensor("sb%d_%s" % (uniq[0], name), list(shape), dt))

    def ps(stack, name, shape, dt=F32):
        uniq[0] += 1
        return stack.enter_context(nc.psum_tensor("ps%d_%s" % (uniq[0], name), list(shape), dt))

    identf = sb(es, "identf", [128, 128], F32)
    identb = sb(es, "identb", [128, 128], BF16)
    normg = sb(es, "normg", [128, 16], F32)
    pleg = sb(es, "pleg", [128, 16], F32)
    glub = sb(es, "glub", [128, 8], F32)
    subg = sb(es, "subg", [128, 2], F32)
    flags = sb(es, "flags", [128, 2], F32)
    lamv = sb(es, "lamv", [128, 4, 128], F32)
    lamt = sb(es, "lamt", [128, 8], F32)
    R_const = S.res("const")
    for dst, src in ((identf, ident_in), (normg, normg_in), (pleg, pleg_in), (glub, glub_in), (subg, subg_in),
                     (flags, flags_in)):
        S.dma("sp", dst[:], src[:, :], [], [R_const], R_const)
    S.dma("sp", lamv[:], lamv_in[:, :, :], [], [R_const], R_const)
    S.op("dve", lambda e: e.tensor_copy(out=identb[:], in_=identf[:]), [R_const], [R_const])
    with ExitStack() as st:
        tmp = sb(st, "lamtmp", [128, 2, 128], F32)
        S.op("dve", lambda e: e.tensor_tensor(out=tmp[:, 0, :], in0=lamv[:, 0, :], in1=lamv[:, 1, :], op=ALU.mult),
             [R_const], [R_const])
        S.op("dve", lambda e: e.tensor_tensor(out=tmp[:, 1, :], in0=lamv[:, 2, :], in1=lamv[:, 3, :], op=ALU.mult),
             [R_const], [R_const])
        S.op("dve", lambda e: e.reduce_sum(out=lamt[:, 2:4], in_=tmp[:], axis=AX.X), [R_const], [R_const])
        S.op("act", lambda e: e.activation(out=lamt[:, 4:6], in_=lamt[:, 2:4], func=AF.Exp), [R_const], [R_const])
        S.op("dve", lambda e: e.tensor_tensor(out=lamt[:, 6:7], in0=lamt[:, 4:5], in1=lamt[:, 5:6], op=ALU.subtract),
             [R_const], [R_const])
        S.op("dve", lambda e: e.tensor_scalar(out=lamt[:, 0:1], in0=lamt[:, 6:7], scalar1=LAM_INIT, scalar2=None,
                                               op0=ALU.add), [R_const], [R_const])
        S.op("dve", lambda e: e.tensor_scalar(out=lamt[:, 1:2], in0=lamt[:, 0:1], scalar1=-1.0, scalar2=None,
                                               op0=ALU.mult), [R_const], [R_const])
        S.barrier()

    R_w = {}
    for name, src, dst, K in (("in", w_in_f, wb_in, D), ("glu", glu_w_f, wb_glu, 1024), ("s", wbs_f, wb_s, 1024),
                              ("a", wba_f, wb_a, 1024), ("o", wo_f, wb_o, D), ("pg", wpg_f, wb_pg, D),
                              ("pp", wpp_f, wb_pp, 256)):
        r = S.res("w" + name)
        R_w[name] = r
        if name == "in":
            R_wg = [S.res("win%d" % j) for j in range(5)]
            R_win = [R_wg[j // 4] for j in range(20)]
            continue
        for k0 in range(0, K, 128):
            S.dma("pool", dst[k0:k0 + 128, :], src[k0:k0 + 128, :], [], [r], r)

    if STOP_AFTER == 'W':
        S.barrier()
        es.close()
        return nc
    NSLAB = 3
    slabs = []
    R_slab = []
    slab_ctr = [0]

    def mk_slabs(stack):
        slabs[:] = [sb(stack, "slab%d" % i, [128, 16, 512], BF16) for i in range(NSLAB)]
        R_slab[:] = [S.res("slab%d" % i) for i in range(NSLAB)]

    def load_slab(wd, wres, KC, c0, ncols=512):
        i = slab_ctr[0] % NSLAB
        slab_ctr[0] += 1
        src = wd[:, c0:c0 + ncols].rearrange("(kc p) c -> p kc c", p=128)
        S.dma("sp", slabs[i][:, 0:KC, 0:ncols], src, [wres], [R_slab[i]], R_slab[i])
        return slabs[i], R_slab[i]

    with ExitStack() as st:
        mk_slabs(st)
        xin = sb(st, "xin", [128, 4, D], F32)
        R_xin = S.res("xin")
        xn = sb(st, "xn", [128, 4, D], BF16)
        R_xn = [S.res("xn") for _ in range(4)]
        junk = sb(st, "junk", [128, D], BF16)
        R_junk = S.res("junk")
        ss = sb(st, "ss", [128, 8], F32)
        R_ss = S.res("ss")
        hnT = [sb(st, "hnT%d" % i, [128, 16, TT], BF16) for i in range(2)]
        R_hnT = [[S.res("hnT") for _ in range(16)] for _ in range(2)]
        stg = [sb(st, "stg%d" % i, [128, 4, 512], BF16) for i in range(3)]
        R_stg = [S.res("stg%d" % i) for i in range(3)]
        psT2 = [ps(st, "psT0", [128, 1024], BF16), ps(st, "psT1", [128, 1024], BF16)]
        R_psT = [S.res("psT0"), S.res("psT1")]
        psA = [ps(st, "psA%d" % i, [128, 512], F32) for i in range(4)]
        R_psA = [S.res("psA%d" % i) for i in range(4)]
        pa_ctr = 0
        stg_ctr = 0
        plan = []
        for j in range(20):
            c0 = j * 512
            if j < 2:
                plan.append((c0, "tm", u_d, c0, "copy", "u"))
            elif j < 4:
                plan.append((c0, "fm", szT_d, (j - 2) * 4, "silu", "szT"))
            elif j < 6:
                plan.append((c0, "fm", qT_d, (j - 4) * 4, "qscale", "qT"))
            elif j < 8:
                plan.append((c0, "fm", kT_d, (j - 6) * 4, "copy", "kT"))
            elif j < 10:
                plan.append((c0, "tm", v_d, (j - 8) * 512, "copy", "v"))
            elif j < 12:
                plan.append((c0, "fm", azT_d, (j - 10) * 4, "silu", "azT"))
            elif j < 16:
                plan.append((c0, "fm", gsT_d, (j - 12) * 4, "sigmoid", "gsT"))
            else:
                plan.append((c0, "fm", gaT_d, (j - 16) * 4, "sigmoid", "gaT"))

        stage32 = [sb(st, "stage32_%d" % i, [128, 8, 512], F32) for i in range(2)]
        R_stage32 = [S.res("stage32") for _ in range(2)]
        st32_ctr = [0]

        def prep(idx, s, t):
            tok0 = t * TT
            hb = idx % 2
            S.dma("sp", xin[:], x_in[s, tok0:tok0 + TT, :].rearrange("(a p) d -> p a d", p=128), [], [R_xin], R_xin)
            for a in range(4):
                S.op("act", lambda e, a=a: e.activation(out=junk[:], in_=xin[:, a, :], func=AF.Square,
                                                        accum_out=ss[:, a:a + 1]), [R_xin], [R_junk, R_ss])
            S.op("dve", lambda e: e.tensor_scalar(out=ss[:, 4:8], in0=ss[:, 0:4], scalar1=1.0 / D, scalar2=EPS,
                                                  op0=ALU.mult, op1=ALU.add), [R_ss], [R_ss])
            S.op("act", lambda e: e.activation(out=ss[:, 4:8], in_=ss[:, 4:8], func=AF.Sqrt), [R_ss], [R_ss])
            S.op("dve", lambda e: e.reciprocal(out=ss[:, 4:8], in_=ss[:, 4:8]), [R_ss], [R_ss])
            for a in range(4):
                S.op("dve", lambda e, a=a: e.tensor_scalar(out=xn[:, a, :], in0=xin[:, a, :], scalar1=ss[:, 4 + a:5 + a],
                                                           scalar2=None, op0=ALU.mult), [R_xin, R_ss], [R_xn[a]])
            for kc in range(16):
                h = kc % 2

                def tr(e, kc=kc, h=h):
                    ins = None
                    for a in range(4):
                        ins = e.transpose(out=psT2[h][:, a * 128:(a + 1) * 128],
                                          in_=xn[:, a, kc * 128:(kc + 1) * 128], identity=identb[:])
                    return ins
                S.op("pe", tr, R_xn, [R_psT[h]])
                S.op("dve", lambda e, kc=kc, h=h, hb=hb: e.tensor_scalar(out=hnT[hb][:, kc, :], in0=psT2[h][:, 0:512],
                                                                          scalar1=normg[:, kc:kc + 1], scalar2=None,
                                                                          op0=ALU.mult), [R_psT[h]], [R_hnT[hb][kc]])

        def slab_iter(idx, s, t, entry):
            nonlocal pa_ctr, stg_ctr
            (c0, kind, dst, dbase, func, rname) = entry
            tok0 = t * TT
            hb = idx % 2
            hT = hnT[hb]
            if idx == 0:
                i = slab_ctr[0] % NSLAB
                slab_ctr[0] += 1
                slab, rslab = slabs[i], R_slab[i]
                for hf in range(2):
                    bi = st32_ctr[0] % 2
                    st32_ctr[0] += 1
                    S.dma("sp", stage32[bi][:],
                          w_in_f[hf * 1024:(hf + 1) * 1024, c0:c0 + 512].rearrange("(kc p) c -> p kc c", p=128),
                          [], [R_stage32[bi]], R_stage32[bi])
                    if hf == 0:
                        S.op("dve", lambda e, bi=bi, slab=slab: e.tensor_copy(out=slab[:, 0:8, :], in_=stage32[bi][:]),
                             [R_stage32[bi]], [rslab])
                    else:
                        S.op("act", lambda e, bi=bi, slab=slab: e.activation(out=slab[:, 8:16, :], in_=stage32[bi][:],
                                                                              func=AF.Copy), [R_stage32[bi]], [rslab])
                S.dma("pool", wb_in[:, c0:c0 + 512].rearrange("(kc p) c -> p kc c", p=128), slab[:], [rslab],
                      [R_win[c0 // 512]], R_win[c0 // 512])
            else:
                slab, rslab = load_slab(wb_in, R_win[c0 // 512], 16, c0)
            si = stg_ctr % 3
            stg_ctr += 1
            for m in range(4):
                pi = pa_ctr % 4
                pa_ctr += 1

                def mm(e, m=m, pi=pi, slab=slab, kind=kind):
                    ins = None
                    for kc in range(16):
                        if kind == "fm":
                            ins = e.matmul(psA[pi][:], lhsT=slab[:, kc, m * 128:(m + 1) * 128], rhs=hT[:, kc, :],
                                           start=(kc == 0), stop=(kc == 15))
                        else:
                            ins = e.matmul(psA[pi][:], lhsT=hT[:, kc, m * 128:(m + 1) * 128], rhs=slab[:, kc, :],
                                           start=(kc == 0), stop=(kc == 15))
                    return ins
                S.op("pe", mm, [rslab] + R_hnT[hb], [R_psA[pi]])
                if func == "copy":
                    S.op("dve", lambda e, m=m, pi=pi, si=si: e.tensor_copy(out=stg[si][:, m, :], in_=psA[pi][:]),
                         [R_psA[pi]], [R_stg[si]])
                elif func == "qscale":
                    S.op("dve", lambda e, m=m, pi=pi, si=si: e.tensor_scalar(
                        out=stg[si][:, m, :], in0=psA[pi][:], scalar1=128.0 ** -0.5, scalar2=None, op0=ALU.mult),
                        [R_psA[pi]], [R_stg[si]])
                else:
                    f = AF.Silu if func == "silu" else AF.Sigmoid
                    S.op("act", lambda e, m=m, pi=pi, si=si, f=f: e.activation(out=stg[si][:, m, :], in_=psA[pi][:],
                                                                               func=f), [R_psA[pi]], [R_stg[si]])
            if kind == "fm":
                dap = dst[s, dbase:dbase + 4, :, tok0:tok0 + TT].rearrange("c p t -> p c t")
            else:
                dap = dst[s, tok0:tok0 + TT, dbase:dbase + 512].rearrange("(a p) c -> p a c", p=128)
            S.dma("pool", dap, stg[si][:], [R_stg[si]], [R_scr[rname]], R_stg[si])

        tiles1 = [(s_, t_) for s_ in range(NSEG) for t_ in range(SEG // TT)]
        prep(0, *tiles1[0])
        for idx, (s_, t_) in enumerate(tiles1):
            for k, entry in enumerate(plan):
                if k == 10 and idx + 1 < len(tiles1):
                    prep(idx + 1, *tiles1[idx + 1])
                slab_iter(idx, s_, t_, entry)
        S.barrier()

    if STOP_AFTER == '1':
        es.close()
        return nc
    def att_phase(st):
        NB = 1536
        Bd = dscr("biasF", [4, 128, NB], F32)
        R_Bd = S.res("Bd")
        strips = sb(st, "strips", [128, 4, 1280], F32)
        biasc = sb(st, "biasc", [128, 4, 4], F32)
        st0 = ExitStack()
        relb = sb(st0, "relb", [32, 4], F32)
        ohs = sb(st0, "ohs", [32, NB], F32)
        ones32 = sb(st0, "ones32", [32, 128], F32)
        rep = sb(st0, "rep", [32, 128], F32)
        Frep = sb(st0, "Frep", [128, NB], F32)
        R_a0 = S.res("a0")
        R_rep = S.res("rep")
        R_Frep = S.res("Frep")
        R_strips = S.res("strips")
        R_biasc = S.res("biasc")
        S.dma("sp", relb[:], relb_in[:, :], [], [R_a0], R_a0)
        S.dma("sp", ohs[:], onehot_in[:, :], [], [R_a0], R_a0)
        S.op("dve", lambda e: e.memset(ones32[:], 1.0), [], [R_a0])
        psS = [ps(st, "psS%d" % i, [128, 512], F32) for i in range(3)]
        R_psS = [S.res("psS%d" % i) for i in range(3)]
        for h in range(4):
            S.op("dve", lambda e, h=h: e.tensor_scalar(out=rep[:], in0=ones32[:], scalar1=relb[:, h:h + 1], scalar2=None,
                                                       op0=ALU.mult), [R_a0], [R_rep])
            for j in range(3):
                S.op("pe", lambda e, j=j: e.matmul(psS[j % 2][:], lhsT=rep[:], rhs=ohs[:, j * 512:(j + 1) * 512],
                                                   start=True, stop=True), [R_rep, R_a0], [R_psS[j % 2]])
                S.op("dve", lambda e, j=j: e.tensor_copy(out=Frep[:, j * 512:(j + 1) * 512], in_=psS[j % 2][:]),
                     [R_psS[j % 2]], [R_Frep])
            S.op("dve", lambda e, h=h: e.tensor_copy(out=biasc[:, h, 0:1], in_=Frep[:, 967:968]), [R_Frep], [R_biasc])
            S.op("dve", lambda e, h=h: e.tensor_copy(out=biasc[:, h, 1:2], in_=Frep[:, 567:568]), [R_Frep], [R_biasc])
            S.op("dve", lambda e, h=h: e.tensor_tensor(out=biasc[:, h, 2:3], in0=Frep[:, 967:968], in1=flags[:, 1:2],
                                                       op=ALU.add), [R_Frep], [R_biasc])
            S.op("dve", lambda e, h=h: e.tensor_tensor(out=biasc[:, h, 3:4], in0=Frep[:, 567:568], in1=flags[:, 1:2],
                                                       op=ALU.add), [R_Frep], [R_biasc])
            S.dma("sp", Bd[h, :, :], Frep[:], [R_Frep], [R_Bd], R_Frep)
            src = bass.AP(Bd.tensor, h * 128 * NB + 127, [[NB - 1, 128], [1, 1280]])
            S.dma("sp", strips[:, h, :], src, [R_Bd], [R_strips], R_strips)

        S.barrier()
        st0.close()
        KT = [sb(st, "KT%d" % i, [128, 2, 4096], BF16) for i in range(1)] * 2
        QT = [sb(st, "QT%d" % i, [128, 2, 4096], BF16) for i in range(1)] * 2
        VX = [sb(st, "VX%d" % i, [128, 32, 257], BF16) for i in range(1)] * 2
        R_KT = [S.res("KT")] * 2
        R_QT = [S.res("QT")] * 2
        R_VX = [S.res("VX")] * 2
        S.op("pool", lambda e: e.memset(VX[0][:, :, 256:257], 1.0), [], [R_VX[0]])
        PT = [sb(st, "PT%d" % i, [128, 512], BF16) for i in range(3)]
        R_PT = [S.res("PT") for _ in range(3)]
        tmpb = [sb(st, "tmpb%d" % i, [128, 512], F32) for i in range(2)]
        R_tmpb = [S.res("tmpb") for _ in range(2)]
        o0 = sb(st, "o0", [128, 4, 256], F32)
        oo = sb(st, "oo", [128, 4, 256], F32)
        R_o0 = [S.res("o0") for _ in range(4)]
        R_oo = [S.res("oo") for _ in range(4)]
        onb = sb(st, "onb", [128, 4, 256], BF16)
        R_onb = [S.res("onb") for _ in range(4)]
        rr = sb(st, "rr", [128, 4, 8], F32)
        R_rr = [S.res("rr") for _ in range(4)]
        junk3 = sb(st, "junk3", [128, 256], BF16)
        R_junk3 = S.res("junk3")
        azt = [sb(st, "azt%d" % i, [128, 2, 512], BF16) for i in range(2)]
        R_azt = [S.res("azt") for _ in range(2)]
        ozs = [sb(st, "ozs%d" % i, [128, 2, 512], BF16) for i in range(2)]
        R_ozs = [S.res("ozs") for _ in range(2)]
        acc = [ps(st, "acc%d" % i, [128, 512], F32) for i in range(4)]
        R_acc = [S.res("acc%d" % i) for i in range(4)]
        psTa = ps(st, "psTa", [128, 1024], BF16)
        R_psTa = S.res("psTa")
        sctr = 0
        ptctr = 0
        tbctr = 0
        qtctr = 0
        hctr = 0
        groups = [([0, 1], 4096), ([2], 2048)]
        if ATT_LIM is not None:
            groups = groups[:ATT_LIM[0]]
        for (gsegs, L) in groups:
            nkb = L // 128
            for h in range(4 if ATT_LIM is None else ATT_LIM[1]):
                bi = hctr % 2
                hctr += 1
                for si, sg in enumerate(gsegs):
                    S.dma("sp", KT[bi][:, :, si * SEG:(si + 1) * SEG],
                          kT_d[sg, 2 * h:2 * h + 2, :, :].rearrange("c p t -> p c t"), [R_scr["kT"]], [R_KT[bi]], R_KT[bi])
                    S.dma("sp", QT[bi][:, :, si * SEG:(si + 1) * SEG],
                          qT_d[sg, 2 * h:2 * h + 2, :, :].rearrange("c p t -> p c t"), [R_scr["qT"]], [R_QT[bi]], R_QT[bi])
                    S.dma("sp", VX[bi][:, si * 16:(si + 1) * 16, 0:256],
                          v_d[sg, :, h * 256:(h + 1) * 256].rearrange("(kb p) e -> p kb e", p=128),
                          [R_scr["v"]], [R_VX[bi]], R_VX[bi])
                nqt = L // 512 if ATT_LIM is None else min(L // 512, ATT_LIM[2])
                iters = [(qt, c, kb) for qt in range(nqt) for c in range(2) for kb in range(nkb)]
                slots = {}
                qinfo = {}

                def start_qt(qt, gsegs=gsegs, h=h):
                    nonlocal qtctr
                    q0 = qt * 512
                    qseg = gsegs[q0 // SEG]
                    qtok = q0 % SEG
                    ai = qtctr % 2
                    qtctr += 1
                    S.dma("sp", azt[ai][:], azT_d[qseg, 2 * h:2 * h + 2, :, qtok:qtok + 512].rearrange("c p t -> p c t"),
                          [R_scr["azT"]], [R_azt[ai]], R_azt[ai])
                    qinfo[qt] = (q0, qseg, qtok, ai)

                def emit_qk(i, bi=bi):
                    nonlocal sctr
                    qt, c, kb = iters[i]
                    if c == 0 and kb == 0:
                        start_qt(qt)
                    q0 = qinfo[qt][0]
                    k0 = kb * 128
                    pi = sctr % 3
                    sctr += 1
                    slots[i] = pi
                    S.op("pe", lambda e: e.matmul(psS[pi][:], lhsT=KT[bi][:, c, k0:k0 + 128],
                                                  rhs=QT[bi][:, c, q0:q0 + 512], start=True, stop=True),
                         [R_KT[bi], R_QT[bi]], [R_psS[pi]])

                def finish_qt(qt, h=h):
                    (q0, qseg, qtok, ai) = qinfo.pop(qt)
                    for qs in range(4):
                        S.op("act", lambda e, qs=qs: e.activation(out=junk3[:], in_=oo[:, qs, :], func=AF.Square,
                                                                  accum_out=rr[:, qs, 3:4]), [R_oo[qs]], [R_junk3, R_rr[qs]])
                        S.op("dve", lambda e, qs=qs: e.tensor_scalar(out=rr[:, qs, 4:5], in0=rr[:, qs, 3:4],
                                                                     scalar1=1.0 / 256, scalar2=EPS, op0=ALU.mult,
                                                                     op1=ALU.add), [R_rr[qs]], [R_rr[qs]])
                        S.op("act", lambda e, qs=qs: e.activation(out=rr[:, qs, 4:5], in_=rr[:, qs, 4:5], func=AF.Sqrt),
                             [R_rr[qs]], [R_rr[qs]])
                        S.op("dve", lambda e, qs=qs: e.reciprocal(out=rr[:, qs, 5:6], in_=rr[:, qs, 4:5]),
                             [R_rr[qs]], [R_rr[qs]])
                        S.op("dve", lambda e, qs=qs: e.tensor_scalar(out=onb[:, qs, :], in0=oo[:, qs, :],
                                                                     scalar1=rr[:, qs, 5:6], scalar2=1.0 - LAM_INIT,
                                                                     op0=ALU.mult, op1=ALU.mult),
                             [R_oo[qs], R_rr[qs]], [R_onb[qs]])
                    oi = ai
                    for ec in range(2):
                        def tr(e, ec=ec):
                            ins = None
                            for qs in range(4):
                                ins = e.transpose(out=psTa[:, qs * 128:(qs + 1) * 128],
                                                  in_=onb[:, qs, ec * 128:(ec + 1) * 128], identity=identb[:])
                            return ins
                        S.op("pe", tr, R_onb, [R_psTa])
                        S.op("dve", lambda e, ec=ec: e.scalar_tensor_tensor(
                            out=ozs[oi][:, ec, :], in0=psTa[:, 0:512], scalar=subg[:, ec:ec + 1], in1=azt[ai][:, ec, :],
                            op0=ALU.mult, op1=ALU.mult), [R_psTa, R_azt[ai]], [R_ozs[oi]])
                    S.dma("pool", ozT_d[qseg, 2 * h:2 * h + 2, :, qtok:qtok + 512].rearrange("c p t -> p c t"),
                          ozs[oi][:], [R_ozs[oi]], [R_scr["ozT"]], R_ozs[oi])

                def emit_rest(i, bi=bi, h=h, nkb=nkb, gsegs=gsegs):
                    nonlocal ptctr, tbctr
                    qt, c, kb = iters[i]
                    q0 = qinfo[qt][0]
                    k0 = kb * 128
                    dk = k0 - q0
                    cross = (len(gsegs) == 2) and ((k0 // SEG) != (q0 // SEG))
                    near = (-128 <= dk <= 512)
                    pi = slots.pop(i)
                    pti = ptctr % 3
                    ptctr += 1
                    if near:
                        j0 = 640 - dk
                        ti = tbctr % 2
                        tbctr += 1
                        S.op("dve", lambda e: e.tensor_tensor(out=tmpb[ti][:], in0=psS[pi][:],
                                                              in1=strips[:, h, j0:j0 + 512], op=ALU.add),
                             [R_psS[pi], R_strips], [R_tmpb[ti]])
                        if cross:
                            S.op("act", lambda e: e.activation(out=PT[pti][:], in_=tmpb[ti][:], func=AF.Exp,
                                                               bias=flags[:, 1:2]), [R_tmpb[ti]], [R_PT[pti]])
                        else:
                            S.op("act", lambda e: e.activation(out=PT[pti][:], in_=tmpb[ti][:], func=AF.Exp),
                                 [R_tmpb[ti]], [R_PT[pti]])
                    else:
                        col = (1 if dk > 0 else 0) + (2 if cross else 0)
                        S.op("act", lambda e: e.activation(out=PT[pti][:], in_=psS[pi][:], func=AF.Exp,
                                                           bias=biasc[:, h, col:col + 1]),
                             [R_psS[pi], R_biasc], [R_PT[pti]])
                    for qs in range(4):
                        S.op("pe", lambda e, qs=qs: e.matmul(
                            acc[qs][:, 0:257], lhsT=PT[pti][:, qs * 128:(qs + 1) * 128], rhs=VX[bi][:, kb, :],
                            start=(kb == 0), stop=(kb == nkb - 1)), [R_PT[pti], R_VX[bi]], [R_acc[qs]])
                    if kb == nkb - 1:
                        for qs in range(4):
                            S.op("dve", lambda e, qs=qs: e.reciprocal(out=rr[:, qs, c:c + 1], in_=acc[qs][:, 256:257]),
                                 [R_acc[qs]], [R_rr[qs]])
                            if c == 0:
                                S.op("dve", lambda e, qs=qs: e.tensor_scalar(
                                    out=o0[:, qs, :], in0=acc[qs][:, 0:256], scalar1=rr[:, qs, 0:1], scalar2=None,
                                    op0=ALU.mult), [R_acc[qs], R_rr[qs]], [R_o0[qs]])
                            else:
                                S.op("dve", lambda e, qs=qs: e.tensor_tensor(
                                    out=rr[:, qs, 2:3], in0=rr[:, qs, 1:2], in1=lamt[:, 1:2], op=ALU.mult),
                                    [R_rr[qs]], [R_rr[qs]])
                                S.op("dve", lambda e, qs=qs: e.scalar_tensor_tensor(
                                    out=oo[:, qs, :], in0=acc[qs][:, 0:256], scalar=rr[:, qs, 2:3], in1=o0[:, qs, :],
                                    op0=ALU.mult, op1=ALU.add), [R_acc[qs], R_rr[qs], R_o0[qs]], [R_oo[qs]])
                        if c == 1:
                            finish_qt(qt)
                LA = 2
                for i in range(len(iters) + LA):
                    if i < len(iters):
                        emit_qk(i)
                    if i >= LA:
                        emit_rest(i - LA)
                    yield


    if not DBG_S5_IDENT:
        TWO_PI = 2.0 * math.pi
        MAGIC = 12582912.0
        NCB = 32
        Ud = dscr("Ud", [NSEG, NCB, 128, 64 * 8])
        Zd = dscr("Zd", [NSEG, 2, NCB, 128, 64 * 8])
        Pd = dscr("Pd", [NSEG, 2, NCB, 128, 64 * 8], F32)
        R_Pd = [[S.res("Pd") for _ in range(2)] for _ in range(NSEG)]
        R_Ud = S.res("Ud")
        R_Zd = [[S.res("Zd") for _ in range(2)] for _ in range(NSEG)]
        with ExitStack() as st5:
            Ym = sb(st5, "Ym", [128, 128, 128], BF16)
            Mall = sb(st5, "Mall", [128, 64, 128], BF16)
            Ar = sb(st5, "Ar", [128, 128], F32)
            Aisw = sb(st5, "Aisw", [128, 128], F32)
            stX = ExitStack()
            Xm = sb(stX, "Xm", [128, 64, 2, 128], BF16)
            R_Xm, R_Ym, R_M, R_A = S.res("Xm"), S.res("Ym"), S.res("Mall"), S.res("A")
            with ExitStack() as st:
                def ld(name, src_ap, shape, dt=F32):
                    t = sb(st, name, shape, dt)
                    r = S.res(name)
                    S.dma("sp", t[:], src_ap, [], [r], r)
                    return t, r
                lr2, R_lr = ld("lr2", s5_lr_in[:, :], [128, 128])
                li2, R_li = ld("li2", s5_li_in[:, :], [128, 128])
                ldt2, R_ldt = ld("ldt2", s5_ldt_in[:, :], [128, 128])
                dcol, R_dcol = ld("dcol", s5_dcol_in[:, :], [128, 64])
                mkL, R_mkL = ld("mkL", s5_maskl_in[:, :], [128, 128])
                mkU, R_mkU = ld("mkU", s5_masku_in[:, :], [128, 128])
                xr = sb(st, "xr", [128, 128], F32)
                th = sb(st, "th", [128, 128], F32)
                R_xr = S.res("xr")
                S.op("act", lambda e: e.activation(out=xr[:], in_=ldt2[:], func=AF.Exp), [R_ldt], [R_xr])
                S.op("dve", lambda e: e.tensor_tensor(out=th[:], in0=li2[:], in1=xr[:], op=ALU.mult), [R_li, R_xr], [R_xr])
                S.op("dve", lambda e: e.tensor_tensor(out=xr[:], in0=lr2[:], in1=xr[:], op=ALU.mult), [R_lr, R_xr], [R_xr])

                pwtmp = {}

                def pw(stk, tag, F, kt, thb, xrb, R_k):
                    shp = [128] + F
                    if len(F) not in pwtmp:
                        pwtmp[len(F)] = (sb(st, "pwang%d" % len(F), shp, F32), sb(st, "pwt%d" % len(F), shp, F32),
                                         S.res("pwtmp"))
                    ang, t, Rt = pwtmp[len(F)]
                    pr = sb(stk, tag + "pr", shp, F32)
                    pi = sb(stk, tag + "pi", shp, F32)
                    R = S.res(tag)
                    S.op("dve", lambda e: e.tensor_tensor(out=ang[:], in0=kt, in1=thb, op=ALU.mult), [R_k, R_xr], [Rt])
                    S.op("dve", lambda e: e.tensor_scalar(out=t[:], in0=ang[:], scalar1=1.0 / TWO_PI, scalar2=MAGIC,
                                                          op0=ALU.mult, op1=ALU.add), [Rt], [Rt])
                    S.op("dve", lambda e: e.tensor_scalar(out=t[:], in0=t[:], scalar1=-MAGIC, scalar2=-TWO_PI,
                                                          op0=ALU.add, op1=ALU.mult), [Rt], [Rt])
                    S.op("dve", lambda e: e.tensor_tensor(out=ang[:], in0=ang[:], in1=t[:], op=ALU.add), [Rt], [Rt])
                    S.op("act", lambda e: e.activation(out=pi[:], in_=ang[:], func=AF.Sin), [Rt], [R])
                    S.op("dve", lambda e: e.tensor_scalar(out=t[:], in0=ang[:], scalar1=-1.0, scalar2=None, op0=ALU.mult),
                         [Rt], [Rt])
                    S.op("dve", lambda e: e.tensor_tensor(out=t[:], in0=t[:], in1=ang[:], op=ALU.max), [Rt], [Rt])
                    S.op("dve", lambda e: e.tensor_scalar(out=t[:], in0=t[:], scalar1=-1.0, scalar2=math.pi / 2,
                                                          op0=ALU.mult, op1=ALU.add), [Rt], [Rt])
                    S.op("act", lambda e: e.activation(out=pr[:], in_=t[:], func=AF.Sin), [Rt], [R])
                    S.op("dve", lambda e: e.tensor_tensor(out=t[:], in0=kt, in1=xrb, op=ALU.mult), [R_k, R_xr], [Rt])
                    S.op("act", lambda e: e.activation(out=t[:], in_=t[:], func=AF.Exp), [Rt], [Rt])
                    S.op("dve", lambda e: e.tensor_tensor(out=pr[:], in0=pr[:], in1=t[:], op=ALU.mult), [R, Rt], [R])
                    S.op("dve", lambda e: e.tensor_tensor(out=pi[:], in0=pi[:], in1=t[:], op=ALU.mult), [R, Rt], [R])
                    return pr, pi, R

                k1 = sb(st, "k1", [128, 128], F32)
                k8 = sb(st, "k8", [128, 128], F32)
                R_k18 = S.res("k18")
                S.op("dve", lambda e: e.memset(k1[:], 1.0), [], [R_k18])
                S.op("dve", lambda e: e.memset(k8[:], 8.0), [], [R_k18])
                p1r, p1i, R_p1 = pw(st, "p1", [128], k1[:], th[:], xr[:], R_k18)
                p8r, p8i, R_p8 = pw(st, "p8", [128], k8[:], th[:], xr[:], R_k18)
                S.op("dve", lambda e: e.tensor_copy(out=Ar[:], in_=p8r[:]), [R_p8], [R_A])
                S.op("dve", lambda e: e.tensor_copy(out=Aisw[0:64, :], in_=p8i[0:64, :]), [R_p8], [R_A])
                S.op("dve", lambda e: e.tensor_scalar(out=Aisw[64:128, :], in0=p8i[64:128, :], scalar1=-1.0, scalar2=None,
                                                      op0=ALU.mult), [R_p8], [R_A])
                kr = sb(st, "kr", [128, 128], F32)
                ki = sb(st, "ki", [128, 128], F32)
                den = sb(st, "den", [128, 128], F32)
                tq = sb(st, "tq", [128, 128], F32)
                R_kap = S.res("kap")
                S.op("dve", lambda e: e.tensor_scalar(out=p1r[:], in0=p1r[:], scalar1=-1.0, scalar2=None, op0=ALU.add),
                     [R_p1], [R_p1])
                S.op("dve", lambda e: e.tensor_tensor(out=den[:], in0=lr2[:], in1=lr2[:], op=ALU.mult), [R_lr], [R_kap])
                S.op("dve", lambda e: e.tensor_tensor(out=tq[:], in0=li2[:], in1=li2[:], op=ALU.mult), [R_li], [R_kap])
                S.op("dve", lambda e: e.tensor_tensor(out=den[:], in0=den[:], in1=tq[:], op=ALU.add), [R_kap], [R_kap])
                S.op("dve", lambda e: e.reciprocal(out=den[:], in_=den[:]), [R_kap], [R_kap])
                S.op("dve", lambda e: e.tensor_tensor(out=kr[:], in0=p1r[:], in1=lr2[:], op=ALU.mult), [R_p1, R_lr], [R_kap])
                S.op("dve", lambda e: e.tensor_tensor(out=tq[:], in0=p1i[:], in1=li2[:], op=ALU.mult), [R_p1, R_li], [R_kap])
                S.op("dve", lambda e: e.tensor_tensor(out=kr[:], in0=kr[:], in1=tq[:], op=ALU.add), [R_kap], [R_kap])
                S.op("dve", lambda e: e.tensor_tensor(out=kr[:], in0=kr[:], in1=den[:], op=ALU.mult), [R_kap], [R_kap])
                S.op("dve", lambda e: e.tensor_tensor(out=ki[:], in0=p1i[:], in1=lr2[:], op=ALU.mult), [R_p1, R_lr], [R_kap])
                S.op("dve", lambda e: e.tensor_tensor(out=tq[:], in0=p1r[:], in1=li2[:], op=ALU.mult), [R_p1, R_li], [R_kap])
                S.op("dve", lambda e: e.tensor_tensor(out=ki[:], in0=ki[:], in1=tq[:], op=ALU.subtract), [R_kap], [R_kap])
                S.op("dve", lambda e: e.tensor_tensor(out=ki[:], in0=ki[:], in1=den[:], op=ALU.mult), [R_kap], [R_kap])
                B8 = [128, 128, 8]
                thb = th[:].unsqueeze(2).to_broadcast(B8)
                xrb = xr[:].unsqueeze(2).to_broadcast(B8)
                krb = kr[:].unsqueeze(2).to_broadcast(B8)
                kib = ki[:].unsqueeze(2).to_broadcast(B8)
                XT = sb(st, "XT", [128, 128, 128], BF16)
                R_XT = S.res("XT")
                t1 = sb(st, "g_t1", [128, 16, 8, 16], F32)
                t2 = sb(st, "g_t2", [128, 16, 8, 16], F32)
                R_t12 = S.res("t12")
                SH = [128, 16, 8, 16]

                def ld2(stk, name, src_ap, shape):
                    t = sb(stk, name, shape, F32)
                    r = S.res(name)
                    S.dma("sp", t[:], src_ap, [], [r], r)
                    return t, r
                pwtmp[2] = (sb(st, "pwang2", B8, F32), sb(st, "pwt2", B8, F32), S.res("pwtmp"))
                for part in ("X", "Y"):
                    with ExitStack() as sx:
                        if part == "X":
                            TA, R_TA = ld2(sx, "BA", s5_ba_in[:, :, :], [128, 128, 16])
                            TB, R_TB = ld2(sx, "BB", s5_bb_in[:, :, :], [128, 128, 16])
                            kT, R_kT = ld2(sx, "kX", s5_kx_in[:, :, :], [128, 128, 8])
                            pr_, pi_, R_p = pw(sx, "px", [128, 8], kT[:], thb, xrb, R_kT)
                            ar_ = sb(sx, "xir", B8, F32)
                            ai_ = sb(sx, "xii", B8, F32)
                            tx = sb(sx, "tx", B8, F32)
                            R_ar = S.res("xi")
                            S.op("dve", lambda e: e.tensor_tensor(out=ar_[:], in0=pr_[:], in1=krb, op=ALU.mult), [R_p, R_kap], [R_ar])
                            S.op("dve", lambda e: e.tensor_tensor(out=tx[:], in0=pi_[:], in1=kib, op=ALU.mult), [R_p, R_kap], [R_ar])
                            S.op("dve", lambda e: e.tensor_tensor(out=ar_[:], in0=ar_[:], in1=tx[:], op=ALU.subtract), [R_ar], [R_ar])
                            S.op("dve", lambda e: e.tensor_tensor(out=ai_[:], in0=pr_[:], in1=kib, op=ALU.mult), [R_p, R_kap], [R_ar])
                            S.op("dve", lambda e: e.tensor_tensor(out=tx[:], in0=pi_[:], in1=krb, op=ALU.mult), [R_p, R_kap], [R_ar])
                            S.op("dve", lambda e: e.tensor_tensor(out=ai_[:], in0=ai_[:], in1=tx[:], op=ALU.add), [R_ar], [R_ar])
                            outT, R_o, upper_neg = XT, R_XT, False
                        else:
                            TA, R_TA = ld2(sx, "CA", s5_ca_in[:, :, :], [128, 128, 16])
                            TB, R_TB = ld2(sx, "CB", s5_cb_in[:, :, :], [128, 128, 16])
                            kT, R_kT = ld2(sx, "kY", s5_ky_in[:, :, :], [128, 128, 8])
                            ar_, ai_, R_ar = pw(sx, "py", [128, 8], kT[:], thb, xrb, R_kT)
                            outT, R_o, upper_neg = Ym, R_Ym, True
                        for cch in range(8):
                            sl = slice(cch * 16, (cch + 1) * 16)
                            a_r = ar_[:, sl, :].unsqueeze(3).to_broadcast(SH)
                            a_i = ai_[:, sl, :].unsqueeze(3).to_broadcast(SH)
                            b_a = TA[:, sl, :].unsqueeze(2).to_broadcast(SH)
                            b_b = TB[:, sl, :].unsqueeze(2).to_broadcast(SH)
                            S.op("dve", lambda e, a_r=a_r, b_a=b_a: e.tensor_tensor(out=t1[:], in0=a_r, in1=b_a, op=ALU.mult),
                                 [R_ar, R_TA], [R_t12])
                            S.op("dve", lambda e, a_i=a_i, b_b=b_b: e.tensor_tensor(out=t2[:], in0=a_i, in1=b_b, op=ALU.mult),
                                 [R_ar, R_TB], [R_t12])
                            ov = outT[:, sl, :].rearrange("p a (j c) -> p a j c", j=8)
                            S.op("dve", lambda e, ov=ov: e.tensor_tensor(out=ov[0:64], in0=t1[0:64], in1=t2[0:64],
                                                                         op=ALU.subtract), [R_t12], [R_o])
                            if upper_neg:
                                S.op("dve", lambda e, sl=sl: e.scalar_tensor_tensor(
                                    out=outT[64:128, sl, :], in0=t1[64:128].rearrange("p a j c -> p a (j c)"), scalar=-1.0,
                                    in1=t2[64:128].rearrange("p a j c -> p a (j c)"), op0=ALU.mult, op1=ALU.subtract),
                                    [R_t12], [R_o])
                            else:
                                S.op("dve", lambda e, ov=ov: e.tensor_tensor(out=ov[64:128], in0=t1[64:128], in1=t2[64:128],
                                                                             op=ALU.add), [R_t12], [R_o])
                        S.barrier()
                psx = [ps(st, "psx%d" % i, [128, 1024], BF16) for i in range(2)]
                R_psx = [S.res("psx") for _ in range(2)]
                psm = [ps(st, "psm%d" % i, [128, 512], F32) for i in range(4)]
                R_psm = [S.res("psm") for _ in range(4)]
                mt = [sb(st, "mt%d" % i, [128, 128], F32) for i in range(2)]
                R_mt = [S.res("mt") for _ in range(2)]
                for g4 in range(16):
                    bi = g4 % 2

                    def trx(e, g4=g4, bi=bi):
                        ins = None
                        for gg in range(4):
                            for d in range(2):
                                ins = e.transpose(out=psx[bi][:, (gg * 2 + d) * 128:(gg * 2 + d + 1) * 128],
                                                  in_=XT[:, d * 64 + g4 * 4 + gg, :], identity=identb[:])
                        return ins
                    S.op("pe", trx, [R_XT], [R_psx[bi]])
                    S.op("act", lambda e, g4=g4, bi=bi: e.activation(
                        out=Xm[:, g4 * 4:(g4 + 1) * 4, :, :].rearrange("p g d m -> p (g d m)"), in_=psx[bi][:], func=AF.Copy),
                        [R_psx[bi]], [R_Xm])
                for g in range(64):
                    pf = (2 * g) % 4
                    pb = (2 * g + 1) % 4
                    S.op("pe", lambda e, g=g, pf=pf: e.matmul(psm[pf][:, 0:128], lhsT=XT[:, g, :], rhs=Ym[:, g, :],
                                                              start=True, stop=True), [R_XT, R_Ym], [R_psm[pf]])
                    S.op("pe", lambda e, g=g, pb=pb: e.matmul(psm[pb][:, 0:128], lhsT=XT[:, 64 + g, :], rhs=Ym[:, 64 + g, :],
                                                              start=True, stop=True), [R_XT, R_Ym], [R_psm[pb]])
                    S.op("dve", lambda e, pf=pf: e.tensor_tensor(out=mt[0][:], in0=psm[pf][:, 0:128], in1=mkL[:], op=ALU.mult),
                         [R_psm[pf], R_mkL], [R_mt[0]])
                    S.op("dve", lambda e, pb=pb: e.tensor_tensor(out=mt[1][:], in0=psm[pb][:, 0:128], in1=mkU[:], op=ALU.mult),
                         [R_psm[pb], R_mkU], [R_mt[1]])
                    S.op("dve", lambda e: e.tensor_tensor(out=mt[0][:], in0=mt[0][:], in1=mt[1][:], op=ALU.add),
                         [R_mt[0], R_mt[1]], [R_mt[0]])
                    S.op("dve", lambda e, g=g: e.scalar_tensor_tensor(out=Mall[:, g, :], in0=identf[:], scalar=dcol[:, g:g + 1],
                                                                      in1=mt[0][:], op0=ALU.mult, op1=ALU.add),
                         [R_mt[0], R_dcol], [R_M])
                S.barrier()
            with ExitStack() as st:
                Uc = [sb(st, "Uc%d" % i, [128, 8 * 1024], BF16) for i in range(1)] * 2
                R_Uc = [S.res("Uc")] * 2
                Ucr = [sb(st, "Ucr%d" % i, [128, 64, 128], BF16) for i in range(1)] * 2
                R_Ucr = [S.res("Ucr")] * 2
                Pst = sb(st, "Pst", [128, 16, 64, 8], F32)
                R_Pst = S.res("Pst")
                psP = [ps(st, "psP%d" % i, [128, 512], F32) for i in range(2)]
                R_psP = [S.res("psP") for _ in range(2)]
                ppc = 0
                Ublk = [sb(st, "Ublk%d" % i, [128, 16, 64, 8], BF16) for i in range(2)]
                R_Ublk = [S.res("Ublk") for _ in range(2)]
                psu = [ps(st, "psu%d" % i, [128, 1024], BF16) for i in range(2)]
                R_psu = [S.res("psu") for _ in range(2)]
                it = 0
                pc = 0
                for s in range(NSEG):
                    for ct in range(2):
                        b = it % 2
                        it += 1
                        S.dma("sp", Uc[b][:], u_d[s, ct * 1024:(ct + 1) * 1024, :].rearrange("(p j) f -> p (j f)", j=8),
                              [R_scr["u"]], [R_Uc[b]], R_Uc[b])
                        for hf in range(2):
                            S.op("dve" if hf else "act", (lambda e, b=b, hf=hf: e.tensor_copy(
                                out=Ucr[b][:, hf * 32:(hf + 1) * 32, :].rearrange("p g (j c) -> p g j c", j=8),
                                in_=Uc[b][:].rearrange("p (j g c) -> p g j c", j=8, g=64)[:, hf * 32:(hf + 1) * 32]))
                                if hf else (lambda e, b=b, hf=hf: e.activation(
                                    out=Ucr[b][:, hf * 32:(hf + 1) * 32, :].rearrange("p g (j c) -> p g j c", j=8),
                                    in_=Uc[b][:].rearrange("p (j g c) -> p g j c", j=8, g=64)[:, hf * 32:(hf + 1) * 32],
                                    func=AF.Copy)), [R_Uc[b]], [R_Ucr[b]])
                        for g8 in range(8):
                            pi_ = pc % 2
                            pc += 1

                            def tru(e, g8=g8, pi_=pi_, b=b):
                                ins = None
                                for gg in range(8):
                                    g = g8 * 8 + gg
                                    ins = e.transpose(out=psu[pi_][:, gg * 128:(gg + 1) * 128],
                                                      in_=Ucr[b][:, g, :], identity=identb[:])
                                return ins
                            S.op("pe", tru, [R_Ucr[b]], [R_psu[pi_]])
                            S.op("act" if g8 % 2 else "dve", lambda e, g8=g8, pi_=pi_, b=b: e.tensor_copy(
                                out=Ublk[b][:, :, g8 * 8:(g8 + 1) * 8, :].rearrange("p cb g c -> p g cb c"),
                                in_=psu[pi_][:].rearrange("p (g cb c) -> p g cb c", g=8, cb=16)) if g8 % 2 == 0 else
                                e.activation(out=Ublk[b][:, :, g8 * 8:(g8 + 1) * 8, :].rearrange("p cb g c -> p g cb c"),
                                             in_=psu[pi_][:].rearrange("p (g cb c) -> p g cb c", g=8, cb=16), func=AF.Copy),
                                [R_psu[pi_]], [R_Ublk[b]])
                        S.dma("pool", Ud[s, ct * 16:(ct + 1) * 16, :, :].rearrange("cb p f -> p cb f"),
                              Ublk[b][:].rearrange("p cb g c -> p cb (g c)"), [R_Ublk[b]], [R_Ud], R_Ublk[b])
                        for d in range(2):
                            for g4 in range(16):
                                pq = ppc % 2
                                ppc += 1

                                def mmP(e, d=d, g4=g4, pq=pq, b=b):
                                    ins = None
                                    for gi in range(4):
                                        ins = e.matmul(psP[pq][:, gi * 128:(gi + 1) * 128], lhsT=Xm[:, g4 * 4 + gi, d, :],
                                                       rhs=Ublk[b][:, :, g4 * 4 + gi, :], start=True, stop=True)
                                    return ins
                                S.op("pe", mmP, [R_Xm, R_Ublk[b]], [R_psP[pq]])
                                if g4 % 2 == 0:
                                    S.op("dve", lambda e, g4=g4, pq=pq: e.tensor_copy(
                                        out=Pst[:, :, g4 * 4:(g4 + 1) * 4, :].rearrange("p cb g c -> p g cb c"),
                                        in_=psP[pq][:].rearrange("p (g cb c) -> p g cb c", g=4, cb=16)),
                                        [R_psP[pq]], [R_Pst])
                                else:
                                    S.op("act", lambda e, g4=g4, pq=pq: e.activation(
                                        out=Pst[:, :, g4 * 4:(g4 + 1) * 4, :].rearrange("p cb g c -> p g cb c"),
                                        in_=psP[pq][:].rearrange("p (g cb c) -> p g cb c", g=4, cb=16), func=AF.Copy),
                                        [R_psP[pq]], [R_Pst])
                            S.dma("pool", Pd[s, d, ct * 16:(ct + 1) * 16, :, :].rearrange("cb p f -> p cb f"),
                                  Pst[:].rearrange("p cb g c -> p cb (g c)"), [R_Pst], [R_Pd[s][d]], R_Pst)
                S.barrier()
            def sweep_phase(st):
                Z = [[sb(st, "Zst%d%d" % (d, i), [128, 2, 64], F32) for i in range(2)] for d in range(2)]
                W = [sb(st, "Wst%d" % d, [128, 2, 64], F32) for d in range(2)]
                T1 = [sb(st, "T1st%d" % d, [128, 2, 64], F32) for d in range(2)]
                T2 = [sb(st, "T2st%d" % d, [128, 2, 64], F32) for d in range(2)]
                R_Z = [[S.res("Zst") for _ in range(2)] for _ in range(2)]
                R_W = [S.res("Wst") for _ in range(2)]
                R_T1 = [S.res("T1st") for _ in range(2)]
                R_T2 = [S.res("T2st") for _ in range(2)]
                Pb = [[sb(st, "Pb%d%d" % (d, i), [128, 2, 64, 8], F32) for i in range(2)] for d in range(2)]
                R_Pb = [[S.res("Pb") for _ in range(2)] for _ in range(2)]
                Zb = [[[sb(st, "Zb%d%d%d" % (d, k, i), [128, 64, 8], BF16) for i in range(2)] for k in range(2)] for d in range(2)]
                R_Zb = [[[S.res("Zbk") for _ in range(2)] for _ in range(2)] for _ in range(2)]
                stages = [([0, 2], [1, 2]), ([1], [0])]
                pp = 0
                for sti, (fsegs, bsegs) in enumerate(stages):
                    segs_d = [fsegs, bsegs]
                    for d in range(2):
                        if sti == 0:
                            S.op("dve", lambda e, d=d: e.memset(Z[d][pp][:], 0.0), [], [R_Z[d][pp]])
                        else:
                            S.op("dve", lambda e, d=d, pp=pp: e.tensor_scalar(
                                out=Z[d][pp][:, 0, :], in0=Z[d][pp][:, 0, :], scalar1=flags[:, 0:1], scalar2=None,
                                op0=ALU.mult), [R_Z[d][pp]], [R_Z[d][pp]])
                    for tb in range(NCB):
                        i2 = tb % 2
                        for d in range(2):
                            cb = tb if d == 0 else NCB - 1 - tb
                            for k, sg in enumerate(segs_d[d]):
                                S.dma("pool", Pb[d][i2][:, k, :, :].rearrange("p g c -> p (g c)"), Pd[sg, d, cb, :, :],
                                      [R_Pd[sg][d]], [R_Pb[d][i2]], R_Pb[d][i2])
                        for cc_ in range(8):
                            nx = 1 - pp
                            for d in range(2):
                                cc = cc_ if d == 0 else 7 - cc_
                                for k in range(len(segs_d[d])):
                                    S.op("pool", lambda e, d=d, k=k, cc=cc, pp=pp: e.tensor_copy(
                                        out=Zb[d][k][i2][:, :, cc], in_=Z[d][pp][:, k, :]),
                                        [R_Z[d][pp]], [R_Zb[d][k][i2]])
                            for d in range(2):
                                cc = cc_ if d == 0 else 7 - cc_
                                nk = len(segs_d[d])
                                S.op("dve", lambda e, d=d, cc=cc, nk=nk, pp=pp: e.tensor_tensor(
                                    out=W[d][:, 0:nk, :], in0=Z[d][pp][:, 0:nk, :], in1=Pb[d][i2][:, 0:nk, :, cc], op=ALU.add),
                                    [R_Z[d][pp], R_Pb[d][i2]], [R_W[d]], nosame=True)
                            yield
                            for d in range(2):
                                nk = len(segs_d[d])
                                S.op("dve", lambda e, d=d, nk=nk: e.tensor_tensor(
                                    out=T1[d][:, 0:nk, :], in0=W[d][:, 0:nk, :],
                                    in1=Ar[:, d * 64:(d + 1) * 64].unsqueeze(1).to_broadcast([128, nk, 64]), op=ALU.mult),
                                    [R_W[d], R_A], [R_T1[d]], nosame=True)
                            yield
                            for d in range(2):
                                nk = len(segs_d[d])
                                S.op("dve", lambda e, d=d, nk=nk: e.tensor_tensor(
                                    out=T2[d][0:64, 0:nk, :], in0=W[d][64:128, 0:nk, :],
                                    in1=Aisw[64:128, d * 64:(d + 1) * 64].unsqueeze(1).to_broadcast([64, nk, 64]), op=ALU.mult),
                                    [R_W[d], R_A], [R_T2[d]], nosame=True)
                            yield
                            for d in range(2):
                                nk = len(segs_d[d])
                                S.op("dve", lambda e, d=d, nk=nk: e.tensor_tensor(
                                    out=T2[d][64:128, 0:nk, :], in0=W[d][0:64, 0:nk, :],
                                    in1=Aisw[0:64, d * 64:(d + 1) * 64].unsqueeze(1).to_broadcast([64, nk, 64]), op=ALU.mult),
                                    [R_W[d], R_A], [R_T2[d]], nosame=True)
                            yield
                            for d in range(2):
                                nk = len(segs_d[d])
                                S.op("dve", lambda e, d=d, nk=nk, nx=nx: e.tensor_tensor(
                                    out=Z[d][nx][:, 0:nk, :], in0=T1[d][:, 0:nk, :], in1=T2[d][:, 0:nk, :], op=ALU.add),
                                    [R_T1[d], R_T2[d]], [R_Z[d][nx]], nosame=True)
                            pp = nx
                            yield
                        for d in range(2):
                            cb = tb if d == 0 else NCB - 1 - tb
                            for k, sg in enumerate(segs_d[d]):
                                S.dma("pool", Zd[sg, d, cb, :, :], Zb[d][k][i2][:].rearrange("p g c -> p (g c)"),
                                      [R_Zb[d][k][i2]], [R_Zd[sg][d]], R_Zb[d][k][i2])


            stX.close()
            with ExitStack() as stc:
                ga = att_phase(stc)
                gs = sweep_phase(stc)
                done_a = done_s = False
                while not (done_a and done_s):
                    if not done_a:
                        for _ in range(ATT_PER_SWEEP):
                            try:
                                next(ga)
                            except StopIteration:
                                done_a = True
                                break
                    if not done_s:
                        try:
                            next(gs)
                        except StopIteration:
                            done_s = True
                S.barrier()
            with ExitStack() as st:
                sel = sb(st, "sel", [128, 64, 128], BF16)
                R_sel = S.res("sel")
                S.dma("pool", sel[:], s5_sel_in[:, :, :], [], [R_sel], R_sel)
                Uo = [sb(st, "Uo%d" % i, [128, 8, 64, 8], BF16) for i in range(2)]
                Zo = [[sb(st, "Zo%d%d" % (d, i), [128, 8, 64, 8], BF16) for i in range(2)] for d in range(2)]
                R_Uo = [S.res("Uo") for _ in range(2)]
                R_Zo = [[S.res("Zo") for _ in range(2)] for _ in range(2)]
                Yg = sb(st, "Yg", [128, 64, 64], BF16)
                R_Yg = [S.res("Yg") for _ in range(8)]
                ygS = [sb(st, "ygS%d" % i, [128, 8, 512], BF16) for i in range(2)]
                R_ygS = [S.res("ygS") for _ in range(2)]
                psy = [ps(st, "psy%d" % i, [128, 512], F32) for i in range(3)]
                R_psy = [S.res("psy") for _ in range(3)]
                pss = [ps(st, "pss%d" % i, [128, 512], F32) for i in range(3)]
                R_pss = [S.res("pss") for _ in range(3)]
                it = 0
                yc = 0
                sc = 0
                for s in range(NSEG):
                    for blk in range(4):
                        b = it % 2
                        it += 1
                        S.dma("sp", Uo[b][:].rearrange("p cb g c -> p cb (g c)"),
                              Ud[s, blk * 8:(blk + 1) * 8, :, :].rearrange("cb p f -> p cb f"), [R_Ud], [R_Uo[b]], R_Uo[b])
                        for d in range(2):
                            S.dma("sp", Zo[d][b][:].rearrange("p cb g c -> p cb (g c)"),
                                  Zd[s, d, blk * 8:(blk + 1) * 8, :, :].rearrange("cb p f -> p cb f"), [R_Zd[s][d]],
                                  [R_Zo[d][b]], R_Zo[d][b])
                        for o in range(8):
                            pi_ = yc % 3
                            yc += 1

                            def mmy(e, o=o, pi_=pi_, b=b):
                                ins = None
                                for gg in range(8):
                                    g = o * 8 + gg
                                    outp = psy[pi_][:, gg * 64:(gg + 1) * 64]
                                    e.matmul(outp, lhsT=Mall[:, g, :], rhs=Uo[b][:, :, g, :], start=True, stop=False)
                                    e.matmul(outp, lhsT=Ym[:, g, :], rhs=Zo[0][b][:, :, g, :], start=False, stop=False)
                                    ins = e.matmul(outp, lhsT=Ym[:, 64 + g, :], rhs=Zo[1][b][:, :, g, :], start=False, stop=True)
                                return ins
                            S.op("pe", mmy, [R_M, R_Ym, R_Uo[b], R_Zo[0][b], R_Zo[1][b]], [R_psy[pi_]])
                            S.op("act", lambda e, o=o, pi_=pi_: e.activation(
                                out=Yg[:, o * 8:(o + 1) * 8, :].rearrange("p g c -> p (g c)"), in_=psy[pi_][:],
                                func=AF.Gelu_apprx_tanh), [R_psy[pi_]], [R_Yg[o]])
                        for o in range(8):
                            si = sc % 3
                            sc += 1

                            def mms(e, o=o, si=si):
                                ins = None
                                for j in range(8):
                                    for gg in range(8):
                                        ins = e.matmul(pss[si][:, j * 64:(j + 1) * 64], lhsT=sel[:, j * 8 + gg, :],
                                                       rhs=Yg[:, o * 8 + gg, :], start=(gg == 0), stop=(gg == 7))
                                return ins
                            S.op("pe", mms, [R_sel, R_Yg[o]], [R_pss[si]])
                            S.op("dve", lambda e, o=o, si=si, b=b: e.tensor_copy(
                                out=ygS[b][:, o, :].rearrange("p (c j) -> p j c", j=8),
                                in_=pss[si][:].rearrange("p (j c) -> p j c", j=8)), [R_pss[si]], [R_ygS[b]])
                        S.dma("pool", ygT_d[s, :, :, blk * 512:(blk + 1) * 512].rearrange("c p t -> p c t"), ygS[b][:],
                              [R_ygS[b]], [R_scr["ygT"]], R_ygS[b])
                S.barrier()
    if DBG_S5_IDENT:
        with ExitStack() as st:
            ub = sb(st, "ub", [128, 1024], BF16)
            R_ub = S.res("ub")
            ut = sb(st, "ut", [128, 8, 128], BF16)
            R_ut = S.res("ut")
            pst = ps(st, "pst", [128, 1024], BF16)
            R_pst = S.res("pst")
            for s in range(NSEG):
                for tb in range(SEG // 128):
                    S.dma("sp", ub[:], u_d[s, tb * 128:(tb + 1) * 128, :], [R_scr["u"]], [R_ub], R_ub)

                    def tr(e):
                        ins = None
                        for c in range(8):
                            ins = e.transpose(out=pst[:, c * 128:(c + 1) * 128], in_=ub[:, c * 128:(c + 1) * 128],
                                              identity=identb[:])
                        return ins
                    S.op("pe", tr, [R_ub], [R_pst])
                    S.op("act", lambda e: e.activation(out=ut[:], in_=pst[:].rearrange("p (c t) -> p c t", c=8),
                                                       func=AF.Gelu_apprx_tanh), [R_pst], [R_ut])
                    S.dma("pool", ygT_d[s, :, :, tb * 128:(tb + 1) * 128].rearrange("c p t -> p c t"), ut[:], [R_ut],
                          [R_scr["ygT"]], R_ut)
            S.barrier()

    if STOP_AFTER == '2':
        es.close()
        return nc
    if DBG_ATT_IDENT:
        with ExitStack() as st:
            vb = sb(st, "vb", [128, 1024], BF16)
            R_vb = S.res("vb")
            azb = sb(st, "azb", [128, 8, 128], BF16)
            R_azb = S.res("azb")
            vt = sb(st, "vt", [128, 8, 128], BF16)
            R_vt = S.res("vt")
            pst = ps(st, "pst", [128, 1024], BF16)
            R_pst = S.res("pst")
            for s in range(NSEG):
                for tb in range(SEG // 128):
                    S.dma("sp", vb[:], v_d[s, tb * 128:(tb + 1) * 128, :], [R_scr["v"]], [R_vb], R_vb)
                    S.dma("sp", azb[:], azT_d[s, :, :, tb * 128:(tb + 1) * 128].rearrange("c p t -> p c t"),
                          [R_scr["azT"]], [R_azb], R_azb)

                    def tr(e):
                        ins = None
                        for c in range(8):
                            ins = e.transpose(out=pst[:, c * 128:(c + 1) * 128], in_=vb[:, c * 128:(c + 1) * 128],
                                              identity=identb[:])
                        return ins
                    S.op("pe", tr, [R_vb], [R_pst])
                    S.op("dve", lambda e: e.tensor_tensor(out=vt[:], in0=pst[:].rearrange("p (c t) -> p c t", c=8),
                                                          in1=azb[:], op=ALU.mult), [R_pst, R_azb], [R_vt])
                    S.dma("pool", ozT_d[s, :, :, tb * 128:(tb + 1) * 128].rearrange("c p t -> p c t"), vt[:], [R_vt],
                          [R_scr["ozT"]], R_vt)
            S.barrier()

    if STOP_AFTER == '3':
        es.close()
        return nc
    with ExitStack() as st:
        mk_slabs(st)
        ygT = sb(st, "ygT", [128, 8, TT], BF16)
        szT = sb(st, "szT", [128, 8, TT], BF16)
        ozT = sb(st, "ozT", [128, 8, TT], BF16)
        gsT = sb(st, "gsT", [128, 16, TT], BF16)
        gaT = sb(st, "gaT", [128, 16, TT], BF16)
        R_yg, R_sz, R_oz, R_gs, R_ga = (S.res(n) for n in ("ygT", "szT", "ozT", "gsT", "gaT"))
        R_szc = [S.res("szc") for _ in range(8)]
        R_gsc = [S.res("gsc") for _ in range(16)]
        xh = sb(st, "xh", [128, 4, D], F32)
        R_xh = [S.res("xh") for _ in range(4)]
        pin = sb(st, "pin", [128, 4, 256], F32)
        pinb = sb(st, "pinb", [128, 4, 256], BF16)
        R_pin = S.res("pin")
        R_pinb = S.res("pinb")
        pT = sb(st, "pT", [128, 2, TT], BF16)
        R_pT = S.res("pT")
        fing = sb(st, "fing", [128, D], F32)
        R_fing = S.res("fing")
        S.dma("sp", fing[:], fing_in[:, :], [], [R_fing], R_fing)
        hnb = sb(st, "hnb", [128, 4, D], BF16)
        R_hnb = [S.res("hnb") for _ in range(4)]
        hn2T = sb(st, "hn2T", [128, 16, TT], BF16)
        R_hn2T = [S.res("hn2T") for _ in range(16)]
        junk = sb(st, "junk4", [128, D], BF16)
        R_junk = S.res("junk4")
        ss = sb(st, "ss4", [128, 16], F32)
        R_ss = S.res("ss4")
        sig = sb(st, "sig", [128, 512], BF16)
        R_sig = S.res("sig")
        tmpf = [sb(st, "tmpf%d" % i, [128, 512], F32) for i in range(2)]
        R_tmpf = [S.res("tmpf") for _ in range(2)]
        tmpg = [sb(st, "tmpg%d" % i, [128, 512], F32) for i in range(2)]
        R_tmpg = [S.res("tmpg") for _ in range(2)]
        psT2 = [ps(st, "psT40", [128, 1024], BF16), ps(st, "psT41", [128, 1024], BF16)]
        R_psT = [S.res("psT0"), S.res("psT1")]
        psA = [ps(st, "psB%d" % i, [128, 512], F32) for i in range(6)]
        R_psA = [S.res("psB%d" % i) for i in range(6)]
        pa_ctr = 0

        def nextps():
            nonlocal pa_ctr
            i = pa_ctr % 6
            pa_ctr += 1
            return i

        def tile_gen(s, t):
            tok0 = t * TT
            tsl = slice(tok0, tok0 + TT)
            for (buf, dsrc, rr, rn, nch) in ((ygT, ygT_d, R_yg, "ygT", 8), (szT, szT_d, R_sz, "szT", 8),
                                             (ozT, ozT_d, R_oz, "ozT", 8), (gsT, gsT_d, R_gs, "gsT", 16),
                                             (gaT, gaT_d, R_ga, "gaT", 16)):
                extra = R_szc if buf is szT else (R_gsc if buf is gsT else [])
                S.dma("sp", buf[:], dsrc[s, :, :, tsl].rearrange("c p t -> p c t"), [R_scr[rn]], [rr] + extra, rr)
            for j in range(2):
                slab, rslab = load_slab(wb_glu, R_w["glu"], 8, j * 512)
                for m in range(4):
                    fo = j * 4 + m
                    pi = nextps()

                    def mm(e, m=m, pi=pi, slab=slab):
                        ins = None
                        for kc in range(8):
                            ins = e.matmul(psA[pi][:], lhsT=slab[:, kc, m * 128:(m + 1) * 128], rhs=ygT[:, kc, :],
                                           start=(kc == 0), stop=(kc == 7))
                        return ins
                    S.op("pe", mm, [rslab, R_yg], [R_psA[pi]])
                    S.op("act", lambda e, pi=pi, fo=fo: e.activation(out=sig[:], in_=psA[pi][:], func=AF.Sigmoid,
                                                                     bias=glub[:, fo:fo + 1]), [R_psA[pi]], [R_sig])
                    S.op("dve", lambda e, fo=fo: e.tensor_tensor(out=sig[:], in0=sig[:], in1=ygT[:, fo, :], op=ALU.mult),
                         [R_sig, R_yg], [R_sig])
                    S.op("dve", lambda e, fo=fo: e.tensor_tensor(out=szT[:, fo, :], in0=sig[:], in1=szT[:, fo, :],
                                                                 op=ALU.mult), [R_sig, R_sz], [R_szc[fo]])
            for j in range(4):
                slab_s, rs_s = load_slab(wb_s, R_w["s"], 8, j * 512)
                slab_a, rs_a = load_slab(wb_a, R_w["a"], 8, j * 512)
                for m in range(4):
                    dm = j * 4 + m
                    p1 = nextps()
                    p2 = nextps()

                    def mm1(e, m=m, p1=p1, slab=slab_s):
                        ins = None
                        for kc in range(8):
                            ins = e.matmul(psA[p1][:], lhsT=slab[:, kc, m * 128:(m + 1) * 128], rhs=szT[:, kc, :],
                                           start=(kc == 0), stop=(kc == 7))
                        return ins

                    def mm2(e, m=m, p2=p2, slab=slab_a):
                        ins = None
                        for kc in range(8):
                            ins = e.matmul(psA[p2][:], lhsT=slab[:, kc, m * 128:(m + 1) * 128], rhs=ozT[:, kc, :],
                                           start=(kc == 0), stop=(kc == 7))
                        return ins
                    S.op("pe", mm1, [rs_s] + R_szc, [R_psA[p1]])
                    S.op("pe", mm2, [rs_a, R_oz], [R_psA[p2]])
                    S.op("dve", lambda e, dm=dm, p1=p1: e.tensor_tensor(out=tmpf[0][:], in0=psA[p1][:], in1=gsT[:, dm, :],
                                                                        op=ALU.mult), [R_psA[p1], R_gs], [R_tmpf[0]])
                    S.op("dve", lambda e, dm=dm, p2=p2: e.tensor_tensor(out=tmpf[1][:], in0=psA[p2][:], in1=gaT[:, dm, :],
                                                                        op=ALU.mult), [R_psA[p2], R_ga], [R_tmpf[1]])
                    S.op("pool", lambda e, dm=dm: e.tensor_tensor(out=gsT[:, dm, :], in0=tmpf[0][:], in1=tmpf[1][:],
                                                                  op=ALU.add), [R_tmpf[0], R_tmpf[1], R_gs], [R_gsc[dm]])
            yield
            pre_slabs = [load_slab(wb_o, R_w["o"], 16, jj * 512) for jj in range(3)]
            S.dma("sp", pin[:], p_in[s, tsl, :].rearrange("(a p) d -> p a d", p=128), [], [R_pin], R_pin)
            for a in range(4):
                S.dma("sp", xh[:, a, :], x_in[s, tok0 + a * 128:tok0 + (a + 1) * 128, :], [], [R_xh[a]], R_xh[a])
            for j in range(4):
                if j == 1:
                    pre_slabs.append(load_slab(wb_o, R_w["o"], 16, 3 * 512))
                slab, rslab = pre_slabs[j]
                for a in range(4):
                    pi = nextps()

                    def mm(e, a=a, pi=pi, slab=slab):
                        ins = None
                        for kc in range(16):
                            ins = e.matmul(psA[pi][:], lhsT=gsT[:, kc, a * 128:(a + 1) * 128], rhs=slab[:, kc, :],
                                           start=(kc == 0), stop=(kc == 15))
                        return ins
                    S.op("pe", mm, [rslab] + R_gsc, [R_psA[pi]])
                    S.op("dve", lambda e, a=a, j=j, pi=pi: e.tensor_tensor(
                        out=xh[:, a, j * 512:(j + 1) * 512], in0=psA[pi][:], in1=xh[:, a, j * 512:(j + 1) * 512],
                        op=ALU.add), [R_psA[pi], R_xh[a]], [R_xh[a]])
            yield
            for a in range(4):
                S.op("act", lambda e, a=a: e.activation(out=junk[:], in_=xh[:, a, :], func=AF.Square,
                                                        accum_out=ss[:, a:a + 1]), [R_xh[a]], [R_junk, R_ss])
            S.op("dve", lambda e: e.tensor_scalar(out=ss[:, 4:8], in0=ss[:, 0:4], scalar1=1.0 / D, scalar2=EPS,
                                                  op0=ALU.mult, op1=ALU.add), [R_ss], [R_ss])
            S.op("act", lambda e: e.activation(out=ss[:, 4:8], in_=ss[:, 4:8], func=AF.Sqrt), [R_ss], [R_ss])
            S.op("dve", lambda e: e.reciprocal(out=ss[:, 4:8], in_=ss[:, 4:8]), [R_ss], [R_ss])
            for a in range(4):
                if a % 2 == 0:
                    S.op("dve", lambda e, a=a: e.tensor_scalar(out=hnb[:, a, :], in0=xh[:, a, :], scalar1=ss[:, 4 + a:5 + a],
                                                               scalar2=None, op0=ALU.mult), [R_xh[a], R_ss], [R_hnb[a]])
                else:
                    S.op("act", lambda e, a=a: e.activation(out=hnb[:, a, :], in_=xh[:, a, :], func=AF.Copy,
                                                            scale=ss[:, 4 + a:5 + a]), [R_xh[a], R_ss], [R_hnb[a]])
            for kc in range(16):
                h = kc % 2

                def tr(e, kc=kc, h=h):
                    ins = None
                    for a in range(4):
                        ins = e.transpose(out=psT2[h][:, a * 128:(a + 1) * 128],
                                          in_=hnb[:, a, kc * 128:(kc + 1) * 128], identity=identb[:])
                    return ins
                S.op("pe", tr, R_hnb, [R_psT[h]])
                S.op("dve", lambda e, kc=kc, h=h: e.tensor_scalar(out=hn2T[:, kc, :], in0=psT2[h][:, 0:512],
                                                                   scalar1=pleg[:, kc:kc + 1], scalar2=None, op0=ALU.mult),
                     [R_psT[h]], [R_hn2T[kc]])
            S.op("act", lambda e: e.activation(out=pinb[:], in_=pin[:], func=AF.Copy), [R_pin], [R_pinb])
            for kc in range(2):
                h = kc % 2

                def tr(e, kc=kc, h=h):
                    ins = None
                    for a in range(4):
                        ins = e.transpose(out=psT2[h][:, a * 128:(a + 1) * 128],
                                          in_=pinb[:, a, kc * 128:(kc + 1) * 128], identity=identb[:])
                    return ins
                S.op("pe", tr, [R_pinb], [R_psT[h]])
                S.op("dve", lambda e, kc=kc, h=h: e.tensor_copy(out=pT[:, kc, :], in_=psT2[h][:, 0:512]),
                     [R_psT[h]], [R_pT])
            for j in range(4):
                slab_g, rs_g = load_slab(wb_pg, R_w["pg"], 16, j * 512)
                slab_p, rs_p = load_slab(wb_pp, R_w["pp"], 2, j * 512)
                for a in range(4):
                    p1 = nextps()
                    p2 = nextps()

                    def mm1(e, a=a, p1=p1, slab=slab_g):
                        ins = None
                        for kc in range(16):
                            ins = e.matmul(psA[p1][:], lhsT=hn2T[:, kc, a * 128:(a + 1) * 128], rhs=slab[:, kc, :],
                                           start=(kc == 0), stop=(kc == 15))
                        return ins

                    def mm2(e, a=a, p2=p2, slab=slab_p):
                        ins = None
                        for kc in range(2):
                            ins = e.matmul(psA[p2][:], lhsT=pT[:, kc, a * 128:(a + 1) * 128], rhs=slab[:, kc, :],
                                           start=(kc == 0), stop=(kc == 1))
                        return ins
                    S.op("pe", mm1, [rs_g] + R_hn2T, [R_psA[p1]])
                    S.op("pe", mm2, [rs_p, R_pT], [R_psA[p2]])
                    S.op("act", lambda e, p1=p1: e.activation(out=tmpg[0][:], in_=psA[p1][:], func=AF.Sigmoid),
                         [R_psA[p1]], [R_tmpg[0]])
                    S.op("dve", lambda e, p2=p2: e.tensor_tensor(out=tmpg[1][:], in0=psA[p2][:], in1=tmpg[0][:],
                                                                 op=ALU.mult), [R_psA[p2], R_tmpg[0]], [R_tmpg[1]])
                    S.op("pool", lambda e, a=a, j=j: e.tensor_tensor(
                        out=xh[:, a, j * 512:(j + 1) * 512], in0=tmpg[1][:], in1=xh[:, a, j * 512:(j + 1) * 512],
                        op=ALU.add), [R_tmpg[1], R_xh[a]], [R_xh[a]])
            for a in range(4):
                S.op("act", lambda e, a=a: e.activation(out=junk[:], in_=xh[:, a, :], func=AF.Square,
                                                        accum_out=ss[:, 8 + a:9 + a]), [R_xh[a]], [R_junk, R_ss])
            S.op("dve", lambda e: e.tensor_scalar(out=ss[:, 12:16], in0=ss[:, 8:12], scalar1=1.0 / D, scalar2=EPS,
                                                  op0=ALU.mult, op1=ALU.add), [R_ss], [R_ss])
            S.op("act", lambda e: e.activation(out=ss[:, 12:16], in_=ss[:, 12:16], func=AF.Sqrt), [R_ss], [R_ss])
            S.op("dve", lambda e: e.reciprocal(out=ss[:, 12:16], in_=ss[:, 12:16]), [R_ss], [R_ss])
            for a in range(4):
                S.op("dve", lambda e, a=a: e.scalar_tensor_tensor(out=xh[:, a, :], in0=xh[:, a, :],
                                                                  scalar=ss[:, 12 + a:13 + a], in1=fing[:],
                                                                  op0=ALU.mult, op1=ALU.mult),
                     [R_xh[a], R_ss, R_fing], [R_xh[a]])
                S.dma("pool", y_out[s, tok0 + a * 128:tok0 + (a + 1) * 128, :], xh[:, a, :], [R_xh[a]], [], R_xh[a])

        tiles = [(s_, t_) for s_ in range(NSEG) for t_ in range(SEG // TT)]
        gens = [tile_gen(s_, t_) for (s_, t_) in tiles]
        next(gens[0])
        for i in range(len(tiles)):
            next(gens[i])
            if i + 1 < len(tiles):
                next(gens[i + 1])
            for _ in gens[i]:
                pass
        S.barrier()
    es.close()
    return nc


_NC_CACHE = {}


def _core_segments(i):
    if i < 4:
        return [("p", i, 0), ("p", i, SEG), ("s", i, 0)]
    b = 4 + 3 * (i - 4)
    return [("s", b, 0), ("s", b + 1, 0), ("s", b + 2, 0)]


def _bucket(rel):
    half = 16
    max_exact = 8
    ret = (rel > 0).astype(np.int32) * half
    n = np.abs(rel)
    nf = np.maximum(n, 1).astype(np.float32)
    large = max_exact + (np.log(nf / max_exact) / math.log(128 / max_exact) * (half - max_exact)).astype(np.int32)
    large = np.minimum(large, half - 1)
    return ret + np.where(n < max_exact, n, large)


def kernel(**inp):
    f32 = np.float32
    xp, xs_ = np.asarray(inp["x_prompt"], f32), np.asarray(inp["x_sample"], f32)
    pp, ps_ = np.asarray(inp["p_prompt"], f32)[0], np.asarray(inp["p_sample"], f32)[0]
    if "nc" not in _NC_CACHE:
        _NC_CACHE["nc"] = build_nc()
    nc = _NC_CACHE["nc"]

    def chunkcols(v, n):
        return np.ascontiguousarray(np.asarray(v, f32).reshape(n, 128).T)

    nn = np.arange(1536)
    oh = np.zeros((32, 1536), f32)
    oh[_bucket(767 - nn), nn] = 1.0
    lamv = np.stack([np.asarray(inp[k], f32)[0] for k in ("lam_q1", "lam_k1", "lam_q2", "lam_k2")])
    common = {
        "ident": np.eye(128, dtype=f32),
        "w_in": np.asarray(inp["w_in"], f32)[0],
        "glu_w": np.asarray(inp["glu_w"], f32)[0],
        "w_branch_s": np.asarray(inp["w_branch_s"], f32)[0],
        "w_branch_a": np.asarray(inp["w_branch_a"], f32)[0],
        "w_out": np.asarray(inp["w_out"], f32)[0],
        "ple_gate_w": np.asarray(inp["ple_gate_w"], f32)[0],
        "ple_proj_w": np.asarray(inp["ple_proj_w"], f32)[0],
        "norm_g": chunkcols(inp["norm_g"][0], 16),
        "ple_norm_g": chunkcols(inp["ple_norm_g"][0], 16),
        "glu_b": chunkcols(inp["glu_b"][0], 8),
        "subln_g": chunkcols(inp["subln_g"][0], 2),
        "final_g": np.ascontiguousarray(np.broadcast_to(np.asarray(inp["final_g"], f32)[None, :], (128, D))),
        "lamv": np.ascontiguousarray(np.broadcast_to(lamv[None], (128, 4, 128))),
        "rel_bias": np.asarray(inp["rel_bias"], f32),
        "onehot": oh,
    }
    def nmaj(a):
        return np.ascontiguousarray(np.asarray(a, f32)[0].transpose(2, 0, 1).reshape(64, 128))
    lr = nmaj(inp["ssm_lambda_re"]); li = nmaj(inp["ssm_lambda_im"])
    ldt = np.ascontiguousarray(np.broadcast_to(np.asarray(inp["ssm_log_dt"], f32)[0].reshape(1, 128), (64, 128)))
    br = np.asarray(inp["ssm_b_re"], f32)[0].transpose(2, 0, 1, 3).reshape(64, 128, 16)
    bim = np.asarray(inp["ssm_b_im"], f32)[0].transpose(2, 0, 1, 3).reshape(64, 128, 16)
    cr = np.asarray(inp["ssm_c_re"], f32)[0].transpose(3, 0, 1, 2).reshape(64, 128, 16)
    cim = np.asarray(inp["ssm_c_im"], f32)[0].transpose(3, 0, 1, 2).reshape(64, 128, 16)
    jj = np.arange(8, dtype=f32)
    kx = np.zeros((128, 128, 8), f32); ky = np.zeros((128, 128, 8), f32)
    kx[:, :64, :] = -jj; kx[:, 64:, :] = jj - 7.0
    ky[:, :64, :] = jj; ky[:, 64:, :] = 7.0 - jj
    pj = np.arange(128) // 16
    pc_ = np.arange(128) % 16
    dvec = np.asarray(inp["ssm_d"], f32)[0].reshape(64, 16)
    sel = np.zeros((128, 64, 128), f32)
    for j in range(8):
        for g8 in range(8):
            for co in range(16):
                sel[j * 16 + co, j * 8 + g8, g8 * 16 + co] = 1.0
    common.update({
        "s5_lr": np.concatenate([lr, lr]), "s5_li": np.concatenate([li, li]), "s5_ldt": np.concatenate([ldt, ldt]),
        "s5_ba": np.concatenate([br, bim]), "s5_bb": np.concatenate([bim, br]),
        "s5_ca": np.concatenate([cr, cim]), "s5_cb": np.concatenate([cim, cr]),
        "s5_kx": kx, "s5_ky": ky,
        "s5_dcol": np.ascontiguousarray(dvec.T[pc_, :]),
        "s5_maskl": (pj[None, :] >= pj[:, None]).astype(f32),
        "s5_masku": (pj[None, :] <= pj[:, None]).astype(f32),
        "s5_sel": sel,
    })
    in_maps = []
    for i in range(8):
        segs = _core_segments(i)
        xs = np.stack([(xp if g == "p" else xs_)[b, st:st + SEG] for (g, b, st) in segs])
        pc = np.stack([(pp if g == "p" else ps_)[b, st:st + SEG] for (g, b, st) in segs])
        fl = np.zeros((128, 2), f32)
        fl[:, 0] = 1.0 if i < 4 else 0.0
        fl[:, 1] = 0.0 if i < 4 else -30000.0
        m = dict(common)
        m.update({"x": np.ascontiguousarray(xs), "p": np.ascontiguousarray(pc), "flags": fl})
        in_maps.append(m)
    res = run_bass_kernel_spmd(nc, in_maps, core_ids=list(range(8)))
    y_p = np.zeros_like(xp)
    y_s = np.zeros_like(xs_)
    for i in range(8):
        y = res.results[i]["y"]
        for j, (g, b, st) in enumerate(_core_segments(i)):
            (y_p if g == "p" else y_s)[b, st:st + SEG] = y[j]
    return (y_p, y_s)
```

```python
import math
from contextlib import ExitStack
import numpy as np
import concourse.bass as bass
import concourse.mybir as mybir
from concourse.bass_utils import run_bass_kernel_spmd

F32 = mybir.dt.float32
BF16 = mybir.dt.bfloat16
AF = mybir.ActivationFunctionType
ALU = mybir.AluOpType
AX = mybir.AxisListType

D = 2048
SEG = 2048
NSEG = 3
TT = 512
NQ = 10240
EPS = 1e-6
LAM_INIT = 0.8 - 0.6 * math.exp(-0.3 * 0)

DBG_S5_IDENT = False
DBG_ATT_IDENT = False
STOP_AFTER = None
ATT_LIM = None
ATT_PER_SWEEP = 1
P1_NSEG = NSEG
P1_NT = SEG // TT
P1_PLAN = None


class Res:
    __slots__ = ("name", "w", "r", "sem", "cnt")

    def __init__(self, name):
        self.name = name
        self.w = {}
        self.r = {}
        self.sem = None
        self.cnt = 0


class Sched:
    def __init__(self, nc, es):
        self.nc = nc
        self.es = es
        self.e = {"pe": nc.tensor, "act": nc.scalar, "dve": nc.vector, "pool": nc.gpsimd, "sp": nc.sync}
        self.sem = {k: es.enter_context(nc.semaphore("c_" + k)) for k in ("pe", "act", "dve", "pool")}
        self.cnt = {k: 0 for k in self.sem}
        self.waited = {k: {} for k in self.e}
        self.dres = []
        self.nres = 0

    def res(self, name):
        self.nres += 1
        return Res("%s_%d" % (name, self.nres))

    def _deps(self, eng, reads, writes, nosame=False):
        need = {}
        for r in reads:
            for k, v in r.w.items():
                if need.get(k, 0) < v:
                    need[k] = v
        for w in writes:
            for k, v in w.w.items():
                if need.get(k, 0) < v:
                    need[k] = v
            for k, v in w.r.items():
                if need.get(k, 0) < v:
                    need[k] = v
        wd = self.waited[eng]
        for k, v in need.items():
            if eng == "pe" and k is self.sem["pe"]:
                continue
            if nosame and eng in self.sem and k is self.sem[eng]:
                continue
            if wd.get(k, 0) < v:
                wd[k] = v
                self.e[eng].wait_ge(k, v)

    def op(self, eng, fn, reads=(), writes=(), nosame=False):
        self._deps(eng, reads, writes, nosame)
        self.cnt[eng] += 1
        c = self.cnt[eng]
        sem = self.sem[eng]
        fn(self.e[eng]).then_inc(sem, 1)
        for r in reads:
            r.r[sem] = c
        for w in writes:
            w.w[sem] = c
            w.r = {}

    def dma(self, q, out, in_, reads, writes, semres):
        self._deps(q, reads, writes)
        if semres.sem is None:
            semres.sem = self.es.enter_context(self.nc.semaphore("d_" + semres.name))
            self.dres.append(semres)
        semres.cnt += 16
        v = semres.cnt
        sem = semres.sem
        self.e[q].dma_start(out=out, in_=in_).then_inc(sem, 16)
        for r in reads:
            r.r[sem] = v
        for w in writes:
            w.w[sem] = v
            w.r = {}

    def barrier(self):
        for eng in self.e:
            wd = self.waited[eng]
            for k in self.sem:
                v = self.cnt[k]
                if v > 0 and wd.get(self.sem[k], 0) < v:
                    wd[self.sem[k]] = v
                    self.e[eng].wait_ge(self.sem[k], v)
            for dr in self.dres:
                if dr.cnt > 0 and wd.get(dr.sem, 0) < dr.cnt:
                    wd[dr.sem] = dr.cnt
                    self.e[eng].wait_ge(dr.sem, dr.cnt)


def build_nc():
    nc = bass.Bass("TRN2", target_bir_lowering=False)
    es = ExitStack()
    S = Sched(nc, es)

    def din(name, shape, dt=F32):
        return nc.dram_tensor(name, list(shape), dt, kind="ExternalInput").ap()

    def dscr(name, shape, dt=BF16):
        return nc.dram_tensor("scr_" + name, list(shape), dt).ap()

    x_in = din("x", [NSEG, SEG, D])
    p_in = din("p", [NSEG, SEG, 256])
    flags_in = din("flags", [128, 2])
    ident_in = din("ident", [128, 128])
    w_in_f = din("w_in", [D, NQ])
    glu_w_f = din("glu_w", [1024, 1024])
    wbs_f = din("w_branch_s", [1024, D])
    wba_f = din("w_branch_a", [1024, D])
    wo_f = din("w_out", [D, D])
    wpg_f = din("ple_gate_w", [D, D])
    wpp_f = din("ple_proj_w", [256, D])
    normg_in = din("norm_g", [128, 16])
    pleg_in = din("ple_norm_g", [128, 16])
    glub_in = din("glu_b", [128, 8])
    subg_in = din("subln_g", [128, 2])
    fing_in = din("final_g", [128, D])
    lamv_in = din("lamv", [128, 4, 128])
    relb_in = din("rel_bias", [32, 4])
    onehot_in = din("onehot", [32, 1536])
    s5_lr_in = din("s5_lr", [128, 128])
    s5_li_in = din("s5_li", [128, 128])
    s5_ldt_in = din("s5_ldt", [128, 128])
    s5_ba_in = din("s5_ba", [128, 128, 16])
    s5_bb_in = din("s5_bb", [128, 128, 16])
    s5_ca_in = din("s5_ca", [128, 128, 16])
    s5_cb_in = din("s5_cb", [128, 128, 16])
    s5_kx_in = din("s5_kx", [128, 128, 8])
    s5_ky_in = din("s5_ky", [128, 128, 8])
    s5_dcol_in = din("s5_dcol", [128, 64])
    s5_maskl_in = din("s5_maskl", [128, 128])
    s5_masku_in = din("s5_masku", [128, 128])
    s5_sel_in = din("s5_sel", [128, 64, 128])
    y_out = nc.dram_tensor("y", [NSEG, SEG, D], F32, kind="ExternalOutput").ap()

    wb_in = dscr("wb_in", [D, NQ])
    wb_glu = dscr("wb_glu", [1024, 1024])
    wb_s = dscr("wb_s", [1024, D])
    wb_a = dscr("wb_a", [1024, D])
    wb_o = dscr("wb_o", [D, D])
    wb_pg = dscr("wb_pg", [D, D])
    wb_pp = dscr("wb_pp", [256, D])
    qT_d = dscr("qT", [NSEG, 8, 128, SEG])
    kT_d = dscr("kT", [NSEG, 8, 128, SEG])
    v_d = dscr("v", [NSEG, SEG, 1024])
    u_d = dscr("u", [NSEG, SEG, 1024])
    szT_d = dscr("szT", [NSEG, 8, 128, SEG])
    azT_d = dscr("azT", [NSEG, 8, 128, SEG])
    gsT_d = dscr("gsT", [NSEG, 16, 128, SEG])
    gaT_d = dscr("gaT", [NSEG, 16, 128, SEG])
    ygT_d = dscr("ygT", [NSEG, 8, 128, SEG])
    ozT_d = dscr("ozT", [NSEG, 8, 128, SEG])
    R_scr = {n: S.res(n) for n in ("qT", "kT", "v", "u", "szT", "azT", "gsT", "gaT", "ygT", "ozT")}

    uniq = [0]

    def sb(stack, name, shape, dt):
        uniq[0] += 1
        return stack.enter_context(nc.sbuf_tensor("sb%d_%s" % (uniq[0], name), list(shape), dt))

    def ps(stack, name, shape, dt=F32):
        uniq[0] += 1
        return stack.enter_context(nc.psum_tensor("ps%d_%s" % (uniq[0], name), list(shape), dt))

    identf = sb(es, "identf", [128, 128], F32)
    identb = sb(es, "identb", [128, 128], BF16)
    normg = sb(es, "normg", [128, 16], F32)
    pleg = sb(es, "pleg", [128, 16], F32)
    glub = sb(es, "glub", [128, 8], F32)
    subg = sb(es, "subg", [128, 2], F32)
    flags = sb(es, "flags", [128, 2], F32)
    lamv = sb(es, "lamv", [128, 4, 128], F32)
    lamt = sb(es, "lamt", [128, 8], F32)
    R_const = S.res("const")
    for dst, src in ((identf, ident_in), (normg, normg_in), (pleg, pleg_in), (glub, glub_in), (subg, subg_in),
                     (flags, flags_in)):
        S.dma("sp", dst[:], src[:, :], [], [R_const], R_const)
    S.dma("sp", lamv[:], lamv_in[:, :, :], [], [R_const], R_const)
    S.op("dve", lambda e: e.tensor_copy(out=identb[:], in_=identf[:]), [R_const], [R_const])
    with ExitStack() as st:
        tmp = sb(st, "lamtmp", [128, 2, 128], F32)
        S.op("dve", lambda e: e.tensor_tensor(out=tmp[:, 0, :], in0=lamv[:, 0, :], in1=lamv[:, 1, :], op=ALU.mult),
             [R_const], [R_const])
        S.op("dve", lambda e: e.tensor_tensor(out=tmp[:, 1, :], in0=lamv[:, 2, :], in1=lamv[:, 3, :], op=ALU.mult),
             [R_const], [R_const])
        S.op("dve", lambda e: e.reduce_sum(out=lamt[:, 2:4], in_=tmp[:], axis=AX.X), [R_const], [R_const])
        S.op("act", lambda e: e.activation(out=lamt[:, 4:6], in_=lamt[:, 2:4], func=AF.Exp), [R_const], [R_const])
        S.op("dve", lambda e: e.tensor_tensor(out=lamt[:, 6:7], in0=lamt[:, 4:5], in1=lamt[:, 5:6], op=ALU.subtract),
             [R_const], [R_const])
        S.op("dve", lambda e: e.tensor_scalar(out=lamt[:, 0:1], in0=lamt[:, 6:7], scalar1=LAM_INIT, scalar2=None,
                                               op0=ALU.add), [R_const], [R_const])
        S.op("dve", lambda e: e.tensor_scalar(out=lamt[:, 1:2], in0=lamt[:, 0:1], scalar1=-1.0, scalar2=None,
                                               op0=ALU.mult), [R_const], [R_const])
        S.barrier()

    R_w = {}
    for name, src, dst, K in (("in", w_in_f, wb_in, D), ("glu", glu_w_f, wb_glu, 1024), ("s", wbs_f, wb_s, 1024),
                              ("a", wba_f, wb_a, 1024), ("o", wo_f, wb_o, D), ("pg", wpg_f, wb_pg, D),
                              ("pp", wpp_f, wb_pp, 256)):
        r = S.res("w" + name)
        R_w[name] = r
        if name == "in":
            R_wg = [S.res("win%d" % j) for j in range(5)]
            R_win = [R_wg[j // 4] for j in range(20)]
            continue
        for k0 in range(0, K, 128):
            S.dma("pool", dst[k0:k0 + 128, :], src[k0:k0 + 128, :], [], [r], r)

    if STOP_AFTER == 'W':
        S.barrier()
        es.close()
        return nc
    NSLAB = 3
    slabs = []
    R_slab = []
    slab_ctr = [0]

    def mk_slabs(stack):
        slabs[:] = [sb(stack, "slab%d" % i, [128, 16, 512], BF16) for i in range(NSLAB)]
        R_slab[:] = [S.res("slab%d" % i) for i in range(NSLAB)]

    def load_slab(wd, wres, KC, c0, ncols=512):
        i = slab_ctr[0] % NSLAB
        slab_ctr[0] += 1
        src = wd[:, c0:c0 + ncols].rearrange("(kc p) c -> p kc c", p=128)
        S.dma("sp", slabs[i][:, 0:KC, 0:ncols], src, [wres], [R_slab[i]], R_slab[i])
        return slabs[i], R_slab[i]

    with ExitStack() as st:
        mk_slabs(st)
        xin = sb(st, "xin", [128, 4, D], F32)
        R_xin = S.res("xin")
        xn = sb(st, "xn", [128, 4, D], BF16)
        R_xn = [S.res("xn") for _ in range(4)]
        junk = sb(st, "junk", [128, D], BF16)
        R_junk = S.res("junk")
        ss = sb(st, "ss", [128, 8], F32)
        R_ss = S.res("ss")
        hnT = [sb(st, "hnT%d" % i, [128, 16, TT], BF16) for i in range(2)]
        R_hnT = [[S.res("hnT") for _ in range(16)] for _ in range(2)]
        stg = [sb(st, "stg%d" % i, [128, 4, 512], BF16) for i in range(3)]
        R_stg = [S.res("stg%d" % i) for i in range(3)]
        psT2 = [ps(st, "psT0", [128, 1024], BF16), ps(st, "psT1", [128, 1024], BF16)]
        R_psT = [S.res("psT0"), S.res("psT1")]
        psA = [ps(st, "psA%d" % i, [128, 512], F32) for i in range(4)]
        R_psA = [S.res("psA%d" % i) for i in range(4)]
        pa_ctr = 0
        stg_ctr = 0
        plan = []
        for j in range(20):
            c0 = j * 512
            if j < 2:
                plan.append((c0, "tm", u_d, c0, "copy", "u"))
            elif j < 4:
                plan.append((c0, "fm", szT_d, (j - 2) * 4, "silu", "szT"))
            elif j < 6:
                plan.append((c0, "fm", qT_d, (j - 4) * 4, "qscale", "qT"))
            elif j < 8:
                plan.append((c0, "fm", kT_d, (j - 6) * 4, "copy", "kT"))
            elif j < 10:
                plan.append((c0, "tm", v_d, (j - 8) * 512, "copy", "v"))
            elif j < 12:
                plan.append((c0, "fm", azT_d, (j - 10) * 4, "silu", "azT"))
            elif j < 16:
                plan.append((c0, "fm", gsT_d, (j - 12) * 4, "sigmoid", "gsT"))
            else:
                plan.append((c0, "fm", gaT_d, (j - 16) * 4, "sigmoid", "gaT"))

        stage32 = [sb(st, "stage32_%d" % i, [128, 8, 512], F32) for i in range(2)]
        R_stage32 = [S.res("stage32") for _ in range(2)]
        st32_ctr = [0]

        def prep(idx, s, t):
            tok0 = t * TT
            hb = idx % 2
            S.dma("sp", xin[:], x_in[s, tok0:tok0 + TT, :].rearrange("(a p) d -> p a d", p=128), [], [R_xin], R_xin)
            for a in range(4):
                S.op("act", lambda e, a=a: e.activation(out=junk[:], in_=xin[:, a, :], func=AF.Square,
                                                        accum_out=ss[:, a:a + 1]), [R_xin], [R_junk, R_ss])
            S.op("dve", lambda e: e.tensor_scalar(out=ss[:, 4:8], in0=ss[:, 0:4], scalar1=1.0 / D, scalar2=EPS,
                                                  op0=ALU.mult, op1=ALU.add), [R_ss], [R_ss])
            S.op("act", lambda e: e.activation(out=ss[:, 4:8], in_=ss[:, 4:8], func=AF.Sqrt), [R_ss], [R_ss])
            S.op("dve", lambda e: e.reciprocal(out=ss[:, 4:8], in_=ss[:, 4:8]), [R_ss], [R_ss])
            for a in range(4):
                S.op("dve", lambda e, a=a: e.tensor_scalar(out=xn[:, a, :], in0=xin[:, a, :], scalar1=ss[:, 4 + a:5 + a],
                                                           scalar2=None, op0=ALU.mult), [R_xin, R_ss], [R_xn[a]])
            for kc in range(16):
                h = kc % 2

                def tr(e, kc=kc, h=h):
                    ins = None
                    for a in range(4):
                        ins = e.transpose(out=psT2[h][:, a * 128:(a + 1) * 128],
                                          in_=xn[:, a, kc * 128:(kc + 1) * 128], identity=identb[:])
                    return ins
                S.op("pe", tr, R_xn, [R_psT[h]])
                S.op("dve", lambda e, kc=kc, h=h, hb=hb: e.tensor_scalar(out=hnT[hb][:, kc, :], in0=psT2[h][:, 0:512],
                                                                          scalar1=normg[:, kc:kc + 1], scalar2=None,
                                                                          op0=ALU.mult), [R_psT[h]], [R_hnT[hb][kc]])

        def slab_iter(idx, s, t, entry):
            nonlocal pa_ctr, stg_ctr
            (c0, kind, dst, dbase, func, rname) = entry
            tok0 = t * TT
            hb = idx % 2
            hT = hnT[hb]
            if idx == 0:
                i = slab_ctr[0] % NSLAB
                slab_ctr[0] += 1
                slab, rslab = slabs[i], R_slab[i]
                for hf in range(2):
                    bi = st32_ctr[0] % 2
                    st32_ctr[0] += 1
                    S.dma("sp", stage32[bi][:],
                          w_in_f[hf * 1024:(hf + 1) * 1024, c0:c0 + 512].rearrange("(kc p) c -> p kc c", p=128),
                          [], [R_stage32[bi]], R_stage32[bi])
                    if hf == 0:
                        S.op("dve", lambda e, bi=bi, slab=slab: e.tensor_copy(out=slab[:, 0:8, :], in_=stage32[bi][:]),
                             [R_stage32[bi]], [rslab])
                    else:
                        S.op("act", lambda e, bi=bi, slab=slab: e.activation(out=slab[:, 8:16, :], in_=stage32[bi][:],
                                                                              func=AF.Copy), [R_stage32[bi]], [rslab])
                S.dma("pool", wb_in[:, c0:c0 + 512].rearrange("(kc p) c -> p kc c", p=128), slab[:], [rslab],
                      [R_win[c0 // 512]], R_win[c0 // 512])
            else:
                slab, rslab = load_slab(wb_in, R_win[c0 // 512], 16, c0)
            si = stg_ctr % 3
            stg_ctr += 1
            for m in range(4):
                pi = pa_ctr % 4
                pa_ctr += 1

                def mm(e, m=m, pi=pi, slab=slab, kind=kind):
                    ins = None
                    for kc in range(16):
                        if kind == "fm":
                            ins = e.matmul(psA[pi][:], lhsT=slab[:, kc, m * 128:(m + 1) * 128], rhs=hT[:, kc, :],
                                           start=(kc == 0), stop=(kc == 15))
                        else:
                            ins = e.matmul(psA[pi][:], lhsT=hT[:, kc, m * 128:(m + 1) * 128], rhs=slab[:, kc, :],
                                           start=(kc == 0), stop=(kc == 15))
                    return ins
                S.op("pe", mm, [rslab] + R_hnT[hb], [R_psA[pi]])
                if func == "copy":
                    S.op("dve", lambda e, m=m, pi=pi, si=si: e.tensor_copy(out=stg[si][:, m, :], in_=psA[pi][:]),
                         [R_psA[pi]], [R_stg[si]])
                elif func == "qscale":
                    S.op("dve", lambda e, m=m, pi=pi, si=si: e.tensor_scalar(
                        out=stg[si][:, m, :], in0=psA[pi][:], scalar1=128.0 ** -0.5, scalar2=None, op0=ALU.mult),
                        [R_psA[pi]], [R_stg[si]])
                else:
                    f = AF.Silu if func == "silu" else AF.Sigmoid
                    S.op("act", lambda e, m=m, pi=pi, si=si, f=f: e.activation(out=stg[si][:, m, :], in_=psA[pi][:],
                                                                               func=f), [R_psA[pi]], [R_stg[si]])
            if kind == "fm":
                dap = dst[s, dbase:dbase + 4, :, tok0:tok0 + TT].rearrange("c p t -> p c t")
            else:
                dap = dst[s, tok0:tok0 + TT, dbase:dbase + 512].rearrange("(a p) c -> p a c", p=128)
            S.dma("pool", dap, stg[si][:], [R_stg[si]], [R_scr[rname]], R_stg[si])

        tiles1 = [(s_, t_) for s_ in range(NSEG) for t_ in range(SEG // TT)]
        prep(0, *tiles1[0])
        for idx, (s_, t_) in enumerate(tiles1):
            for k, entry in enumerate(plan):
                if k == 10 and idx + 1 < len(tiles1):
                    prep(idx + 1, *tiles1[idx + 1])
                slab_iter(idx, s_, t_, entry)
        S.barrier()

    if STOP_AFTER == '1':
        es.close()
        return nc
    def att_phase(st):
        NB = 1536
        Bd = dscr("biasF", [4, 128, NB], F32)
        R_Bd = S.res("Bd")
        strips_hi = sb(st, "strips_hi", [128, 4, 1280], BF16)
        strips_lo = sb(st, "strips_lo", [128, 4, 1280], BF16)
        biasc = sb(st, "biasc", [128, 4, 4], F32)
        st0 = ExitStack()
        strips = sb(st0, "strips", [128, 4, 1280], F32)
        strips_d = sb(st0, "strips_d", [128, 4, 1280], F32)
        relb = sb(st0, "relb", [32, 4], F32)
        ohs = sb(st0, "ohs", [32, NB], F32)
        ones32 = sb(st0, "ones32", [32, 128], F32)
        rep = sb(st0, "rep", [32, 128], F32)
        Frep = sb(st0, "Frep", [128, NB], F32)
        R_a0 = S.res("a0")
        R_rep = S.res("rep")
        R_Frep = S.res("Frep")
        R_strips = S.res("strips")
        R_biasc = S.res("biasc")
        S.dma("sp", relb[:], relb_in[:, :], [], [R_a0], R_a0)
        S.dma("sp", ohs[:], onehot_in[:, :], [], [R_a0], R_a0)
        S.op("dve", lambda e: e.memset(ones32[:], 1.0), [], [R_a0])
        psS = [ps(st, "psS%d" % i, [128, 512], F32) for i in range(3)]
        R_psS = [S.res("psS%d" % i) for i in range(3)]
        for h in range(4):
            S.op("dve", lambda e, h=h: e.tensor_scalar(out=rep[:], in0=ones32[:], scalar1=relb[:, h:h + 1], scalar2=None,
                                                       op0=ALU.mult), [R_a0], [R_rep])
            for j in range(3):
                S.op("pe", lambda e, j=j: e.matmul(psS[j % 2][:], lhsT=rep[:], rhs=ohs[:, j * 512:(j + 1) * 512],
                                                   start=True, stop=True), [R_rep, R_a0], [R_psS[j % 2]])
                S.op("dve", lambda e, j=j: e.tensor_copy(out=Frep[:, j * 512:(j + 1) * 512], in_=psS[j % 2][:]),
                     [R_psS[j % 2]], [R_Frep])
            S.op("dve", lambda e, h=h: e.tensor_copy(out=biasc[:, h, 0:1], in_=Frep[:, 967:968]), [R_Frep], [R_biasc])
            S.op("dve", lambda e, h=h: e.tensor_copy(out=biasc[:, h, 1:2], in_=Frep[:, 567:568]), [R_Frep], [R_biasc])
            S.op("dve", lambda e, h=h: e.tensor_tensor(out=biasc[:, h, 2:3], in0=Frep[:, 967:968], in1=flags[:, 1:2],
                                                       op=ALU.add), [R_Frep], [R_biasc])
            S.op("dve", lambda e, h=h: e.tensor_tensor(out=biasc[:, h, 3:4], in0=Frep[:, 567:568], in1=flags[:, 1:2],
                                                       op=ALU.add), [R_Frep], [R_biasc])
            S.dma("sp", Bd[h, :, :], Frep[:], [R_Frep], [R_Bd], R_Frep)
            src = bass.AP(Bd.tensor, h * 128 * NB + 127, [[NB - 1, 128], [1, 1280]])
            S.dma("sp", strips[:, h, :], src, [R_Bd], [R_strips], R_strips)

        S.op("dve", lambda e: e.tensor_copy(out=strips_hi[:], in_=strips[:]), [R_strips], [R_strips])
        S.op("dve", lambda e: e.tensor_tensor(out=strips_d[:], in0=strips[:], in1=strips_hi[:], op=ALU.subtract),
             [R_strips], [R_strips])
        S.op("dve", lambda e: e.tensor_copy(out=strips_lo[:], in_=strips_d[:]), [R_strips], [R_strips])
        S.barrier()
        st0.close()
        KT = [sb(st, "KT%d" % i, [128, 2, 4096], BF16) for i in range(1)] * 2
        QT = [sb(st, "QT%d" % i, [128, 2, 4096], BF16) for i in range(1)] * 2
        VX = [sb(st, "VX%d" % i, [128, 32, 257], BF16) for i in range(1)] * 2
        R_KT = [S.res("KT")] * 2
        R_QT = [S.res("QT")] * 2
        R_VX = [S.res("VX")] * 2
        S.op("pool", lambda e: e.memset(VX[0][:, :, 256:257], 1.0), [], [R_VX[0]])
        PT = [sb(st, "PT%d" % i, [128, 512], BF16) for i in range(3)]
        R_PT = [S.res("PT") for _ in range(3)]
        tmpb = [sb(st, "tmpb%d" % i, [128, 512], F32) for i in range(2)]
        R_tmpb = [S.res("tmpb") for _ in range(2)]
        o0 = sb(st, "o0", [128, 4, 256], F32)
        oo = sb(st, "oo", [128, 4, 256], F32)
        R_o0 = [S.res("o0") for _ in range(4)]
        R_oo = [S.res("oo") for _ in range(4)]
        onb = sb(st, "onb", [128, 4, 256], BF16)
        R_onb = [S.res("onb") for _ in range(4)]
        rr = sb(st, "rr", [128, 4, 8], F32)
        R_rr = [S.res("rr") for _ in range(4)]
        junk3 = sb(st, "junk3", [128, 256], BF16)
        R_junk3 = S.res("junk3")
        azt = [sb(st, "azt%d" % i, [128, 2, 512], BF16) for i in range(2)]
        R_azt = [S.res("azt") for _ in range(2)]
        ozs = [sb(st, "ozs%d" % i, [128, 2, 512], BF16) for i in range(2)]
        R_ozs = [S.res("ozs") for _ in range(2)]
        acc = [ps(st, "acc%d" % i, [128, 512], F32) for i in range(4)]
        R_acc = [S.res("acc%d" % i) for i in range(4)]
        psTa = ps(st, "psTa", [128, 1024], BF16)
        R_psTa = S.res("psTa")
        sctr = 0
        ptctr = 0
        tbctr = 0
        qtctr = 0
        hctr = 0
        groups = [([0, 1], 4096), ([2], 2048)]
        if ATT_LIM is not None:
            groups = groups[:ATT_LIM[0]]
        for (gsegs, L) in groups:
            nkb = L // 128
            for h in range(4 if ATT_LIM is None else ATT_LIM[1]):
                bi = hctr % 2
                hctr += 1
                for si, sg in enumerate(gsegs):
                    S.dma("sp", KT[bi][:, :, si * SEG:(si + 1) * SEG],
                          kT_d[sg, 2 * h:2 * h + 2, :, :].rearrange("c p t -> p c t"), [R_scr["kT"]], [R_KT[bi]], R_KT[bi])
                    S.dma("sp", QT[bi][:, :, si * SEG:(si + 1) * SEG],
                          qT_d[sg, 2 * h:2 * h + 2, :, :].rearrange("c p t -> p c t"), [R_scr["qT"]], [R_QT[bi]], R_QT[bi])
                    S.dma("sp", VX[bi][:, si * 16:(si + 1) * 16, 0:256],
                          v_d[sg, :, h * 256:(h + 1) * 256].rearrange("(kb p) e -> p kb e", p=128),
                          [R_scr["v"]], [R_VX[bi]], R_VX[bi])
                nqt = L // 512 if ATT_LIM is None else min(L // 512, ATT_LIM[2])
                iters = [(qt, c, kb) for qt in range(nqt) for c in range(2) for kb in range(nkb)]
                slots = {}
                qinfo = {}

                def start_qt(qt, gsegs=gsegs, h=h):
                    nonlocal qtctr
                    q0 = qt * 512
                    qseg = gsegs[q0 // SEG]
                    qtok = q0 % SEG
                    ai = qtctr % 2
                    qtctr += 1
                    S.dma("sp", azt[ai][:], azT_d[qseg, 2 * h:2 * h + 2, :, qtok:qtok + 512].rearrange("c p t -> p c t"),
                          [R_scr["azT"]], [R_azt[ai]], R_azt[ai])
                    qinfo[qt] = (q0, qseg, qtok, ai)

                def emit_qk(i, bi=bi, h=h):
                    nonlocal sctr
                    qt, c, kb = iters[i]
                    if c == 0 and kb == 0:
                        start_qt(qt)
                    q0 = qinfo[qt][0]
                    k0 = kb * 128
                    pi = sctr % 3
                    sctr += 1
                    slots[i] = pi
                    dk = k0 - q0
                    if -128 <= dk <= 512:
                        j0 = 640 - dk

                        def qkb(e):
                            e.matmul(psS[pi][:], lhsT=KT[bi][:, c, k0:k0 + 128], rhs=QT[bi][:, c, q0:q0 + 512],
                                     start=True, stop=False)
                            e.matmul(psS[pi][:], lhsT=identb[:], rhs=strips_hi[:, h, j0:j0 + 512], start=False, stop=False)
                            return e.matmul(psS[pi][:], lhsT=identb[:], rhs=strips_lo[:, h, j0:j0 + 512],
                                            start=False, stop=True)
                        S.op("pe", qkb, [R_KT[bi], R_QT[bi], R_strips], [R_psS[pi]])
                    else:
                        S.op("pe", lambda e: e.matmul(psS[pi][:], lhsT=KT[bi][:, c, k0:k0 + 128],
                                                      rhs=QT[bi][:, c, q0:q0 + 512], start=True, stop=True),
                             [R_KT[bi], R_QT[bi]], [R_psS[pi]])

                def finish_qt(qt, h=h):
                    (q0, qseg, qtok, ai) = qinfo.pop(qt)
                    for qs in range(4):
                        S.op("act", lambda e, qs=qs: e.activation(out=junk3[:], in_=oo[:, qs, :], func=AF.Square,
                                                                  accum_out=rr[:, qs, 3:4]), [R_oo[qs]], [R_junk3, R_rr[qs]])
                        S.op("dve", lambda e, qs=qs: e.tensor_scalar(out=rr[:, qs, 4:5], in0=rr[:, qs, 3:4],
                                                                     scalar1=1.0 / 256, scalar2=EPS, op0=ALU.mult,
                                                                     op1=ALU.add), [R_rr[qs]], [R_rr[qs]])
                        S.op("act", lambda e, qs=qs: e.activation(out=rr[:, qs, 4:5], in_=rr[:, qs, 4:5], func=AF.Sqrt),
                             [R_rr[qs]], [R_rr[qs]])
                        S.op("dve", lambda e, qs=qs: e.reciprocal(out=rr[:, qs, 5:6], in_=rr[:, qs, 4:5]),
                             [R_rr[qs]], [R_rr[qs]])
                        S.op("dve", lambda e, qs=qs: e.tensor_scalar(out=onb[:, qs, :], in0=oo[:, qs, :],
                                                                     scalar1=rr[:, qs, 5:6], scalar2=1.0 - LAM_INIT,
                                                                     op0=ALU.mult, op1=ALU.mult),
                             [R_oo[qs], R_rr[qs]], [R_onb[qs]])
                    oi = ai
                    for ec in range(2):
                        def tr(e, ec=ec):
                            ins = None
                            for qs in range(4):
                                ins = e.transpose(out=psTa[:, qs * 128:(qs + 1) * 128],
                                                  in_=onb[:, qs, ec * 128:(ec + 1) * 128], identity=identb[:])
                            return ins
                        S.op("pe", tr, R_onb, [R_psTa])
                        S.op("dve", lambda e, ec=ec: e.scalar_tensor_tensor(
                            out=ozs[oi][:, ec, :], in0=psTa[:, 0:512], scalar=subg[:, ec:ec + 1], in1=azt[ai][:, ec, :],
                            op0=ALU.mult, op1=ALU.mult), [R_psTa, R_azt[ai]], [R_ozs[oi]])
                    S.dma("pool", ozT_d[qseg, 2 * h:2 * h + 2, :, qtok:qtok + 512].rearrange("c p t -> p c t"),
                          ozs[oi][:], [R_ozs[oi]], [R_scr["ozT"]], R_ozs[oi])

                def emit_rest(i, bi=bi, h=h, nkb=nkb, gsegs=gsegs):
                    nonlocal ptctr, tbctr
                    qt, c, kb = iters[i]
                    q0 = qinfo[qt][0]
                    k0 = kb * 128
                    dk = k0 - q0
                    cross = (len(gsegs) == 2) and ((k0 // SEG) != (q0 // SEG))
                    near = (-128 <= dk <= 512)
                    pi = slots.pop(i)
                    pti = ptctr % 3
                    ptctr += 1
                    if near:
                        if cross:
                            S.op("act", lambda e: e.activation(out=PT[pti][:], in_=psS[pi][:], func=AF.Exp,
                                                               bias=flags[:, 1:2]), [R_psS[pi]], [R_PT[pti]])
                        else:
                            S.op("act", lambda e: e.activation(out=PT[pti][:], in_=psS[pi][:], func=AF.Exp),
                                 [R_psS[pi]], [R_PT[pti]])
                    else:
                        col = (1 if dk > 0 else 0) + (2 if cross else 0)
                        S.op("act", lambda e: e.activation(out=PT[pti][:], in_=psS[pi][:], func=AF.Exp,
                                                           bias=biasc[:, h, col:col + 1]),
                             [R_psS[pi], R_biasc], [R_PT[pti]])
                    for qs in range(4):
                        S.op("pe", lambda e, qs=qs: e.matmul(
                            acc[qs][:, 0:257], lhsT=PT[pti][:, qs * 128:(qs + 1) * 128], rhs=VX[bi][:, kb, :],
                            start=(kb == 0), stop=(kb == nkb - 1)), [R_PT[pti], R_VX[bi]], [R_acc[qs]])
                    if kb == nkb - 1:
                        for qs in range(4):
                            S.op("dve", lambda e, qs=qs: e.reciprocal(out=rr[:, qs, c:c + 1], in_=acc[qs][:, 256:257]),
                                 [R_acc[qs]], [R_rr[qs]])
                            if c == 0:
                                S.op("dve", lambda e, qs=qs: e.tensor_scalar(
                                    out=o0[:, qs, :], in0=acc[qs][:, 0:256], scalar1=rr[:, qs, 0:1], scalar2=None,
                                    op0=ALU.mult), [R_acc[qs], R_rr[qs]], [R_o0[qs]])
                            else:
                                S.op("dve", lambda e, qs=qs: e.tensor_tensor(
                                    out=rr[:, qs, 2:3], in0=rr[:, qs, 1:2], in1=lamt[:, 1:2], op=ALU.mult),
                                    [R_rr[qs]], [R_rr[qs]])
                                S.op("dve", lambda e, qs=qs: e.scalar_tensor_tensor(
                                    out=oo[:, qs, :], in0=acc[qs][:, 0:256], scalar=rr[:, qs, 2:3], in1=o0[:, qs, :],
                                    op0=ALU.mult, op1=ALU.add), [R_acc[qs], R_rr[qs], R_o0[qs]], [R_oo[qs]])
                        if c == 1:
                            finish_qt(qt)
                LA = 2
                for i in range(len(iters) + LA):
                    if i < len(iters):
                        emit_qk(i)
                    if i >= LA:
                        emit_rest(i - LA)
                    yield


    if not DBG_S5_IDENT:
        TWO_PI = 2.0 * math.pi
        MAGIC = 12582912.0
        NCB = 32
        Ud = dscr("Ud", [NSEG, NCB, 128, 64 * 8])
        Zd = dscr("Zd", [NSEG, 2, NCB, 128, 64 * 8])
        Pd = dscr("Pd", [NSEG, 2, NCB, 128, 64 * 8], F32)
        R_Pd = [[S.res("Pd") for _ in range(2)] for _ in range(NSEG)]
        R_Ud = S.res("Ud")
        R_Zd = [[S.res("Zd") for _ in range(2)] for _ in range(NSEG)]
        with ExitStack() as st5:
            Ym = sb(st5, "Ym", [128, 128, 128], BF16)
            Mall = sb(st5, "Mall", [128, 64, 128], BF16)
            Ar = sb(st5, "Ar", [128, 128], F32)
            Aisw = sb(st5, "Aisw", [128, 128], F32)
            stX = ExitStack()
            Xm = sb(stX, "Xm", [128, 64, 2, 128], BF16)
            R_Xm, R_Ym, R_M, R_A = S.res("Xm"), S.res("Ym"), S.res("Mall"), S.res("A")
            with ExitStack() as st:
                def ld(name, src_ap, shape, dt=F32):
                    t = sb(st, name, shape, dt)
                    r = S.res(name)
                    S.dma("sp", t[:], src_ap, [], [r], r)
                    return t, r
                lr2, R_lr = ld("lr2", s5_lr_in[:, :], [128, 128])
                li2, R_li = ld("li2", s5_li_in[:, :], [128, 128])
                ldt2, R_ldt = ld("ldt2", s5_ldt_in[:, :], [128, 128])
                dcol, R_dcol = ld("dcol", s5_dcol_in[:, :], [128, 64])
                mkL, R_mkL = ld("mkL", s5_maskl_in[:, :], [128, 128])
                mkU, R_mkU = ld("mkU", s5_masku_in[:, :], [128, 128])
                xr = sb(st, "xr", [128, 128], F32)
                th = sb(st, "th", [128, 128], F32)
                R_xr = S.res("xr")
                S.op("act", lambda e: e.activation(out=xr[:], in_=ldt2[:], func=AF.Exp), [R_ldt], [R_xr])
                S.op("dve", lambda e: e.tensor_tensor(out=th[:], in0=li2[:], in1=xr[:], op=ALU.mult), [R_li, R_xr], [R_xr])
                S.op("dve", lambda e: e.tensor_tensor(out=xr[:], in0=lr2[:], in1=xr[:], op=ALU.mult), [R_lr, R_xr], [R_xr])

                pwtmp = {}

                def pw(stk, tag, F, kt, thb, xrb, R_k):
                    shp = [128] + F
                    if len(F) not in pwtmp:
                        pwtmp[len(F)] = (sb(st, "pwang%d" % len(F), shp, F32), sb(st, "pwt%d" % len(F), shp, F32),
                                         S.res("pwtmp"))
                    ang, t, Rt = pwtmp[len(F)]
                    pr = sb(stk, tag + "pr", shp, F32)
                    pi = sb(stk, tag + "pi", shp, F32)
                    R = S.res(tag)
                    S.op("dve", lambda e: e.tensor_tensor(out=ang[:], in0=kt, in1=thb, op=ALU.mult), [R_k, R_xr], [Rt])
                    S.op("dve", lambda e: e.tensor_scalar(out=t[:], in0=ang[:], scalar1=1.0 / TWO_PI, scalar2=MAGIC,
                                                          op0=ALU.mult, op1=ALU.add), [Rt], [Rt])
                    S.op("dve", lambda e: e.tensor_scalar(out=t[:], in0=t[:], scalar1=-MAGIC, scalar2=-TWO_PI,
                                                          op0=ALU.add, op1=ALU.mult), [Rt], [Rt])
                    S.op("dve", lambda e: e.tensor_tensor(out=ang[:], in0=ang[:], in1=t[:], op=ALU.add), [Rt], [Rt])
                    S.op("act", lambda e: e.activation(out=pi[:], in_=ang[:], func=AF.Sin), [Rt], [R])
                    S.op("dve", lambda e: e.tensor_scalar(out=t[:], in0=ang[:], scalar1=-1.0, scalar2=None, op0=ALU.mult),
                         [Rt], [Rt])
                    S.op("dve", lambda e: e.tensor_tensor(out=t[:], in0=t[:], in1=ang[:], op=ALU.max), [Rt], [Rt])
                    S.op("dve", lambda e: e.tensor_scalar(out=t[:], in0=t[:], scalar1=-1.0, scalar2=math.pi / 2,
                                                          op0=ALU.mult, op1=ALU.add), [Rt], [Rt])
                    S.op("act", lambda e: e.activation(out=pr[:], in_=t[:], func=AF.Sin), [Rt], [R])
                    S.op("dve", lambda e: e.tensor_tensor(out=t[:], in0=kt, in1=xrb, op=ALU.mult), [R_k, R_xr], [Rt])
                    S.op("act", lambda e: e.activation(out=t[:], in_=t[:], func=AF.Exp), [Rt], [Rt])
                    S.op("dve", lambda e: e.tensor_tensor(out=pr[:], in0=pr[:], in1=t[:], op=ALU.mult), [R, Rt], [R])
                    S.op("dve", lambda e: e.tensor_tensor(out=pi[:], in0=pi[:], in1=t[:], op=ALU.mult), [R, Rt], [R])
                    return pr, pi, R

                k1 = sb(st, "k1", [128, 128], F32)
                k8 = sb(st, "k8", [128, 128], F32)
                R_k18 = S.res("k18")
                S.op("dve", lambda e: e.memset(k1[:], 1.0), [], [R_k18])
                S.op("dve", lambda e: e.memset(k8[:], 8.0), [], [R_k18])
                p1r, p1i, R_p1 = pw(st, "p1", [128], k1[:], th[:], xr[:], R_k18)
                p8r, p8i, R_p8 = pw(st, "p8", [128], k8[:], th[:], xr[:], R_k18)
                S.op("dve", lambda e: e.tensor_copy(out=Ar[:], in_=p8r[:]), [R_p8], [R_A])
                S.op("dve", lambda e: e.tensor_copy(out=Aisw[0:64, :], in_=p8i[0:64, :]), [R_p8], [R_A])
                S.op("dve", lambda e: e.tensor_scalar(out=Aisw[64:128, :], in0=p8i[64:128, :], scalar1=-1.0, scalar2=None,
                                                      op0=ALU.mult), [R_p8], [R_A])
                kr = sb(st, "kr", [128, 128], F32)
                ki = sb(st, "ki", [128, 128], F32)
                den = sb(st, "den", [128, 128], F32)
                tq = sb(st, "tq", [128, 128], F32)
                R_kap = S.res("kap")
                S.op("dve", lambda e: e.tensor_scalar(out=p1r[:], in0=p1r[:], scalar1=-1.0, scalar2=None, op0=ALU.add),
                     [R_p1], [R_p1])
                S.op("dve", lambda e: e.tensor_tensor(out=den[:], in0=lr2[:], in1=lr2[:], op=ALU.mult), [R_lr], [R_kap])
                S.op("dve", lambda e: e.tensor_tensor(out=tq[:], in0=li2[:], in1=li2[:], op=ALU.mult), [R_li], [R_kap])
                S.op("dve", lambda e: e.tensor_tensor(out=den[:], in0=den[:], in1=tq[:], op=ALU.add), [R_kap], [R_kap])
                S.op("dve", lambda e: e.reciprocal(out=den[:], in_=den[:]), [R_kap], [R_kap])
                S.op("dve", lambda e: e.tensor_tensor(out=kr[:], in0=p1r[:], in1=lr2[:], op=ALU.mult), [R_p1, R_lr], [R_kap])
                S.op("dve", lambda e: e.tensor_tensor(out=tq[:], in0=p1i[:], in1=li2[:], op=ALU.mult), [R_p1, R_li], [R_kap])
                S.op("dve", lambda e: e.tensor_tensor(out=kr[:], in0=kr[:], in1=tq[:], op=ALU.add), [R_kap], [R_kap])
                S.op("dve", lambda e: e.tensor_tensor(out=kr[:], in0=kr[:], in1=den[:], op=ALU.mult), [R_kap], [R_kap])
                S.op("dve", lambda e: e.tensor_tensor(out=ki[:], in0=p1i[:], in1=lr2[:], op=ALU.mult), [R_p1, R_lr], [R_kap])
                S.op("dve", lambda e: e.tensor_tensor(out=tq[:], in0=p1r[:], in1=li2[:], op=ALU.mult), [R_p1, R_li], [R_kap])
                S.op("dve", lambda e: e.tensor_tensor(out=ki[:], in0=ki[:], in1=tq[:], op=ALU.subtract), [R_kap], [R_kap])
                S.op("dve", lambda e: e.tensor_tensor(out=ki[:], in0=ki[:], in1=den[:], op=ALU.mult), [R_kap], [R_kap])
                B8 = [128, 128, 8]
                thb = th[:].unsqueeze(2).to_broadcast(B8)
                xrb = xr[:].unsqueeze(2).to_broadcast(B8)
                krb = kr[:].unsqueeze(2).to_broadcast(B8)
                kib = ki[:].unsqueeze(2).to_broadcast(B8)
                XT = sb(st, "XT", [128, 128, 128], BF16)
                R_XT = S.res("XT")
                t1 = sb(st, "g_t1", [128, 16, 8, 16], F32)
                t2 = sb(st, "g_t2", [128, 16, 8, 16], F32)
                R_t12 = S.res("t12")
                SH = [128, 16, 8, 16]

                def ld2(stk, name, src_ap, shape):
                    t = sb(stk, name, shape, F32)
                    r = S.res(name)
                    S.dma("sp", t[:], src_ap, [], [r], r)
                    return t, r
                pwtmp[2] = (sb(st, "pwang2", B8, F32), sb(st, "pwt2", B8, F32), S.res("pwtmp"))
                for part in ("X", "Y"):
                    with ExitStack() as sx:
                        if part == "X":
                            TA, R_TA = ld2(sx, "BA", s5_ba_in[:, :, :], [128, 128, 16])
                            TB, R_TB = ld2(sx, "BB", s5_bb_in[:, :, :], [128, 128, 16])
                            kT, R_kT = ld2(sx, "kX", s5_kx_in[:, :, :], [128, 128, 8])
                            pr_, pi_, R_p = pw(sx, "px", [128, 8], kT[:], thb, xrb, R_kT)
                            ar_ = sb(sx, "xir", B8, F32)
                            ai_ = sb(sx, "xii", B8, F32)
                            tx = sb(sx, "tx", B8, F32)
                            R_ar = S.res("xi")
                            S.op("dve", lambda e: e.tensor_tensor(out=ar_[:], in0=pr_[:], in1=krb, op=ALU.mult), [R_p, R_kap], [R_ar])
                            S.op("dve", lambda e: e.tensor_tensor(out=tx[:], in0=pi_[:], in1=kib, op=ALU.mult), [R_p, R_kap], [R_ar])
                            S.op("dve", lambda e: e.tensor_tensor(out=ar_[:], in0=ar_[:], in1=tx[:], op=ALU.subtract), [R_ar], [R_ar])
                            S.op("dve", lambda e: e.tensor_tensor(out=ai_[:], in0=pr_[:], in1=kib, op=ALU.mult), [R_p, R_kap], [R_ar])
                            S.op("dve", lambda e: e.tensor_tensor(out=tx[:], in0=pi_[:], in1=krb, op=ALU.mult), [R_p, R_kap], [R_ar])
                            S.op("dve", lambda e: e.tensor_tensor(out=ai_[:], in0=ai_[:], in1=tx[:], op=ALU.add), [R_ar], [R_ar])
                            outT, R_o, upper_neg = XT, R_XT, False
                        else:
                            TA, R_TA = ld2(sx, "CA", s5_ca_in[:, :, :], [128, 128, 16])
                            TB, R_TB = ld2(sx, "CB", s5_cb_in[:, :, :], [128, 128, 16])
                            kT, R_kT = ld2(sx, "kY", s5_ky_in[:, :, :], [128, 128, 8])
                            ar_, ai_, R_ar = pw(sx, "py", [128, 8], kT[:], thb, xrb, R_kT)
                            outT, R_o, upper_neg = Ym, R_Ym, True
                        for cch in range(8):
                            sl = slice(cch * 16, (cch + 1) * 16)
                            a_r = ar_[:, sl, :].unsqueeze(3).to_broadcast(SH)
                            a_i = ai_[:, sl, :].unsqueeze(3).to_broadcast(SH)
                            b_a = TA[:, sl, :].unsqueeze(2).to_broadcast(SH)
                            b_b = TB[:, sl, :].unsqueeze(2).to_broadcast(SH)
                            S.op("dve", lambda e, a_r=a_r, b_a=b_a: e.tensor_tensor(out=t1[:], in0=a_r, in1=b_a, op=ALU.mult),
                                 [R_ar, R_TA], [R_t12])
                            S.op("dve", lambda e, a_i=a_i, b_b=b_b: e.tensor_tensor(out=t2[:], in0=a_i, in1=b_b, op=ALU.mult),
                                 [R_ar, R_TB], [R_t12])
                            ov = outT[:, sl, :].rearrange("p a (j c) -> p a j c", j=8)
                            S.op("dve", lambda e, ov=ov: e.tensor_tensor(out=ov[0:64], in0=t1[0:64], in1=t2[0:64],
                                                                         op=ALU.subtract), [R_t12], [R_o])
                            if upper_neg:
                                S.op("dve", lambda e, sl=sl: e.scalar_tensor_tensor(
                                    out=outT[64:128, sl, :], in0=t1[64:128].rearrange("p a j c -> p a (j c)"), scalar=-1.0,
                                    in1=t2[64:128].rearrange("p a j c -> p a (j c)"), op0=ALU.mult, op1=ALU.subtract),
                                    [R_t12], [R_o])
                            else:
                                S.op("dve", lambda e, ov=ov: e.tensor_tensor(out=ov[64:128], in0=t1[64:128], in1=t2[64:128],
                                                                             op=ALU.add), [R_t12], [R_o])
                        S.barrier()
                psx = [ps(st, "psx%d" % i, [128, 1024], BF16) for i in range(2)]
                R_psx = [S.res("psx") for _ in range(2)]
                psm = [ps(st, "psm%d" % i, [128, 512], F32) for i in range(4)]
                R_psm = [S.res("psm") for _ in range(4)]
                mt = [sb(st, "mt%d" % i, [128, 128], F32) for i in range(2)]
                R_mt = [S.res("mt") for _ in range(2)]
                for g4 in range(16):
                    bi = g4 % 2

                    def trx(e, g4=g4, bi=bi):
                        ins = None
                        for gg in range(4):
                            for d in range(2):
                                ins = e.transpose(out=psx[bi][:, (gg * 2 + d) * 128:(gg * 2 + d + 1) * 128],
                                                  in_=XT[:, d * 64 + g4 * 4 + gg, :], identity=identb[:])
                        return ins
                    S.op("pe", trx, [R_XT], [R_psx[bi]])
                    S.op("act", lambda e, g4=g4, bi=bi: e.activation(
                        out=Xm[:, g4 * 4:(g4 + 1) * 4, :, :].rearrange("p g d m -> p (g d m)"), in_=psx[bi][:], func=AF.Copy),
                        [R_psx[bi]], [R_Xm])
                for g in range(64):
                    pf = (2 * g) % 4
                    pb = (2 * g + 1) % 4
                    S.op("pe", lambda e, g=g, pf=pf: e.matmul(psm[pf][:, 0:128], lhsT=XT[:, g, :], rhs=Ym[:, g, :],
                                                              start=True, stop=True), [R_XT, R_Ym], [R_psm[pf]])
                    S.op("pe", lambda e, g=g, pb=pb: e.matmul(psm[pb][:, 0:128], lhsT=XT[:, 64 + g, :], rhs=Ym[:, 64 + g, :],
                                                              start=True, stop=True), [R_XT, R_Ym], [R_psm[pb]])
                    S.op("dve", lambda e, pf=pf: e.tensor_tensor(out=mt[0][:], in0=psm[pf][:, 0:128], in1=mkL[:], op=ALU.mult),
                         [R_psm[pf], R_mkL], [R_mt[0]])
                    S.op("dve", lambda e, pb=pb: e.tensor_tensor(out=mt[1][:], in0=psm[pb][:, 0:128], in1=mkU[:], op=ALU.mult),
                         [R_psm[pb], R_mkU], [R_mt[1]])
                    S.op("dve", lambda e: e.tensor_tensor(out=mt[0][:], in0=mt[0][:], in1=mt[1][:], op=ALU.add),
                         [R_mt[0], R_mt[1]], [R_mt[0]])
                    S.op("dve", lambda e, g=g: e.scalar_tensor_tensor(out=Mall[:, g, :], in0=identf[:], scalar=dcol[:, g:g + 1],
                                                                      in1=mt[0][:], op0=ALU.mult, op1=ALU.add),
                         [R_mt[0], R_dcol], [R_M])
                S.barrier()
            with ExitStack() as st:
                Uc = [sb(st, "Uc%d" % i, [128, 8 * 1024], BF16) for i in range(1)] * 2
                R_Uc = [S.res("Uc")] * 2
                Ucr = [sb(st, "Ucr%d" % i, [128, 64, 128], BF16) for i in range(1)] * 2
                R_Ucr = [S.res("Ucr")] * 2
                Pst = sb(st, "Pst", [128, 16, 64, 8], F32)
                R_Pst = S.res("Pst")
                psP = [ps(st, "psP%d" % i, [128, 512], F32) for i in range(2)]
                R_psP = [S.res("psP") for _ in range(2)]
                ppc = 0
                Ublk = [sb(st, "Ublk%d" % i, [128, 16, 64, 8], BF16) for i in range(2)]
                R_Ublk = [S.res("Ublk") for _ in range(2)]
                psu = [ps(st, "psu%d" % i, [128, 1024], BF16) for i in range(2)]
                R_psu = [S.res("psu") for _ in range(2)]
                it = 0
                pc = 0
                for s in range(NSEG):
                    for ct in range(2):
                        b = it % 2
                        it += 1
                        S.dma("sp", Uc[b][:], u_d[s, ct * 1024:(ct + 1) * 1024, :].rearrange("(p j) f -> p (j f)", j=8),
                              [R_scr["u"]], [R_Uc[b]], R_Uc[b])
                        for hf in range(2):
                            S.op("dve" if hf else "act", (lambda e, b=b, hf=hf: e.tensor_copy(
                                out=Ucr[b][:, hf * 32:(hf + 1) * 32, :].rearrange("p g (j c) -> p g j c", j=8),
                                in_=Uc[b][:].rearrange("p (j g c) -> p g j c", j=8, g=64)[:, hf * 32:(hf + 1) * 32]))
                                if hf else (lambda e, b=b, hf=hf: e.activation(
                                    out=Ucr[b][:, hf * 32:(hf + 1) * 32, :].rearrange("p g (j c) -> p g j c", j=8),
                                    in_=Uc[b][:].rearrange("p (j g c) -> p g j c", j=8, g=64)[:, hf * 32:(hf + 1) * 32],
                                    func=AF.Copy)), [R_Uc[b]], [R_Ucr[b]])
                        for g8 in range(8):
                            pi_ = pc % 2
                            pc += 1

                            def tru(e, g8=g8, pi_=pi_, b=b):
                                ins = None
                                for gg in range(8):
                                    g = g8 * 8 + gg
                                    ins = e.transpose(out=psu[pi_][:, gg * 128:(gg + 1) * 128],
                                                      in_=Ucr[b][:, g, :], identity=identb[:])
                                return ins
                            S.op("pe", tru, [R_Ucr[b]], [R_psu[pi_]])
                            S.op("act" if g8 % 2 else "dve", lambda e, g8=g8, pi_=pi_, b=b: e.tensor_copy(
                                out=Ublk[b][:, :, g8 * 8:(g8 + 1) * 8, :].rearrange("p cb g c -> p g cb c"),
                                in_=psu[pi_][:].rearrange("p (g cb c) -> p g cb c", g=8, cb=16)) if g8 % 2 == 0 else
                                e.activation(out=Ublk[b][:, :, g8 * 8:(g8 + 1) * 8, :].rearrange("p cb g c -> p g cb c"),
                                             in_=psu[pi_][:].rearrange("p (g cb c) -> p g cb c", g=8, cb=16), func=AF.Copy),
                                [R_psu[pi_]], [R_Ublk[b]])
                        S.dma("pool", Ud[s, ct * 16:(ct + 1) * 16, :, :].rearrange("cb p f -> p cb f"),
                              Ublk[b][:].rearrange("p cb g c -> p cb (g c)"), [R_Ublk[b]], [R_Ud], R_Ublk[b])
                        for d in range(2):
                            for g4 in range(16):
                                pq = ppc % 2
                                ppc += 1

                                def mmP(e, d=d, g4=g4, pq=pq, b=b):
                                    ins = None
                                    for gi in range(4):
                                        ins = e.matmul(psP[pq][:, gi * 128:(gi + 1) * 128], lhsT=Xm[:, g4 * 4 + gi, d, :],
                                                       rhs=Ublk[b][:, :, g4 * 4 + gi, :], start=True, stop=True)
                                    return ins
                                S.op("pe", mmP, [R_Xm, R_Ublk[b]], [R_psP[pq]])
                                if g4 % 2 == 0:
                                    S.op("dve", lambda e, g4=g4, pq=pq: e.tensor_copy(
                                        out=Pst[:, :, g4 * 4:(g4 + 1) * 4, :].rearrange("p cb g c -> p g cb c"),
                                        in_=psP[pq][:].rearrange("p (g cb c) -> p g cb c", g=4, cb=16)),
                                        [R_psP[pq]], [R_Pst])
                                else:
                                    S.op("act", lambda e, g4=g4, pq=pq: e.activation(
                                        out=Pst[:, :, g4 * 4:(g4 + 1) * 4, :].rearrange("p cb g c -> p g cb c"),
                                        in_=psP[pq][:].rearrange("p (g cb c) -> p g cb c", g=4, cb=16), func=AF.Copy),
                                        [R_psP[pq]], [R_Pst])
                            S.dma("pool", Pd[s, d, ct * 16:(ct + 1) * 16, :, :].rearrange("cb p f -> p cb f"),
                                  Pst[:].rearrange("p cb g c -> p cb (g c)"), [R_Pst], [R_Pd[s][d]], R_Pst)
                S.barrier()
            def sweep_phase(st):
                Z = [[sb(st, "Zst%d%d" % (d, i), [128, 2, 64], F32) for i in range(2)] for d in range(2)]
                W = [sb(st, "Wst%d" % d, [128, 2, 64], F32) for d in range(2)]
                T1 = [sb(st, "T1st%d" % d, [128, 2, 64], F32) for d in range(2)]
                T2 = [sb(st, "T2st%d" % d, [128, 2, 64], F32) for d in range(2)]
                R_Z = [[S.res("Zst") for _ in range(2)] for _ in range(2)]
                R_W = [S.res("Wst") for _ in range(2)]
                R_T1 = [S.res("T1st") for _ in range(2)]
                R_T2 = [S.res("T2st") for _ in range(2)]
                Pb = [[sb(st, "Pb%d%d" % (d, i), [128, 2, 64, 8], F32) for i in range(2)] for d in range(2)]
                R_Pb = [[S.res("Pb") for _ in range(2)] for _ in range(2)]
                Zb = [[[sb(st, "Zb%d%d%d" % (d, k, i), [128, 64, 8], BF16) for i in range(2)] for k in range(2)] for d in range(2)]
                R_Zb = [[[S.res("Zbk") for _ in range(2)] for _ in range(2)] for _ in range(2)]
                stages = [([0, 2], [1, 2]), ([1], [0])]
                pp = 0
                for sti, (fsegs, bsegs) in enumerate(stages):
                    segs_d = [fsegs, bsegs]
                    for d in range(2):
                        if sti == 0:
                            S.op("dve", lambda e, d=d: e.memset(Z[d][pp][:], 0.0), [], [R_Z[d][pp]])
                        else:
                            S.op("dve", lambda e, d=d, pp=pp: e.tensor_scalar(
                                out=Z[d][pp][:, 0, :], in0=Z[d][pp][:, 0, :], scalar1=flags[:, 0:1], scalar2=None,
                                op0=ALU.mult), [R_Z[d][pp]], [R_Z[d][pp]])
                    for tb in range(NCB):
                        i2 = tb % 2
                        for d in range(2):
                            cb = tb if d == 0 else NCB - 1 - tb
                            for k, sg in enumerate(segs_d[d]):
                                S.dma("pool", Pb[d][i2][:, k, :, :].rearrange("p g c -> p (g c)"), Pd[sg, d, cb, :, :],
                                      [R_Pd[sg][d]], [R_Pb[d][i2]], R_Pb[d][i2])
                        for cc_ in range(8):
                            nx = 1 - pp
                            for d in range(2):
                                cc = cc_ if d == 0 else 7 - cc_
                                for k in range(len(segs_d[d])):
                                    S.op("pool", lambda e, d=d, k=k, cc=cc, pp=pp: e.tensor_copy(
                                        out=Zb[d][k][i2][:, :, cc], in_=Z[d][pp][:, k, :]),
                                        [R_Z[d][pp]], [R_Zb[d][k][i2]])
                            for d in range(2):
                                cc = cc_ if d == 0 else 7 - cc_
                                nk = len(segs_d[d])
                                S.op("dve", lambda e, d=d, cc=cc, nk=nk, pp=pp: e.tensor_tensor(
                                    out=W[d][:, 0:nk, :], in0=Z[d][pp][:, 0:nk, :], in1=Pb[d][i2][:, 0:nk, :, cc], op=ALU.add),
                                    [R_Z[d][pp], R_Pb[d][i2]], [R_W[d]], nosame=True)
                            yield
                            for d in range(2):
                                nk = len(segs_d[d])
                                S.op("dve", lambda e, d=d, nk=nk: e.tensor_tensor(
                                    out=T1[d][:, 0:nk, :], in0=W[d][:, 0:nk, :],
                                    in1=Ar[:, d * 64:(d + 1) * 64].unsqueeze(1).to_broadcast([128, nk, 64]), op=ALU.mult),
                                    [R_W[d], R_A], [R_T1[d]], nosame=True)
                            yield
                            for d in range(2):
                                nk = len(segs_d[d])
                                S.op("dve", lambda e, d=d, nk=nk: e.tensor_tensor(
                                    out=T2[d][0:64, 0:nk, :], in0=W[d][64:128, 0:nk, :],
                                    in1=Aisw[64:128, d * 64:(d + 1) * 64].unsqueeze(1).to_broadcast([64, nk, 64]), op=ALU.mult),
                                    [R_W[d], R_A], [R_T2[d]], nosame=True)
                            yield
                            for d in range(2):
                                nk = len(segs_d[d])
                                S.op("dve", lambda e, d=d, nk=nk: e.tensor_tensor(
                                    out=T2[d][64:128, 0:nk, :], in0=W[d][0:64, 0:nk, :],
                                    in1=Aisw[0:64, d * 64:(d + 1) * 64].unsqueeze(1).to_broadcast([64, nk, 64]), op=ALU.mult),
                                    [R_W[d], R_A], [R_T2[d]], nosame=True)
                            yield
                            for d in range(2):
                                nk = len(segs_d[d])
                                S.op("dve", lambda e, d=d, nk=nk, nx=nx: e.tensor_tensor(
                                    out=Z[d][nx][:, 0:nk, :], in0=T1[d][:, 0:nk, :], in1=T2[d][:, 0:nk, :], op=ALU.add),
                                    [R_T1[d], R_T2[d]], [R_Z[d][nx]], nosame=True)
                            pp = nx
                            yield
                        for d in range(2):
                            cb = tb if d == 0 else NCB - 1 - tb
                            for k, sg in enumerate(segs_d[d]):
                                S.dma("pool", Zd[sg, d, cb, :, :], Zb[d][k][i2][:].rearrange("p g c -> p (g c)"),
                                      [R_Zb[d][k][i2]], [R_Zd[sg][d]], R_Zb[d][k][i2])


            stX.close()
            with ExitStack() as stc:
                ga = att_phase(stc)
                gs = sweep_phase(stc)
                done_a = done_s = False
                while not (done_a and done_s):
                    if not done_a:
                        for _ in range(ATT_PER_SWEEP):
                            try:
                                next(ga)
                            except StopIteration:
                                done_a = True
                                break
                    if not done_s:
                        try:
                            next(gs)
                        except StopIteration:
                            done_s = True
                S.barrier()
            with ExitStack() as st:
                sel = sb(st, "sel", [128, 64, 128], BF16)
                R_sel = S.res("sel")
                S.dma("pool", sel[:], s5_sel_in[:, :, :], [], [R_sel], R_sel)
                Uo = [sb(st, "Uo%d" % i, [128, 8, 64, 8], BF16) for i in range(2)]
                Zo = [[sb(st, "Zo%d%d" % (d, i), [128, 8, 64, 8], BF16) for i in range(2)] for d in range(2)]
                R_Uo = [S.res("Uo") for _ in range(2)]
                R_Zo = [[S.res("Zo") for _ in range(2)] for _ in range(2)]
                Yg = sb(st, "Yg", [128, 64, 64], BF16)
                R_Yg = [S.res("Yg") for _ in range(8)]
                ygS = [sb(st, "ygS%d" % i, [128, 8, 512], BF16) for i in range(2)]
                R_ygS = [S.res("ygS") for _ in range(2)]
                psy = [ps(st, "psy%d" % i, [128, 512], F32) for i in range(3)]
                R_psy = [S.res("psy") for _ in range(3)]
                pss = [ps(st, "pss%d" % i, [128, 512], F32) for i in range(3)]
                R_pss = [S.res("pss") for _ in range(3)]
                it = 0
                yc = 0
                sc = 0
                for s in range(NSEG):
                    for blk in range(4):
                        b = it % 2
                        it += 1
                        S.dma("sp", Uo[b][:].rearrange("p cb g c -> p cb (g c)"),
                              Ud[s, blk * 8:(blk + 1) * 8, :, :].rearrange("cb p f -> p cb f"), [R_Ud], [R_Uo[b]], R_Uo[b])
                        for d in range(2):
                            S.dma("sp", Zo[d][b][:].rearrange("p cb g c -> p cb (g c)"),
                                  Zd[s, d, blk * 8:(blk + 1) * 8, :, :].rearrange("cb p f -> p cb f"), [R_Zd[s][d]],
                                  [R_Zo[d][b]], R_Zo[d][b])
                        for o in range(8):
                            pi_ = yc % 3
                            yc += 1

                            def mmy(e, o=o, pi_=pi_, b=b):
                                ins = None
                                for gg in range(8):
                                    g = o * 8 + gg
                                    outp = psy[pi_][:, gg * 64:(gg + 1) * 64]
                                    e.matmul(outp, lhsT=Mall[:, g, :], rhs=Uo[b][:, :, g, :], start=True, stop=False)
                                    e.matmul(outp, lhsT=Ym[:, g, :], rhs=Zo[0][b][:, :, g, :], start=False, stop=False)
                                    ins = e.matmul(outp, lhsT=Ym[:, 64 + g, :], rhs=Zo[1][b][:, :, g, :], start=False, stop=True)
                                return ins
                            S.op("pe", mmy, [R_M, R_Ym, R_Uo[b], R_Zo[0][b], R_Zo[1][b]], [R_psy[pi_]])
                            S.op("act", lambda e, o=o, pi_=pi_: e.activation(
                                out=Yg[:, o * 8:(o + 1) * 8, :].rearrange("p g c -> p (g c)"), in_=psy[pi_][:],
                                func=AF.Gelu_apprx_tanh), [R_psy[pi_]], [R_Yg[o]])
                        for o in range(8):
                            si = sc % 3
                            sc += 1

                            def mms(e, o=o, si=si):
                                ins = None
                                for j in range(8):
                                    for gg in range(8):
                                        ins = e.matmul(pss[si][:, j * 64:(j + 1) * 64], lhsT=sel[:, j * 8 + gg, :],
                                                       rhs=Yg[:, o * 8 + gg, :], start=(gg == 0), stop=(gg == 7))
                                return ins
                            S.op("pe", mms, [R_sel, R_Yg[o]], [R_pss[si]])
                            S.op("dve", lambda e, o=o, si=si, b=b: e.tensor_copy(
                                out=ygS[b][:, o, :].rearrange("p (c j) -> p j c", j=8),
                                in_=pss[si][:].rearrange("p (j c) -> p j c", j=8)), [R_pss[si]], [R_ygS[b]])
                        S.dma("pool", ygT_d[s, :, :, blk * 512:(blk + 1) * 512].rearrange("c p t -> p c t"), ygS[b][:],
                              [R_ygS[b]], [R_scr["ygT"]], R_ygS[b])
                S.barrier()
    if DBG_S5_IDENT:
        with ExitStack() as st:
            ub = sb(st, "ub", [128, 1024], BF16)
            R_ub = S.res("ub")
            ut = sb(st, "ut", [128, 8, 128], BF16)
            R_ut = S.res("ut")
            pst = ps(st, "pst", [128, 1024], BF16)
            R_pst = S.res("pst")
            for s in range(NSEG):
                for tb in range(SEG // 128):
                    S.dma("sp", ub[:], u_d[s, tb * 128:(tb + 1) * 128, :], [R_scr["u"]], [R_ub], R_ub)

                    def tr(e):
                        ins = None
                        for c in range(8):
                            ins = e.transpose(out=pst[:, c * 128:(c + 1) * 128], in_=ub[:, c * 128:(c + 1) * 128],
                                              identity=identb[:])
                        return ins
                    S.op("pe", tr, [R_ub], [R_pst])
                    S.op("act", lambda e: e.activation(out=ut[:], in_=pst[:].rearrange("p (c t) -> p c t", c=8),
                                                       func=AF.Gelu_apprx_tanh), [R_pst], [R_ut])
                    S.dma("pool", ygT_d[s, :, :, tb * 128:(tb + 1) * 128].rearrange("c p t -> p c t"), ut[:], [R_ut],
                          [R_scr["ygT"]], R_ut)
            S.barrier()

    if STOP_AFTER == '2':
        es.close()
        return nc
    if DBG_ATT_IDENT:
        with ExitStack() as st:
            vb = sb(st, "vb", [128, 1024], BF16)
            R_vb = S.res("vb")
            azb = sb(st, "azb", [128, 8, 128], BF16)
            R_azb = S.res("azb")
            vt = sb(st, "vt", [128, 8, 128], BF16)
            R_vt = S.res("vt")
            pst = ps(st, "pst", [128, 1024], BF16)
            R_pst = S.res("pst")
            for s in range(NSEG):
                for tb in range(SEG // 128):
                    S.dma("sp", vb[:], v_d[s, tb * 128:(tb + 1) * 128, :], [R_scr["v"]], [R_vb], R_vb)
                    S.dma("sp", azb[:], azT_d[s, :, :, tb * 128:(tb + 1) * 128].rearrange("c p t -> p c t"),
                          [R_scr["azT"]], [R_azb], R_azb)

                    def tr(e):
                        ins = None
                        for c in range(8):
                            ins = e.transpose(out=pst[:, c * 128:(c + 1) * 128], in_=vb[:, c * 128:(c + 1) * 128],
                                              identity=identb[:])
                        return ins
                    S.op("pe", tr, [R_vb], [R_pst])
                    S.op("dve", lambda e: e.tensor_tensor(out=vt[:], in0=pst[:].rearrange("p (c t) -> p c t", c=8),
                                                          in1=azb[:], op=ALU.mult), [R_pst, R_azb], [R_vt])
                    S.dma("pool", ozT_d[s, :, :, tb * 128:(tb + 1) * 128].rearrange("c p t -> p c t"), vt[:], [R_vt],
                          [R_scr["ozT"]], R_vt)
            S.barrier()

    if STOP_AFTER == '3':
        es.close()
        return nc
    with ExitStack() as st:
        mk_slabs(st)
        ygT = sb(st, "ygT", [128, 8, TT], BF16)
        szT = sb(st, "szT", [128, 8, TT], BF16)
        ozT = sb(st, "ozT", [128, 8, TT], BF16)
        gsT = sb(st, "gsT", [128, 16, TT], BF16)
        gaT = sb(st, "gaT", [128, 16, TT], BF16)
        R_yg, R_sz, R_oz, R_gs, R_ga = (S.res(n) for n in ("ygT", "szT", "ozT", "gsT", "gaT"))
        R_szc = [S.res("szc") for _ in range(8)]
        R_gsc = [S.res("gsc") for _ in range(16)]
        xh = sb(st, "xh", [128, 4, D], F32)
        R_xh = [S.res("xh") for _ in range(4)]
        pin = sb(st, "pin", [128, 4, 256], F32)
        pinb = sb(st, "pinb", [128, 4, 256], BF16)
        R_pin = S.res("pin")
        R_pinb = S.res("pinb")
        pT = sb(st, "pT", [128, 2, TT], BF16)
        R_pT = S.res("pT")
        fing = sb(st, "fing", [128, D], F32)
        R_fing = S.res("fing")
        S.dma("sp", fing[:], fing_in[:, :], [], [R_fing], R_fing)
        hnb = sb(st, "hnb", [128, 4, D], BF16)
        R_hnb = [S.res("hnb") for _ in range(4)]
        hn2T = sb(st, "hn2T", [128, 16, TT], BF16)
        R_hn2T = [S.res("hn2T") for _ in range(16)]
        junk = sb(st, "junk4", [128, D], BF16)
        R_junk = S.res("junk4")
        ss = sb(st, "ss4", [128, 16], F32)
        R_ss = S.res("ss4")
        sig = sb(st, "sig", [128, 512], BF16)
        R_sig = S.res("sig")
        tmpf = [sb(st, "tmpf%d" % i, [128, 512], F32) for i in range(2)]
        R_tmpf = [S.res("tmpf") for _ in range(2)]
        tmpg = [sb(st, "tmpg%d" % i, [128, 512], F32) for i in range(2)]
        R_tmpg = [S.res("tmpg") for _ in range(2)]
        psT2 = [ps(st, "psT40", [128, 1024], BF16), ps(st, "psT41", [128, 1024], BF16)]
        R_psT = [S.res("psT0"), S.res("psT1")]
        psA = [ps(st, "psB%d" % i, [128, 512], F32) for i in range(6)]
        R_psA = [S.res("psB%d" % i) for i in range(6)]
        pa_ctr = 0

        def nextps():
            nonlocal pa_ctr
            i = pa_ctr % 6
            pa_ctr += 1
            return i

        def tile_gen(s, t):
            tok0 = t * TT
            tsl = slice(tok0, tok0 + TT)
            for (buf, dsrc, rr, rn, nch) in ((ygT, ygT_d, R_yg, "ygT", 8), (szT, szT_d, R_sz, "szT", 8),
                                             (ozT, ozT_d, R_oz, "ozT", 8), (gsT, gsT_d, R_gs, "gsT", 16),
                                             (gaT, gaT_d, R_ga, "gaT", 16)):
                extra = R_szc if buf is szT else (R_gsc if buf is gsT else [])
                S.dma("sp", buf[:], dsrc[s, :, :, tsl].rearrange("c p t -> p c t"), [R_scr[rn]], [rr] + extra, rr)
            for j in range(2):
                slab, rslab = load_slab(wb_glu, R_w["glu"], 8, j * 512)
                for m in range(4):
                    fo = j * 4 + m
                    pi = nextps()

                    def mm(e, m=m, pi=pi, slab=slab):
                        ins = None
                        for kc in range(8):
                            ins = e.matmul(psA[pi][:], lhsT=slab[:, kc, m * 128:(m + 1) * 128], rhs=ygT[:, kc, :],
                                           start=(kc == 0), stop=(kc == 7))
                        return ins
                    S.op("pe", mm, [rslab, R_yg], [R_psA[pi]])
                    S.op("act", lambda e, pi=pi, fo=fo: e.activation(out=sig[:], in_=psA[pi][:], func=AF.Sigmoid,
                                                                     bias=glub[:, fo:fo + 1]), [R_psA[pi]], [R_sig])
                    S.op("dve", lambda e, fo=fo: e.tensor_tensor(out=sig[:], in0=sig[:], in1=ygT[:, fo, :], op=ALU.mult),
                         [R_sig, R_yg], [R_sig])
                    S.op("dve", lambda e, fo=fo: e.tensor_tensor(out=szT[:, fo, :], in0=sig[:], in1=szT[:, fo, :],
                                                                 op=ALU.mult), [R_sig, R_sz], [R_szc[fo]])
            for j in range(4):
                slab_s, rs_s = load_slab(wb_s, R_w["s"], 8, j * 512)
                slab_a, rs_a = load_slab(wb_a, R_w["a"], 8, j * 512)
                for m in range(4):
                    dm = j * 4 + m
                    p1 = nextps()
                    p2 = nextps()

                    def mm1(e, m=m, p1=p1, slab=slab_s):
                        ins = None
                        for kc in range(8):
                            ins = e.matmul(psA[p1][:], lhsT=slab[:, kc, m * 128:(m + 1) * 128], rhs=szT[:, kc, :],
                                           start=(kc == 0), stop=(kc == 7))
                        return ins

                    def mm2(e, m=m, p2=p2, slab=slab_a):
                        ins = None
                        for kc in range(8):
                            ins = e.matmul(psA[p2][:], lhsT=slab[:, kc, m * 128:(m + 1) * 128], rhs=ozT[:, kc, :],
                                           start=(kc == 0), stop=(kc == 7))
                        return ins
                    S.op("pe", mm1, [rs_s] + R_szc, [R_psA[p1]])
                    S.op("pe", mm2, [rs_a, R_oz], [R_psA[p2]])
                    S.op("dve", lambda e, dm=dm, p1=p1: e.tensor_tensor(out=tmpf[0][:], in0=psA[p1][:], in1=gsT[:, dm, :],
                                                                        op=ALU.mult), [R_psA[p1], R_gs], [R_tmpf[0]])
                    S.op("dve", lambda e, dm=dm, p2=p2: e.tensor_tensor(out=tmpf[1][:], in0=psA[p2][:], in1=gaT[:, dm, :],
                                                                        op=ALU.mult), [R_psA[p2], R_ga], [R_tmpf[1]])
                    S.op("pool", lambda e, dm=dm: e.tensor_tensor(out=gsT[:, dm, :], in0=tmpf[0][:], in1=tmpf[1][:],
                                                                  op=ALU.add), [R_tmpf[0], R_tmpf[1], R_gs], [R_gsc[dm]])
            yield
            pre_slabs = [load_slab(wb_o, R_w["o"], 16, jj * 512) for jj in range(3)]
            S.dma("sp", pin[:], p_in[s, tsl, :].rearrange("(a p) d -> p a d", p=128), [], [R_pin], R_pin)
            for a in range(4):
                S.dma("sp", xh[:, a, :], x_in[s, tok0 + a * 128:tok0 + (a + 1) * 128, :], [], [R_xh[a]], R_xh[a])
            for j in range(4):
                if j == 1:
                    pre_slabs.append(load_slab(wb_o, R_w["o"], 16, 3 * 512))
                slab, rslab = pre_slabs[j]
                for a in range(4):
                    pi = nextps()

                    def mm(e, a=a, pi=pi, slab=slab):
                        ins = None
                        for kc in range(16):
                            ins = e.matmul(psA[pi][:], lhsT=gsT[:, kc, a * 128:(a + 1) * 128], rhs=slab[:, kc, :],
                                           start=(kc == 0), stop=(kc == 15))
                        return ins
                    S.op("pe", mm, [rslab] + R_gsc, [R_psA[pi]])
                    S.op("dve", lambda e, a=a, j=j, pi=pi: e.tensor_tensor(
                        out=xh[:, a, j * 512:(j + 1) * 512], in0=psA[pi][:], in1=xh[:, a, j * 512:(j + 1) * 512],
                        op=ALU.add), [R_psA[pi], R_xh[a]], [R_xh[a]])
            yield
            for a in range(4):
                S.op("act", lambda e, a=a: e.activation(out=junk[:], in_=xh[:, a, :], func=AF.Square,
                                                        accum_out=ss[:, a:a + 1]), [R_xh[a]], [R_junk, R_ss])
            S.op("dve", lambda e: e.tensor_scalar(out=ss[:, 4:8], in0=ss[:, 0:4], scalar1=1.0 / D, scalar2=EPS,
                                                  op0=ALU.mult, op1=ALU.add), [R_ss], [R_ss])
            S.op("act", lambda e: e.activation(out=ss[:, 4:8], in_=ss[:, 4:8], func=AF.Sqrt), [R_ss], [R_ss])
            S.op("dve", lambda e: e.reciprocal(out=ss[:, 4:8], in_=ss[:, 4:8]), [R_ss], [R_ss])
            for a in range(4):
                if a % 2 == 0:
                    S.op("dve", lambda e, a=a: e.tensor_scalar(out=hnb[:, a, :], in0=xh[:, a, :], scalar1=ss[:, 4 + a:5 + a],
                                                               scalar2=None, op0=ALU.mult), [R_xh[a], R_ss], [R_hnb[a]])
                else:
                    S.op("act", lambda e, a=a: e.activation(out=hnb[:, a, :], in_=xh[:, a, :], func=AF.Copy,
                                                            scale=ss[:, 4 + a:5 + a]), [R_xh[a], R_ss], [R_hnb[a]])
            for kc in range(16):
                h = kc % 2

                def tr(e, kc=kc, h=h):
                    ins = None
                    for a in range(4):
                        ins = e.transpose(out=psT2[h][:, a * 128:(a + 1) * 128],
                                          in_=hnb[:, a, kc * 128:(kc + 1) * 128], identity=identb[:])
                    return ins
                S.op("pe", tr, R_hnb, [R_psT[h]])
                S.op("dve", lambda e, kc=kc, h=h: e.tensor_scalar(out=hn2T[:, kc, :], in0=psT2[h][:, 0:512],
                                                                   scalar1=pleg[:, kc:kc + 1], scalar2=None, op0=ALU.mult),
                     [R_psT[h]], [R_hn2T[kc]])
            S.op("act", lambda e: e.activation(out=pinb[:], in_=pin[:], func=AF.Copy), [R_pin], [R_pinb])
            for kc in range(2):
                h = kc % 2

                def tr(e, kc=kc, h=h):
                    ins = None
                    for a in range(4):
                        ins = e.transpose(out=psT2[h][:, a * 128:(a + 1) * 128],
                                          in_=pinb[:, a, kc * 128:(kc + 1) * 128], identity=identb[:])
                    return ins
                S.op("pe", tr, [R_pinb], [R_psT[h]])
                S.op("dve", lambda e, kc=kc, h=h: e.tensor_copy(out=pT[:, kc, :], in_=psT2[h][:, 0:512]),
                     [R_psT[h]], [R_pT])
            for j in range(4):
                slab_g, rs_g = load_slab(wb_pg, R_w["pg"], 16, j * 512)
                slab_p, rs_p = load_slab(wb_pp, R_w["pp"], 2, j * 512)
                for a in range(4):
                    p1 = nextps()
                    p2 = nextps()

                    def mm1(e, a=a, p1=p1, slab=slab_g):
                        ins = None
                        for kc in range(16):
                            ins = e.matmul(psA[p1][:], lhsT=hn2T[:, kc, a * 128:(a + 1) * 128], rhs=slab[:, kc, :],
                                           start=(kc == 0), stop=(kc == 15))
                        return ins

                    def mm2(e, a=a, p2=p2, slab=slab_p):
                        ins = None
                        for kc in range(2):
                            ins = e.matmul(psA[p2][:], lhsT=pT[:, kc, a * 128:(a + 1) * 128], rhs=slab[:, kc, :],
                                           start=(kc == 0), stop=(kc == 1))
                        return ins
                    S.op("pe", mm1, [rs_g] + R_hn2T, [R_psA[p1]])
                    S.op("pe", mm2, [rs_p, R_pT], [R_psA[p2]])
                    S.op("act", lambda e, p1=p1: e.activation(out=tmpg[0][:], in_=psA[p1][:], func=AF.Sigmoid),
                         [R_psA[p1]], [R_tmpg[0]])
                    S.op("dve", lambda e, p2=p2: e.tensor_tensor(out=tmpg[1][:], in0=psA[p2][:], in1=tmpg[0][:],
                                                                 op=ALU.mult), [R_psA[p2], R_tmpg[0]], [R_tmpg[1]])
                    S.op("pool", lambda e, a=a, j=j: e.tensor_tensor(
                        out=xh[:, a, j * 512:(j + 1) * 512], in0=tmpg[1][:], in1=xh[:, a, j * 512:(j + 1) * 512],
                        op=ALU.add), [R_tmpg[1], R_xh[a]], [R_xh[a]])
            for a in range(4):
                S.op("act", lambda e, a=a: e.activation(out=junk[:], in_=xh[:, a, :], func=AF.Square,
                                                        accum_out=ss[:, 8 + a:9 + a]), [R_xh[a]], [R_junk, R_ss])
            S.op("dve", lambda e: e.tensor_scalar(out=ss[:, 12:16], in0=ss[:, 8:12], scalar1=1.0 / D, scalar2=EPS,
                                                  op0=ALU.mult, op1=ALU.add), [R_ss], [R_ss])
            S.op("act", lambda e: e.activation(out=ss[:, 12:16], in_=ss[:, 12:16], func=AF.Sqrt), [R_ss], [R_ss])
            S.op("dve", lambda e: e.reciprocal(out=ss[:, 12:16], in_=ss[:, 12:16]), [R_ss], [R_ss])
            for a in range(4):
                S.op("dve", lambda e, a=a: e.scalar_tensor_tensor(out=xh[:, a, :], in0=xh[:, a, :],
                                                                  scalar=ss[:, 12 + a:13 + a], in1=fing[:],
                                                                  op0=ALU.mult, op1=ALU.mult),
                     [R_xh[a], R_ss, R_fing], [R_xh[a]])
                S.dma("pool", y_out[s, tok0 + a * 128:tok0 + (a + 1) * 128, :], xh[:, a, :], [R_xh[a]], [], R_xh[a])

        tiles = [(s_, t_) for s_ in range(NSEG) for t_ in range(SEG // TT)]
        gens = [tile_gen(s_, t_) for (s_, t_) in tiles]
        next(gens[0])
        for i in range(len(tiles)):
            next(gens[i])
            if i + 1 < len(tiles):
                next(gens[i + 1])
            for _ in gens[i]:
                pass
        S.barrier()
    es.close()
    return nc


_NC_CACHE = {}


def _core_segments(i):
    if i < 4:
        return [("p", i, 0), ("p", i, SEG), ("s", i, 0)]
    b = 4 + 3 * (i - 4)
    return [("s", b, 0), ("s", b + 1, 0), ("s", b + 2, 0)]


def _bucket(rel):
    half = 16
    max_exact = 8
    ret = (rel > 0).astype(np.int32) * half
    n = np.abs(rel)
    nf = np.maximum(n, 1).astype(np.float32)
    large = max_exact + (np.log(nf / max_exact) / math.log(128 / max_exact) * (half - max_exact)).astype(np.int32)
    large = np.minimum(large, half - 1)
    return ret + np.where(n < max_exact, n, large)


def kernel(**inp):
    f32 = np.float32
    xp, xs_ = np.asarray(inp["x_prompt"], f32), np.asarray(inp["x_sample"], f32)
    pp, ps_ = np.asarray(inp["p_prompt"], f32)[0], np.asarray(inp["p_sample"], f32)[0]
    if "nc" not in _NC_CACHE:
        _NC_CACHE["nc"] = build_nc()
    nc = _NC_CACHE["nc"]

    def chunkcols(v, n):
        return np.ascontiguousarray(np.asarray(v, f32).reshape(n, 128).T)

    nn = np.arange(1536)
    oh = np.zeros((32, 1536), f32)
    oh[_bucket(767 - nn), nn] = 1.0
    lamv = np.stack([np.asarray(inp[k], f32)[0] for k in ("lam_q1", "lam_k1", "lam_q2", "lam_k2")])
    common = {
        "ident": np.eye(128, dtype=f32),
        "w_in": np.asarray(inp["w_in"], f32)[0],
        "glu_w": np.asarray(inp["glu_w"], f32)[0],
        "w_branch_s": np.asarray(inp["w_branch_s"], f32)[0],
        "w_branch_a": np.asarray(inp["w_branch_a"], f32)[0],
        "w_out": np.asarray(inp["w_out"], f32)[0],
        "ple_gate_w": np.asarray(inp["ple_gate_w"], f32)[0],
        "ple_proj_w": np.asarray(inp["ple_proj_w"], f32)[0],
        "norm_g": chunkcols(inp["norm_g"][0], 16),
        "ple_norm_g": chunkcols(inp["ple_norm_g"][0], 16),
        "glu_b": chunkcols(inp["glu_b"][0], 8),
        "subln_g": chunkcols(inp["subln_g"][0], 2),
        "final_g": np.ascontiguousarray(np.broadcast_to(np.asarray(inp["final_g"], f32)[None, :], (128, D))),
        "lamv": np.ascontiguousarray(np.broadcast_to(lamv[None], (128, 4, 128))),
        "rel_bias": np.asarray(inp["rel_bias"], f32),
        "onehot": oh,
    }
    def nmaj(a):
        return np.ascontiguousarray(np.asarray(a, f32)[0].transpose(2, 0, 1).reshape(64, 128))
    lr = nmaj(inp["ssm_lambda_re"]); li = nmaj(inp["ssm_lambda_im"])
    ldt = np.ascontiguousarray(np.broadcast_to(np.asarray(inp["ssm_log_dt"], f32)[0].reshape(1, 128), (64, 128)))
    br = np.asarray(inp["ssm_b_re"], f32)[0].transpose(2, 0, 1, 3).reshape(64, 128, 16)
    bim = np.asarray(inp["ssm_b_im"], f32)[0].transpose(2, 0, 1, 3).reshape(64, 128, 16)
    cr = np.asarray(inp["ssm_c_re"], f32)[0].transpose(3, 0, 1, 2).reshape(64, 128, 16)
    cim = np.asarray(inp["ssm_c_im"], f32)[0].transpose(3, 0, 1, 2).reshape(64, 128, 16)
    jj = np.arange(8, dtype=f32)
    kx = np.zeros((128, 128, 8), f32); ky = np.zeros((128, 128, 8), f32)
    kx[:, :64, :] = -jj; kx[:, 64:, :] = jj - 7.0
    ky[:, :64, :] = jj; ky[:, 64:, :] = 7.0 - jj
    pj = np.arange(128) // 16
    pc_ = np.arange(128) % 16
    dvec = np.asarray(inp["ssm_d"], f32)[0].reshape(64, 16)
    sel = np.zeros((128, 64, 128), f32)
    for j in range(8):
        for g8 in range(8):
            for co in range(16):
                sel[j * 16 + co, j * 8 + g8, g8 * 16 + co] = 1.0
    common.update({
        "s5_lr": np.concatenate([lr, lr]), "s5_li": np.concatenate([li, li]), "s5_ldt": np.concatenate([ldt, ldt]),
        "s5_ba": np.concatenate([br, bim]), "s5_bb": np.concatenate([bim, br]),
        "s5_ca": np.concatenate([cr, cim]), "s5_cb": np.concatenate([cim, cr]),
        "s5_kx": kx, "s5_ky": ky,
        "s5_dcol": np.ascontiguousarray(dvec.T[pc_, :]),
        "s5_maskl": (pj[None, :] >= pj[:, None]).astype(f32),
        "s5_masku": (pj[None, :] <= pj[:, None]).astype(f32),
        "s5_sel": sel,
    })
    in_maps = []
    for i in range(8):
        segs = _core_segments(i)
        xs = np.stack([(xp if g == "p" else xs_)[b, st:st + SEG] for (g, b, st) in segs])
        pc = np.stack([(pp if g == "p" else ps_)[b, st:st + SEG] for (g, b, st) in segs])
        fl = np.zeros((128, 2), f32)
        fl[:, 0] = 1.0 if i < 4 else 0.0
        fl[:, 1] = 0.0 if i < 4 else -30000.0
        m = dict(common)
        m.update({"x": np.ascontiguousarray(xs), "p": np.ascontiguousarray(pc), "flags": fl})
        in_maps.append(m)
    res = run_bass_kernel_spmd(nc, in_maps, core_ids=list(range(8)))
    y_p = np.zeros_like(xp)
    y_s = np.zeros_like(xs_)
    for i in range(8):
        y = res.results[i]["y"]
        for j, (g, b, st) in enumerate(_core_segments(i)):
            (y_p if g == "p" else y_s)[b, st:st + SEG] = y[j]
    return (y_p, y_s)
```

```python
import math
from contextlib import ExitStack
import numpy as np
import concourse.bass as bass
import concourse.mybir as mybir
from concourse.bass_utils import run_bass_kernel_spmd

F32 = mybir.dt.float32
BF16 = mybir.dt.bfloat16
AF = mybir.ActivationFunctionType
ALU = mybir.AluOpType
AX = mybir.AxisListType

D = 2048
SEG = 2048
NSEG = 3
TT = 512
NQ = 10240
EPS = 1e-6
LAM_INIT = 0.8 - 0.6 * math.exp(-0.3 * 0)

DBG_S5_IDENT = False
DBG_ATT_IDENT = False
STOP_AFTER = None
ATT_LIM = None
ATT_PER_SWEEP = 1
P1_NSEG = NSEG
P1_NT = SEG // TT
P1_PLAN = None


class Res:
    __slots__ = ("name", "w", "r", "sem", "cnt")

    def __init__(self, name):
        self.name = name
        self.w = {}
        self.r = {}
        self.sem = None
        self.cnt = 0


class Sched:
    def __init__(self, nc, es):
        self.nc = nc
        self.es = es
        self.e = {"pe": nc.tensor, "act": nc.scalar, "dve": nc.vector, "pool": nc.gpsimd, "sp": nc.sync}
        self.sem = {k: es.enter_context(nc.semaphore("c_" + k)) for k in ("pe", "act", "dve", "pool")}
        self.cnt = {k: 0 for k in self.sem}
        self.waited = {k: {} for k in self.e}
        self.dres = []
        self.nres = 0

    def res(self, name):
        self.nres += 1
        return Res("%s_%d" % (name, self.nres))

    def _deps(self, eng, reads, writes, nosame=False):
        need = {}
        for r in reads:
            for k, v in r.w.items():
                if need.get(k, 0) < v:
                    need[k] = v
        for w in writes:
            for k, v in w.w.items():
                if need.get(k, 0) < v:
                    need[k] = v
            for k, v in w.r.items():
                if need.get(k, 0) < v:
                    need[k] = v
        wd = self.waited[eng]
        for k, v in need.items():
            if eng == "pe" and k is self.sem["pe"]:
                continue
            if nosame and eng in self.sem and k is self.sem[eng]:
                continue
            if wd.get(k, 0) < v:
                wd[k] = v
                self.e[eng].wait_ge(k, v)

    def op(self, eng, fn, reads=(), writes=(), nosame=False):
        self._deps(eng, reads, writes, nosame)
        self.cnt[eng] += 1
        c = self.cnt[eng]
        sem = self.sem[eng]
        fn(self.e[eng]).then_inc(sem, 1)
        for r in reads:
            r.r[sem] = c
        for w in writes:
            w.w[sem] = c
            w.r = {}

    def dma(self, q, out, in_, reads, writes, semres):
        self._deps(q, reads, writes)
        if semres.sem is None:
            semres.sem = self.es.enter_context(self.nc.semaphore("d_" + semres.name))
            self.dres.append(semres)
        semres.cnt += 16
        v = semres.cnt
        sem = semres.sem
        self.e[q].dma_start(out=out, in_=in_).then_inc(sem, 16)
        for r in reads:
            r.r[sem] = v
        for w in writes:
            w.w[sem] = v
            w.r = {}

    def barrier(self):
        for eng in self.e:
            wd = self.waited[eng]
            for k in self.sem:
                v = self.cnt[k]
                if v > 0 and wd.get(self.sem[k], 0) < v:
                    wd[self.sem[k]] = v
                    self.e[eng].wait_ge(self.sem[k], v)
            for dr in self.dres:
                if dr.cnt > 0 and wd.get(dr.sem, 0) < dr.cnt:
                    wd[dr.sem] = dr.cnt
                    self.e[eng].wait_ge(dr.sem, dr.cnt)


def build_nc():
    nc = bass.Bass("TRN2", target_bir_lowering=False)
    es = ExitStack()
    S = Sched(nc, es)

    def din(name, shape, dt=F32):
        return nc.dram_tensor(name, list(shape), dt, kind="ExternalInput").ap()

    def dscr(name, shape, dt=BF16):
        return nc.dram_tensor("scr_" + name, list(shape), dt).ap()

    x_in = din("x", [NSEG, SEG, D])
    p_in = din("p", [NSEG, SEG, 256])
    flags_in = din("flags", [128, 2])
    ident_in = din("ident", [128, 128])
    w_in_f = din("w_in", [D, NQ])
    glu_w_f = din("glu_w", [1024, 1024])
    wbs_f = din("w_branch_s", [1024, D])
    wba_f = din("w_branch_a", [1024, D])
    wo_f = din("w_out", [D, D])
    wpg_f = din("ple_gate_w", [D, D])
    wpp_f = din("ple_proj_w", [256, D])
    normg_in = din("norm_g", [128, 16])
    pleg_in = din("ple_norm_g", [128, 16])
    glub_in = din("glu_b", [128, 8])
    subg_in = din("subln_g", [128, 2])
    fing_in = din("final_g", [128, D])
    lamv_in = din("lamv", [128, 4, 128])
    relb_in = din("rel_bias", [32, 4])
    onehot_in = din("onehot", [32, 1536])
    s5_lr_in = din("s5_lr", [128, 128])
    s5_li_in = din("s5_li", [128, 128])
    s5_ldt_in = din("s5_ldt", [128, 128])
    s5_ba_in = din("s5_ba", [128, 128, 16])
    s5_bb_in = din("s5_bb", [128, 128, 16])
    s5_ca_in = din("s5_ca", [128, 128, 16])
    s5_cb_in = din("s5_cb", [128, 128, 16])
    s5_kx_in = din("s5_kx", [128, 128, 8])
    s5_ky_in = din("s5_ky", [128, 128, 8])
    s5_dcol_in = din("s5_dcol", [128, 64])
    s5_maskl_in = din("s5_maskl", [128, 128])
    s5_masku_in = din("s5_masku", [128, 128])
    s5_sel_in = din("s5_sel", [128, 64, 128])
    y_out = nc.dram_tensor("y", [NSEG, SEG, D], F32, kind="ExternalOutput").ap()

    wb_in = dscr("wb_in", [D, NQ])
    wb_glu = dscr("wb_glu", [1024, 1024])
    wb_s = dscr("wb_s", [1024, D])
    wb_a = dscr("wb_a", [1024, D])
    wb_o = dscr("wb_o", [D, D])
    wb_pg = dscr("wb_pg", [D, D])
    wb_pp = dscr("wb_pp", [256, D])
    qT_d = dscr("qT", [NSEG, 8, 128, SEG])
    kT_d = dscr("kT", [NSEG, 8, 128, SEG])
    v_d = dscr("v", [NSEG, SEG, 1024])
    u_d = dscr("u", [NSEG, SEG, 1024])
    szT_d = dscr("szT", [NSEG, 8, 128, SEG])
    azT_d = dscr("azT", [NSEG, 8, 128, SEG])
    gsT_d = dscr("gsT", [NSEG, 16, 128, SEG])
    gaT_d = dscr("gaT", [NSEG, 16, 128, SEG])
    ygT_d = dscr("ygT", [NSEG, 8, 128, SEG])
    ozT_d = dscr("ozT", [NSEG, 8, 128, SEG])
    R_scr = {n: S.res(n) for n in ("qT", "kT", "v", "u", "szT", "azT", "gsT", "gaT", "ygT", "ozT")}

    uniq = [0]

    def sb(stack, name, shape, dt):
        uniq[0] += 1
        return stack.enter_context(nc.sbuf_tensor("sb%d_%s" % (uniq[0], name), list(shape), dt))

    def ps(stack, name, shape, dt=F32):
        uniq[0] += 1
        return stack.enter_context(nc.psum_tensor("ps%d_%s" % (uniq[0], name), list(shape), dt))

    identf = sb(es, "identf", [128, 128], F32)
    identb = sb(es, "identb", [128, 128], BF16)
    normg = sb(es, "normg", [128, 16], F32)
    pleg = sb(es, "pleg", [128, 16], F32)
    glub = sb(es, "glub", [128, 8], F32)
    subg = sb(es, "subg", [128, 2], F32)
    flags = sb(es, "flags", [128, 2], F32)
    lamv = sb(es, "lamv", [128, 4, 128], F32)
    lamt = sb(es, "lamt", [128, 8], F32)
    R_const = S.res("const")
    for dst, src in ((identf, ident_in), (normg, normg_in), (pleg, pleg_in), (glub, glub_in), (subg, subg_in),
                     (flags, flags_in)):
        S.dma("sp", dst[:], src[:, :], [], [R_const], R_const)
    S.dma("sp", lamv[:], lamv_in[:, :, :], [], [R_const], R_const)
    S.op("dve", lambda e: e.tensor_copy(out=identb[:], in_=identf[:]), [R_const], [R_const])
    with ExitStack() as st:
        tmp = sb(st, "lamtmp", [128, 2, 128], F32)
        S.op("dve", lambda e: e.tensor_tensor(out=tmp[:, 0, :], in0=lamv[:, 0, :], in1=lamv[:, 1, :], op=ALU.mult),
             [R_const], [R_const])
        S.op("dve", lambda e: e.tensor_tensor(out=tmp[:, 1, :], in0=lamv[:, 2, :], in1=lamv[:, 3, :], op=ALU.mult),
             [R_const], [R_const])
        S.op("dve", lambda e: e.reduce_sum(out=lamt[:, 2:4], in_=tmp[:], axis=AX.X), [R_const], [R_const])
        S.op("act", lambda e: e.activation(out=lamt[:, 4:6], in_=lamt[:, 2:4], func=AF.Exp), [R_const], [R_const])
        S.op("dve", lambda e: e.tensor_tensor(out=lamt[:, 6:7], in0=lamt[:, 4:5], in1=lamt[:, 5:6], op=ALU.subtract),
             [R_const], [R_const])
        S.op("dve", lambda e: e.tensor_scalar(out=lamt[:, 0:1], in0=lamt[:, 6:7], scalar1=LAM_INIT, scalar2=None,
                                               op0=ALU.add), [R_const], [R_const])
        S.op("dve", lambda e: e.tensor_scalar(out=lamt[:, 1:2], in0=lamt[:, 0:1], scalar1=-1.0, scalar2=None,
                                               op0=ALU.mult), [R_const], [R_const])
        S.barrier()

    R_w = {}
    for name, src, dst, K in (("in", w_in_f, wb_in, D), ("glu", glu_w_f, wb_glu, 1024), ("s", wbs_f, wb_s, 1024),
                              ("a", wba_f, wb_a, 1024), ("o", wo_f, wb_o, D), ("pg", wpg_f, wb_pg, D),
                              ("pp", wpp_f, wb_pp, 256)):
        r = S.res("w" + name)
        R_w[name] = r
        if name == "in":
            R_wg = [S.res("win%d" % j) for j in range(5)]
            R_win = [R_wg[j // 4] for j in range(20)]
            continue
        for k0 in range(0, K, 128):
            S.dma("pool", dst[k0:k0 + 128, :], src[k0:k0 + 128, :], [], [r], r)

    if STOP_AFTER == 'W':
        S.barrier()
        es.close()
        return nc
    NSLAB = 3
    slabs = []
    R_slab = []
    slab_ctr = [0]

    def mk_slabs(stack):
        slabs[:] = [sb(stack, "slab%d" % i, [128, 16, 512], BF16) for i in range(NSLAB)]
        R_slab[:] = [S.res("slab%d" % i) for i in range(NSLAB)]

    def load_slab(wd, wres, KC, c0, ncols=512):
        i = slab_ctr[0] % NSLAB
        slab_ctr[0] += 1
        src = wd[:, c0:c0 + ncols].rearrange("(kc p) c -> p kc c", p=128)
        S.dma("sp", slabs[i][:, 0:KC, 0:ncols], src, [wres], [R_slab[i]], R_slab[i])
        return slabs[i], R_slab[i]

    with ExitStack() as st:
        mk_slabs(st)
        xin = sb(st, "xin", [128, 4, D], F32)
        R_xin = S.res("xin")
        xn = sb(st, "xn", [128, 4, D], BF16)
        R_xn = [S.res("xn") for _ in range(4)]
        junk = sb(st, "junk", [128, D], BF16)
        R_junk = S.res("junk")
        ss = sb(st, "ss", [128, 8], F32)
        R_ss = S.res("ss")
        hnT = [sb(st, "hnT%d" % i, [128, 16, TT], BF16) for i in range(2)]
        R_hnT = [[S.res("hnT") for _ in range(16)] for _ in range(2)]
        stg = [sb(st, "stg%d" % i, [128, 4, 512], BF16) for i in range(3)]
        R_stg = [S.res("stg%d" % i) for i in range(3)]
        psT2 = [ps(st, "psT0", [128, 1024], BF16), ps(st, "psT1", [128, 1024], BF16)]
        R_psT = [S.res("psT0"), S.res("psT1")]
        psA = [ps(st, "psA%d" % i, [128, 512], F32) for i in range(4)]
        R_psA = [S.res("psA%d" % i) for i in range(4)]
        pa_ctr = 0
        stg_ctr = 0
        plan = []
        for j in range(20):
            c0 = j * 512
            if j < 2:
                plan.append((c0, "tm", u_d, c0, "copy", "u"))
            elif j < 4:
                plan.append((c0, "fm", szT_d, (j - 2) * 4, "silu", "szT"))
            elif j < 6:
                plan.append((c0, "fm", qT_d, (j - 4) * 4, "qscale", "qT"))
            elif j < 8:
                plan.append((c0, "fm", kT_d, (j - 6) * 4, "copy", "kT"))
            elif j < 10:
                plan.append((c0, "tm", v_d, (j - 8) * 512, "copy", "v"))
            elif j < 12:
                plan.append((c0, "fm", azT_d, (j - 10) * 4, "silu", "azT"))
            elif j < 16:
                plan.append((c0, "fm", gsT_d, (j - 12) * 4, "sigmoid", "gsT"))
            else:
                plan.append((c0, "fm", gaT_d, (j - 16) * 4, "sigmoid", "gaT"))

        stage32 = [sb(st, "stage32_%d" % i, [128, 8, 512], F32) for i in range(2)]
        R_stage32 = [S.res("stage32") for _ in range(2)]
        st32_ctr = [0]

        def prep(idx, s, t):
            tok0 = t * TT
            hb = idx % 2
            S.dma("sp", xin[:], x_in[s, tok0:tok0 + TT, :].rearrange("(a p) d -> p a d", p=128), [], [R_xin], R_xin)
            for a in range(4):
                S.op("act", lambda e, a=a: e.activation(out=junk[:], in_=xin[:, a, :], func=AF.Square,
                                                        accum_out=ss[:, a:a + 1]), [R_xin], [R_junk, R_ss])
            S.op("dve", lambda e: e.tensor_scalar(out=ss[:, 4:8], in0=ss[:, 0:4], scalar1=1.0 / D, scalar2=EPS,
                                                  op0=ALU.mult, op1=ALU.add), [R_ss], [R_ss])
            S.op("act", lambda e: e.activation(out=ss[:, 4:8], in_=ss[:, 4:8], func=AF.Sqrt), [R_ss], [R_ss])
            S.op("dve", lambda e: e.reciprocal(out=ss[:, 4:8], in_=ss[:, 4:8]), [R_ss], [R_ss])
            for a in range(4):
                S.op("dve", lambda e, a=a: e.tensor_scalar(out=xn[:, a, :], in0=xin[:, a, :], scalar1=ss[:, 4 + a:5 + a],
                                                           scalar2=None, op0=ALU.mult), [R_xin, R_ss], [R_xn[a]])
            for kc in range(16):
                h = kc % 2

                def tr(e, kc=kc, h=h):
                    ins = None
                    for a in range(4):
                        ins = e.transpose(out=psT2[h][:, a * 128:(a + 1) * 128],
                                          in_=xn[:, a, kc * 128:(kc + 1) * 128], identity=identb[:])
                    return ins
                S.op("pe", tr, R_xn, [R_psT[h]])
                S.op("dve", lambda e, kc=kc, h=h, hb=hb: e.tensor_scalar(out=hnT[hb][:, kc, :], in0=psT2[h][:, 0:512],
                                                                          scalar1=normg[:, kc:kc + 1], scalar2=None,
                                                                          op0=ALU.mult), [R_psT[h]], [R_hnT[hb][kc]])

        def slab_iter(idx, s, t, entry):
            nonlocal pa_ctr, stg_ctr
            (c0, kind, dst, dbase, func, rname) = entry
            tok0 = t * TT
            hb = idx % 2
            hT = hnT[hb]
            if idx == 0:
                i = slab_ctr[0] % NSLAB
                slab_ctr[0] += 1
                slab, rslab = slabs[i], R_slab[i]
                for hf in range(2):
                    bi = st32_ctr[0] % 2
                    st32_ctr[0] += 1
                    S.dma("sp", stage32[bi][:],
                          w_in_f[hf * 1024:(hf + 1) * 1024, c0:c0 + 512].rearrange("(kc p) c -> p kc c", p=128),
                          [], [R_stage32[bi]], R_stage32[bi])
                    if hf == 0:
                        S.op("dve", lambda e, bi=bi, slab=slab: e.tensor_copy(out=slab[:, 0:8, :], in_=stage32[bi][:]),
                             [R_stage32[bi]], [rslab])
                    else:
                        S.op("act", lambda e, bi=bi, slab=slab: e.activation(out=slab[:, 8:16, :], in_=stage32[bi][:],
                                                                              func=AF.Copy), [R_stage32[bi]], [rslab])
                S.dma("pool", wb_in[:, c0:c0 + 512].rearrange("(kc p) c -> p kc c", p=128), slab[:], [rslab],
                      [R_win[c0 // 512]], R_win[c0 // 512])
            else:
                slab, rslab = load_slab(wb_in, R_win[c0 // 512], 16, c0)
            si = stg_ctr % 3
            stg_ctr += 1
            for m in range(4):
                pi = pa_ctr % 4
                pa_ctr += 1

                def mm(e, m=m, pi=pi, slab=slab, kind=kind):
                    ins = None
                    for kc in range(16):
                        if kind == "fm":
                            ins = e.matmul(psA[pi][:], lhsT=slab[:, kc, m * 128:(m + 1) * 128], rhs=hT[:, kc, :],
                                           start=(kc == 0), stop=(kc == 15))
                        else:
                            ins = e.matmul(psA[pi][:], lhsT=hT[:, kc, m * 128:(m + 1) * 128], rhs=slab[:, kc, :],
                                           start=(kc == 0), stop=(kc == 15))
                    return ins
                S.op("pe", mm, [rslab] + R_hnT[hb], [R_psA[pi]])
                if func == "copy":
                    S.op("dve", lambda e, m=m, pi=pi, si=si: e.tensor_copy(out=stg[si][:, m, :], in_=psA[pi][:]),
                         [R_psA[pi]], [R_stg[si]])
                elif func == "qscale":
                    S.op("dve", lambda e, m=m, pi=pi, si=si: e.tensor_scalar(
                        out=stg[si][:, m, :], in0=psA[pi][:], scalar1=128.0 ** -0.5, scalar2=None, op0=ALU.mult),
                        [R_psA[pi]], [R_stg[si]])
                else:
                    f = AF.Silu if func == "silu" else AF.Sigmoid
                    S.op("act", lambda e, m=m, pi=pi, si=si, f=f: e.activation(out=stg[si][:, m, :], in_=psA[pi][:],
                                                                               func=f), [R_psA[pi]], [R_stg[si]])
            if kind == "fm":
                dap = dst[s, dbase:dbase + 4, :, tok0:tok0 + TT].rearrange("c p t -> p c t")
            else:
                dap = dst[s, tok0:tok0 + TT, dbase:dbase + 512].rearrange("(a p) c -> p a c", p=128)
            S.dma("pool", dap, stg[si][:], [R_stg[si]], [R_scr[rname]], R_stg[si])

        tiles1 = [(s_, t_) for s_ in range(NSEG) for t_ in range(SEG // TT)]
        prep(0, *tiles1[0])
        for idx, (s_, t_) in enumerate(tiles1):
            for k, entry in enumerate(plan):
                if k == 10 and idx + 1 < len(tiles1):
                    prep(idx + 1, *tiles1[idx + 1])
                slab_iter(idx, s_, t_, entry)
        S.barrier()

    if STOP_AFTER == '1':
        es.close()
        return nc
    def att_phase(st):
        NB = 1536
        Bd = dscr("biasF", [4, 128, NB], F32)
        R_Bd = S.res("Bd")
        strips_hi = sb(st, "strips_hi", [128, 4, 1280], BF16)
        strips_lo = sb(st, "strips_lo", [128, 4, 1280], BF16)
        biasc = sb(st, "biasc", [128, 4, 4], F32)
        st0 = ExitStack()
        strips = sb(st0, "strips", [128, 4, 1280], F32)
        strips_d = sb(st0, "strips_d", [128, 4, 1280], F32)
        relb = sb(st0, "relb", [32, 4], F32)
        ohs = sb(st0, "ohs", [32, NB], F32)
        ones32 = sb(st0, "ones32", [32, 128], F32)
        rep = sb(st0, "rep", [32, 128], F32)
        Frep = sb(st0, "Frep", [128, NB], F32)
        R_a0 = S.res("a0")
        R_rep = S.res("rep")
        R_Frep = S.res("Frep")
        R_strips = S.res("strips")
        R_biasc = S.res("biasc")
        S.dma("sp", relb[:], relb_in[:, :], [], [R_a0], R_a0)
        S.dma("sp", ohs[:], onehot_in[:, :], [], [R_a0], R_a0)
        S.op("dve", lambda e: e.memset(ones32[:], 1.0), [], [R_a0])
        psS = [ps(st, "psS%d" % i, [128, 512], F32) for i in range(3)]
        R_psS = [S.res("psS%d" % i) for i in range(3)]
        for h in range(4):
            S.op("dve", lambda e, h=h: e.tensor_scalar(out=rep[:], in0=ones32[:], scalar1=relb[:, h:h + 1], scalar2=None,
                                                       op0=ALU.mult), [R_a0], [R_rep])
            for j in range(3):
                S.op("pe", lambda e, j=j: e.matmul(psS[j % 2][:], lhsT=rep[:], rhs=ohs[:, j * 512:(j + 1) * 512],
                                                   start=True, stop=True), [R_rep, R_a0], [R_psS[j % 2]])
                S.op("dve", lambda e, j=j: e.tensor_copy(out=Frep[:, j * 512:(j + 1) * 512], in_=psS[j % 2][:]),
                     [R_psS[j % 2]], [R_Frep])
            S.op("dve", lambda e, h=h: e.tensor_copy(out=biasc[:, h, 0:1], in_=Frep[:, 967:968]), [R_Frep], [R_biasc])
            S.op("dve", lambda e, h=h: e.tensor_copy(out=biasc[:, h, 1:2], in_=Frep[:, 567:568]), [R_Frep], [R_biasc])
            S.op("dve", lambda e, h=h: e.tensor_tensor(out=biasc[:, h, 2:3], in0=Frep[:, 967:968], in1=flags[:, 1:2],
                                                       op=ALU.add), [R_Frep], [R_biasc])
            S.op("dve", lambda e, h=h: e.tensor_tensor(out=biasc[:, h, 3:4], in0=Frep[:, 567:568], in1=flags[:, 1:2],
                                                       op=ALU.add), [R_Frep], [R_biasc])
            S.dma("sp", Bd[h, :, :], Frep[:], [R_Frep], [R_Bd], R_Frep)
            src = bass.AP(Bd.tensor, h * 128 * NB + 127, [[NB - 1, 128], [1, 1280]])
            S.dma("sp", strips[:, h, :], src, [R_Bd], [R_strips], R_strips)

        S.op("dve", lambda e: e.tensor_copy(out=strips_hi[:], in_=strips[:]), [R_strips], [R_strips])
        S.op("dve", lambda e: e.tensor_tensor(out=strips_d[:], in0=strips[:], in1=strips_hi[:], op=ALU.subtract),
             [R_strips], [R_strips])
        S.op("dve", lambda e: e.tensor_copy(out=strips_lo[:], in_=strips_d[:]), [R_strips], [R_strips])
        S.barrier()
        st0.close()
        KT = [sb(st, "KT%d" % i, [128, 2, 4096], BF16) for i in range(1)] * 2
        QT = [sb(st, "QT%d" % i, [128, 2, 4096], BF16) for i in range(1)] * 2
        VX = [sb(st, "VX%d" % i, [128, 32, 257], BF16) for i in range(1)] * 2
        R_KT = [S.res("KT")] * 2
        R_QT = [S.res("QT")] * 2
        R_VX = [S.res("VX")] * 2
        S.op("pool", lambda e: e.memset(VX[0][:, :, 256:257], 1.0), [], [R_VX[0]])
        PT = [sb(st, "PT%d" % i, [128, 512], BF16) for i in range(3)]
        R_PT = [S.res("PT") for _ in range(3)]
        tmpb = [sb(st, "tmpb%d" % i, [128, 512], F32) for i in range(2)]
        R_tmpb = [S.res("tmpb") for _ in range(2)]
        o0 = sb(st, "o0", [128, 4, 256], F32)
        oo = sb(st, "oo", [128, 4, 256], F32)
        R_o0 = [S.res("o0") for _ in range(4)]
        R_oo = [S.res("oo") for _ in range(4)]
        onb = sb(st, "onb", [128, 4, 256], BF16)
        R_onb = [S.res("onb") for _ in range(4)]
        rr = sb(st, "rr", [128, 4, 8], F32)
        R_rr = [S.res("rr") for _ in range(4)]
        junk3 = sb(st, "junk3", [128, 256], BF16)
        R_junk3 = S.res("junk3")
        azt = [sb(st, "azt%d" % i, [128, 2, 512], BF16) for i in range(2)]
        R_azt = [S.res("azt") for _ in range(2)]
        ozs = [sb(st, "ozs%d" % i, [128, 2, 512], BF16) for i in range(2)]
        R_ozs = [S.res("ozs") for _ in range(2)]
        acc = [ps(st, "acc%d" % i, [128, 512], F32) for i in range(4)]
        R_acc = [S.res("acc%d" % i) for i in range(4)]
        psTa = ps(st, "psTa", [128, 1024], BF16)
        R_psTa = S.res("psTa")
        sctr = 0
        ptctr = 0
        tbctr = 0
        qtctr = 0
        hctr = 0
        groups = [([0, 1], 4096), ([2], 2048)]
        if ATT_LIM is not None:
            groups = groups[:ATT_LIM[0]]
        for (gsegs, L) in groups:
            nkb = L // 128
            for h in range(4 if ATT_LIM is None else ATT_LIM[1]):
                bi = hctr % 2
                hctr += 1
                for si, sg in enumerate(gsegs):
                    S.dma("sp", KT[bi][:, :, si * SEG:(si + 1) * SEG],
                          kT_d[sg, 2 * h:2 * h + 2, :, :].rearrange("c p t -> p c t"), [R_scr["kT"]], [R_KT[bi]], R_KT[bi])
                    S.dma("sp", QT[bi][:, :, si * SEG:(si + 1) * SEG],
                          qT_d[sg, 2 * h:2 * h + 2, :, :].rearrange("c p t -> p c t"), [R_scr["qT"]], [R_QT[bi]], R_QT[bi])
                    S.dma("sp", VX[bi][:, si * 16:(si + 1) * 16, 0:256],
                          v_d[sg, :, h * 256:(h + 1) * 256].rearrange("(kb p) e -> p kb e", p=128),
                          [R_scr["v"]], [R_VX[bi]], R_VX[bi])
                nqt = L // 512 if ATT_LIM is None else min(L // 512, ATT_LIM[2])
                iters = [(qt, c, kb) for qt in range(nqt) for c in range(2) for kb in range(nkb)]
                slots = {}
                qinfo = {}

                def start_qt(qt, gsegs=gsegs, h=h):
                    nonlocal qtctr
                    q0 = qt * 512
                    qseg = gsegs[q0 // SEG]
                    qtok = q0 % SEG
                    ai = qtctr % 2
                    qtctr += 1
                    S.dma("sp", azt[ai][:], azT_d[qseg, 2 * h:2 * h + 2, :, qtok:qtok + 512].rearrange("c p t -> p c t"),
                          [R_scr["azT"]], [R_azt[ai]], R_azt[ai])
                    qinfo[qt] = (q0, qseg, qtok, ai)

                def emit_qk(i, bi=bi, h=h):
                    nonlocal sctr
                    qt, c, kb = iters[i]
                    if c == 0 and kb == 0:
                        start_qt(qt)
                    q0 = qinfo[qt][0]
                    k0 = kb * 128
                    pi = sctr % 3
                    sctr += 1
                    slots[i] = pi
                    dk = k0 - q0
                    if -128 <= dk <= 512:
                        j0 = 640 - dk

                        def qkb(e):
                            e.matmul(psS[pi][:], lhsT=KT[bi][:, c, k0:k0 + 128], rhs=QT[bi][:, c, q0:q0 + 512],
                                     start=True, stop=False)
                            e.matmul(psS[pi][:], lhsT=identb[:], rhs=strips_hi[:, h, j0:j0 + 512], start=False, stop=False)
                            return e.matmul(psS[pi][:], lhsT=identb[:], rhs=strips_lo[:, h, j0:j0 + 512],
                                            start=False, stop=True)
                        S.op("pe", qkb, [R_KT[bi], R_QT[bi], R_strips], [R_psS[pi]])
                    else:
                        S.op("pe", lambda e: e.matmul(psS[pi][:], lhsT=KT[bi][:, c, k0:k0 + 128],
                                                      rhs=QT[bi][:, c, q0:q0 + 512], start=True, stop=True),
                             [R_KT[bi], R_QT[bi]], [R_psS[pi]])

                def finish_qt(qt, h=h):
                    (q0, qseg, qtok, ai) = qinfo.pop(qt)
                    for qs in range(4):
                        S.op("act", lambda e, qs=qs: e.activation(out=junk3[:], in_=oo[:, qs, :], func=AF.Square,
                                                                  accum_out=rr[:, qs, 3:4]), [R_oo[qs]], [R_junk3, R_rr[qs]])
                        S.op("dve", lambda e, qs=qs: e.tensor_scalar(out=rr[:, qs, 4:5], in0=rr[:, qs, 3:4],
                                                                     scalar1=1.0 / 256, scalar2=EPS, op0=ALU.mult,
                                                                     op1=ALU.add), [R_rr[qs]], [R_rr[qs]])
                        S.op("act", lambda e, qs=qs: e.activation(out=rr[:, qs, 4:5], in_=rr[:, qs, 4:5], func=AF.Sqrt),
                             [R_rr[qs]], [R_rr[qs]])
                        S.op("dve", lambda e, qs=qs: e.reciprocal(out=rr[:, qs, 5:6], in_=rr[:, qs, 4:5]),
                             [R_rr[qs]], [R_rr[qs]])
                        S.op("dve", lambda e, qs=qs: e.tensor_scalar(out=onb[:, qs, :], in0=oo[:, qs, :],
                                                                     scalar1=rr[:, qs, 5:6], scalar2=1.0 - LAM_INIT,
                                                                     op0=ALU.mult, op1=ALU.mult),
                             [R_oo[qs], R_rr[qs]], [R_onb[qs]])
                    oi = ai
                    for ec in range(2):
                        def tr(e, ec=ec):
                            ins = None
                            for qs in range(4):
                                ins = e.transpose(out=psTa[:, qs * 128:(qs + 1) * 128],
                                                  in_=onb[:, qs, ec * 128:(ec + 1) * 128], identity=identb[:])
                            return ins
                        S.op("pe", tr, R_onb, [R_psTa])
                        S.op("dve", lambda e, ec=ec: e.scalar_tensor_tensor(
                            out=ozs[oi][:, ec, :], in0=psTa[:, 0:512], scalar=subg[:, ec:ec + 1], in1=azt[ai][:, ec, :],
                            op0=ALU.mult, op1=ALU.mult), [R_psTa, R_azt[ai]], [R_ozs[oi]])
                    S.dma("pool", ozT_d[qseg, 2 * h:2 * h + 2, :, qtok:qtok + 512].rearrange("c p t -> p c t"),
                          ozs[oi][:], [R_ozs[oi]], [R_scr["ozT"]], R_ozs[oi])

                def emit_rest(i, bi=bi, h=h, nkb=nkb, gsegs=gsegs):
                    nonlocal ptctr, tbctr
                    qt, c, kb = iters[i]
                    q0 = qinfo[qt][0]
                    k0 = kb * 128
                    dk = k0 - q0
                    cross = (len(gsegs) == 2) and ((k0 // SEG) != (q0 // SEG))
                    near = (-128 <= dk <= 512)
                    pi = slots.pop(i)
                    pti = ptctr % 3
                    ptctr += 1
                    if near:
                        if cross:
                            S.op("act", lambda e: e.activation(out=PT[pti][:], in_=psS[pi][:], func=AF.Exp,
                                                               bias=flags[:, 1:2]), [R_psS[pi]], [R_PT[pti]])
                        else:
                            S.op("act", lambda e: e.activation(out=PT[pti][:], in_=psS[pi][:], func=AF.Exp),
                                 [R_psS[pi]], [R_PT[pti]])
                    else:
                        col = (1 if dk > 0 else 0) + (2 if cross else 0)
                        S.op("act", lambda e: e.activation(out=PT[pti][:], in_=psS[pi][:], func=AF.Exp,
                                                           bias=biasc[:, h, col:col + 1]),
                             [R_psS[pi], R_biasc], [R_PT[pti]])
                    for qs in range(4):
                        S.op("pe", lambda e, qs=qs: e.matmul(
                            acc[qs][:, 0:257], lhsT=PT[pti][:, qs * 128:(qs + 1) * 128], rhs=VX[bi][:, kb, :],
                            start=(kb == 0), stop=(kb == nkb - 1)), [R_PT[pti], R_VX[bi]], [R_acc[qs]])
                    if kb == nkb - 1:
                        for qs in range(4):
                            S.op("dve", lambda e, qs=qs: e.reciprocal(out=rr[:, qs, c:c + 1], in_=acc[qs][:, 256:257]),
                                 [R_acc[qs]], [R_rr[qs]])
                            if c == 0:
                                S.op("dve", lambda e, qs=qs: e.tensor_scalar(
                                    out=o0[:, qs, :], in0=acc[qs][:, 0:256], scalar1=rr[:, qs, 0:1], scalar2=None,
                                    op0=ALU.mult), [R_acc[qs], R_rr[qs]], [R_o0[qs]])
                            else:
                                S.op("dve", lambda e, qs=qs: e.tensor_tensor(
                                    out=rr[:, qs, 2:3], in0=rr[:, qs, 1:2], in1=lamt[:, 1:2], op=ALU.mult),
                                    [R_rr[qs]], [R_rr[qs]])
                                S.op("dve", lambda e, qs=qs: e.scalar_tensor_tensor(
                                    out=oo[:, qs, :], in0=acc[qs][:, 0:256], scalar=rr[:, qs, 2:3], in1=o0[:, qs, :],
                                    op0=ALU.mult, op1=ALU.add), [R_acc[qs], R_rr[qs], R_o0[qs]], [R_oo[qs]])
                        if c == 1:
                            finish_qt(qt)
                LA = 2
                for i in range(len(iters) + LA):
                    if i < len(iters):
                        emit_qk(i)
                    if i >= LA:
                        emit_rest(i - LA)
                    yield


    if not DBG_S5_IDENT:
        TWO_PI = 2.0 * math.pi
        MAGIC = 12582912.0
        NCB = 32
        Ud = dscr("Ud", [NSEG, NCB, 128, 64 * 8])
        Zd = dscr("Zd", [NSEG, 2, NCB, 128, 64 * 8])
        Pd = dscr("Pd", [NSEG, 2, NCB, 128, 64 * 8], F32)
        R_Pd = [[S.res("Pd") for _ in range(2)] for _ in range(NSEG)]
        R_Ud = S.res("Ud")
        R_Zd = [[S.res("Zd") for _ in range(2)] for _ in range(NSEG)]
        with ExitStack() as st5:
            Ym = sb(st5, "Ym", [128, 128, 128], BF16)
            Mall = sb(st5, "Mall", [128, 64, 128], BF16)
            Ar = sb(st5, "Ar", [128, 128], F32)
            Aisw = sb(st5, "Aisw", [128, 128], F32)
            stX = ExitStack()
            Xm = sb(stX, "Xm", [128, 64, 2, 128], BF16)
            R_Xm, R_Ym, R_M, R_A = S.res("Xm"), S.res("Ym"), S.res("Mall"), S.res("A")
            with ExitStack() as st:
                def ld(name, src_ap, shape, dt=F32):
                    t = sb(st, name, shape, dt)
                    r = S.res(name)
                    S.dma("sp", t[:], src_ap, [], [r], r)
                    return t, r
                lr2, R_lr = ld("lr2", s5_lr_in[:, :], [128, 128])
                li2, R_li = ld("li2", s5_li_in[:, :], [128, 128])
                ldt2, R_ldt = ld("ldt2", s5_ldt_in[:, :], [128, 128])
                dcol, R_dcol = ld("dcol", s5_dcol_in[:, :], [128, 64])
                mkL, R_mkL = ld("mkL", s5_maskl_in[:, :], [128, 128])
                mkU, R_mkU = ld("mkU", s5_masku_in[:, :], [128, 128])
                xr = sb(st, "xr", [128, 128], F32)
                th = sb(st, "th", [128, 128], F32)
                R_xr = S.res("xr")
                S.op("act", lambda e: e.activation(out=xr[:], in_=ldt2[:], func=AF.Exp), [R_ldt], [R_xr])
                S.op("dve", lambda e: e.tensor_tensor(out=th[:], in0=li2[:], in1=xr[:], op=ALU.mult), [R_li, R_xr], [R_xr])
                S.op("dve", lambda e: e.tensor_tensor(out=xr[:], in0=lr2[:], in1=xr[:], op=ALU.mult), [R_lr, R_xr], [R_xr])

                pwtmp = {}

                def pw(stk, tag, F, kt, thb, xrb, R_k):
                    shp = [128] + F
                    if len(F) not in pwtmp:
                        pwtmp[len(F)] = (sb(st, "pwang%d" % len(F), shp, F32), sb(st, "pwt%d" % len(F), shp, F32),
                                         S.res("pwtmp"))
                    ang, t, Rt = pwtmp[len(F)]
                    pr = sb(stk, tag + "pr", shp, F32)
                    pi = sb(stk, tag + "pi", shp, F32)
                    R = S.res(tag)
                    S.op("dve", lambda e: e.tensor_tensor(out=ang[:], in0=kt, in1=thb, op=ALU.mult), [R_k, R_xr], [Rt])
                    S.op("dve", lambda e: e.tensor_scalar(out=t[:], in0=ang[:], scalar1=1.0 / TWO_PI, scalar2=MAGIC,
                                                          op0=ALU.mult, op1=ALU.add), [Rt], [Rt])
                    S.op("dve", lambda e: e.tensor_scalar(out=t[:], in0=t[:], scalar1=-MAGIC, scalar2=-TWO_PI,
                                                          op0=ALU.add, op1=ALU.mult), [Rt], [Rt])
                    S.op("dve", lambda e: e.tensor_tensor(out=ang[:], in0=ang[:], in1=t[:], op=ALU.add), [Rt], [Rt])
                    S.op("act", lambda e: e.activation(out=pi[:], in_=ang[:], func=AF.Sin), [Rt], [R])
                    S.op("dve", lambda e: e.tensor_scalar(out=t[:], in0=ang[:], scalar1=-1.0, scalar2=None, op0=ALU.mult),
                         [Rt], [Rt])
                    S.op("dve", lambda e: e.tensor_tensor(out=t[:], in0=t[:], in1=ang[:], op=ALU.max), [Rt], [Rt])
                    S.op("dve", lambda e: e.tensor_scalar(out=t[:], in0=t[:], scalar1=-1.0, scalar2=math.pi / 2,
                                                          op0=ALU.mult, op1=ALU.add), [Rt], [Rt])
                    S.op("act", lambda e: e.activation(out=pr[:], in_=t[:], func=AF.Sin), [Rt], [R])
                    S.op("dve", lambda e: e.tensor_tensor(out=t[:], in0=kt, in1=xrb, op=ALU.mult), [R_k, R_xr], [Rt])
                    S.op("act", lambda e: e.activation(out=t[:], in_=t[:], func=AF.Exp), [Rt], [Rt])
                    S.op("dve", lambda e: e.tensor_tensor(out=pr[:], in0=pr[:], in1=t[:], op=ALU.mult), [R, Rt], [R])
                    S.op("dve", lambda e: e.tensor_tensor(out=pi[:], in0=pi[:], in1=t[:], op=ALU.mult), [R, Rt], [R])
                    return pr, pi, R

                k1 = sb(st, "k1", [128, 128], F32)
                k8 = sb(st, "k8", [128, 128], F32)
                R_k18 = S.res("k18")
                S.op("dve", lambda e: e.memset(k1[:], 1.0), [], [R_k18])
                S.op("dve", lambda e: e.memset(k8[:], 8.0), [], [R_k18])
                p1r, p1i, R_p1 = pw(st, "p1", [128], k1[:], th[:], xr[:], R_k18)
                p8r, p8i, R_p8 = pw(st, "p8", [128], k8[:], th[:], xr[:], R_k18)
                S.op("dve", lambda e: e.tensor_copy(out=Ar[:], in_=p8r[:]), [R_p8], [R_A])
                S.op("dve", lambda e: e.tensor_copy(out=Aisw[0:64, :], in_=p8i[0:64, :]), [R_p8], [R_A])
                S.op("dve", lambda e: e.tensor_scalar(out=Aisw[64:128, :], in0=p8i[64:128, :], scalar1=-1.0, scalar2=None,
                                                      op0=ALU.mult), [R_p8], [R_A])
                kr = sb(st, "kr", [128, 128], F32)
                ki = sb(st, "ki", [128, 128], F32)
                den = sb(st, "den", [128, 128], F32)
                tq = sb(st, "tq", [128, 128], F32)
                R_kap = S.res("kap")
                S.op("dve", lambda e: e.tensor_scalar(out=p1r[:], in0=p1r[:], scalar1=-1.0, scalar2=None, op0=ALU.add),
                     [R_p1], [R_p1])
                S.op("dve", lambda e: e.tensor_tensor(out=den[:], in0=lr2[:], in1=lr2[:], op=ALU.mult), [R_lr], [R_kap])
                S.op("dve", lambda e: e.tensor_tensor(out=tq[:], in0=li2[:], in1=li2[:], op=ALU.mult), [R_li], [R_kap])
                S.op("dve", lambda e: e.tensor_tensor(out=den[:], in0=den[:], in1=tq[:], op=ALU.add), [R_kap], [R_kap])
                S.op("dve", lambda e: e.reciprocal(out=den[:], in_=den[:]), [R_kap], [R_kap])
                S.op("dve", lambda e: e.tensor_tensor(out=kr[:], in0=p1r[:], in1=lr2[:], op=ALU.mult), [R_p1, R_lr], [R_kap])
                S.op("dve", lambda e: e.tensor_tensor(out=tq[:], in0=p1i[:], in1=li2[:], op=ALU.mult), [R_p1, R_li], [R_kap])
                S.op("dve", lambda e: e.tensor_tensor(out=kr[:], in0=kr[:], in1=tq[:], op=ALU.add), [R_kap], [R_kap])
                S.op("dve", lambda e: e.tensor_tensor(out=kr[:], in0=kr[:], in1=den[:], op=ALU.mult), [R_kap], [R_kap])
                S.op("dve", lambda e: e.tensor_tensor(out=ki[:], in0=p1i[:], in1=lr2[:], op=ALU.mult), [R_p1, R_lr], [R_kap])
                S.op("dve", lambda e: e.tensor_tensor(out=tq[:], in0=p1r[:], in1=li2[:], op=ALU.mult), [R_p1, R_li], [R_kap])
                S.op("dve", lambda e: e.tensor_tensor(out=ki[:], in0=ki[:], in1=tq[:], op=ALU.subtract), [R_kap], [R_kap])
                S.op("dve", lambda e: e.tensor_tensor(out=ki[:], in0=ki[:], in1=den[:], op=ALU.mult), [R_kap], [R_kap])
                B8 = [128, 128, 8]
                thb = th[:].unsqueeze(2).to_broadcast(B8)
                xrb = xr[:].unsqueeze(2).to_broadcast(B8)
                krb = kr[:].unsqueeze(2).to_broadcast(B8)
                kib = ki[:].unsqueeze(2).to_broadcast(B8)
                XT = sb(st, "XT", [128, 128, 128], BF16)
                R_XT = S.res("XT")
                t1 = sb(st, "g_t1", [128, 16, 8, 16], F32)
                t2 = sb(st, "g_t2", [128, 16, 8, 16], F32)
                R_t12 = S.res("t12")
                SH = [128, 16, 8, 16]

                def ld2(stk, name, src_ap, shape):
                    t = sb(stk, name, shape, F32)
                    r = S.res(name)
                    S.dma("sp", t[:], src_ap, [], [r], r)
                    return t, r
                pwtmp[2] = (sb(st, "pwang2", B8, F32), sb(st, "pwt2", B8, F32), S.res("pwtmp"))
                for part in ("X", "Y"):
                    with ExitStack() as sx:
                        if part == "X":
                            TA, R_TA = ld2(sx, "BA", s5_ba_in[:, :, :], [128, 128, 16])
                            TB, R_TB = ld2(sx, "BB", s5_bb_in[:, :, :], [128, 128, 16])
                            kT, R_kT = ld2(sx, "kX", s5_kx_in[:, :, :], [128, 128, 8])
                            pr_, pi_, R_p = pw(sx, "px", [128, 8], kT[:], thb, xrb, R_kT)
                            ar_ = sb(sx, "xir", B8, F32)
                            ai_ = sb(sx, "xii", B8, F32)
                            tx = sb(sx, "tx", B8, F32)
                            R_ar = S.res("xi")
                            S.op("dve", lambda e: e.tensor_tensor(out=ar_[:], in0=pr_[:], in1=krb, op=ALU.mult), [R_p, R_kap], [R_ar])
                            S.op("dve", lambda e: e.tensor_tensor(out=tx[:], in0=pi_[:], in1=kib, op=ALU.mult), [R_p, R_kap], [R_ar])
                            S.op("dve", lambda e: e.tensor_tensor(out=ar_[:], in0=ar_[:], in1=tx[:], op=ALU.subtract), [R_ar], [R_ar])
                            S.op("dve", lambda e: e.tensor_tensor(out=ai_[:], in0=pr_[:], in1=kib, op=ALU.mult), [R_p, R_kap], [R_ar])
                            S.op("dve", lambda e: e.tensor_tensor(out=tx[:], in0=pi_[:], in1=krb, op=ALU.mult), [R_p, R_kap], [R_ar])
                            S.op("dve", lambda e: e.tensor_tensor(out=ai_[:], in0=ai_[:], in1=tx[:], op=ALU.add), [R_ar], [R_ar])
                            outT, R_o, upper_neg = XT, R_XT, False
                        else:
                            TA, R_TA = ld2(sx, "CA", s5_ca_in[:, :, :], [128, 128, 16])
                            TB, R_TB = ld2(sx, "CB", s5_cb_in[:, :, :], [128, 128, 16])
                            kT, R_kT = ld2(sx, "kY", s5_ky_in[:, :, :], [128, 128, 8])
                            ar_, ai_, R_ar = pw(sx, "py", [128, 8], kT[:], thb, xrb, R_kT)
                            outT, R_o, upper_neg = Ym, R_Ym, True
                        for cch in range(8):
                            sl = slice(cch * 16, (cch + 1) * 16)
                            a_r = ar_[:, sl, :].unsqueeze(3).to_broadcast(SH)
                            a_i = ai_[:, sl, :].unsqueeze(3).to_broadcast(SH)
                            b_a = TA[:, sl, :].unsqueeze(2).to_broadcast(SH)
                            b_b = TB[:, sl, :].unsqueeze(2).to_broadcast(SH)
                            S.op("dve", lambda e, a_r=a_r, b_a=b_a: e.tensor_tensor(out=t1[:], in0=a_r, in1=b_a, op=ALU.mult),
                                 [R_ar, R_TA], [R_t12])
                            S.op("dve", lambda e, a_i=a_i, b_b=b_b: e.tensor_tensor(out=t2[:], in0=a_i, in1=b_b, op=ALU.mult),
                                 [R_ar, R_TB], [R_t12])
                            ov = outT[:, sl, :].rearrange("p a (j c) -> p a j c", j=8)
                            S.op("dve", lambda e, ov=ov: e.tensor_tensor(out=ov[0:64], in0=t1[0:64], in1=t2[0:64],
                                                                         op=ALU.subtract), [R_t12], [R_o])
                            if upper_neg:
                                S.op("dve", lambda e, sl=sl: e.scalar_tensor_tensor(
                                    out=outT[64:128, sl, :], in0=t1[64:128].rearrange("p a j c -> p a (j c)"), scalar=-1.0,
                                    in1=t2[64:128].rearrange("p a j c -> p a (j c)"), op0=ALU.mult, op1=ALU.subtract),
                                    [R_t12], [R_o])
                            else:
                                S.op("dve", lambda e, ov=ov: e.tensor_tensor(out=ov[64:128], in0=t1[64:128], in1=t2[64:128],
                                                                             op=ALU.add), [R_t12], [R_o])
                        S.barrier()
                psx = [ps(st, "psx%d" % i, [128, 1024], BF16) for i in range(2)]
                R_psx = [S.res("psx") for _ in range(2)]
                psm = [ps(st, "psm%d" % i, [128, 512], F32) for i in range(4)]
                R_psm = [S.res("psm") for _ in range(4)]
                mt = [sb(st, "mt%d" % i, [128, 128], F32) for i in range(2)]
                R_mt = [S.res("mt") for _ in range(2)]
                for g4 in range(16):
                    bi = g4 % 2

                    def trx(e, g4=g4, bi=bi):
                        ins = None
                        for gg in range(4):
                            for d in range(2):
                                ins = e.transpose(out=psx[bi][:, (gg * 2 + d) * 128:(gg * 2 + d + 1) * 128],
                                                  in_=XT[:, d * 64 + g4 * 4 + gg, :], identity=identb[:])
                        return ins
                    S.op("pe", trx, [R_XT], [R_psx[bi]])
                    S.op("act", lambda e, g4=g4, bi=bi: e.activation(
                        out=Xm[:, g4 * 4:(g4 + 1) * 4, :, :].rearrange("p g d m -> p (g d m)"), in_=psx[bi][:], func=AF.Copy),
                        [R_psx[bi]], [R_Xm])
                for g in range(64):
                    pf = (2 * g) % 4
                    pb = (2 * g + 1) % 4
                    S.op("pe", lambda e, g=g, pf=pf: e.matmul(psm[pf][:, 0:128], lhsT=XT[:, g, :], rhs=Ym[:, g, :],
                                                              start=True, stop=True), [R_XT, R_Ym], [R_psm[pf]])
                    S.op("pe", lambda e, g=g, pb=pb: e.matmul(psm[pb][:, 0:128], lhsT=XT[:, 64 + g, :], rhs=Ym[:, 64 + g, :],
                                                              start=True, stop=True), [R_XT, R_Ym], [R_psm[pb]])
                    S.op("dve", lambda e, pf=pf: e.tensor_tensor(out=mt[0][:], in0=psm[pf][:, 0:128], in1=mkL[:], op=ALU.mult),
                         [R_psm[pf], R_mkL], [R_mt[0]])
                    S.op("dve", lambda e, pb=pb: e.tensor_tensor(out=mt[1][:], in0=psm[pb][:, 0:128], in1=mkU[:], op=ALU.mult),
                         [R_psm[pb], R_mkU], [R_mt[1]])
                    S.op("dve", lambda e: e.tensor_tensor(out=mt[0][:], in0=mt[0][:], in1=mt[1][:], op=ALU.add),
                         [R_mt[0], R_mt[1]], [R_mt[0]])
                    S.op("dve", lambda e, g=g: e.scalar_tensor_tensor(out=Mall[:, g, :], in0=identf[:], scalar=dcol[:, g:g + 1],
                                                                      in1=mt[0][:], op0=ALU.mult, op1=ALU.add),
                         [R_mt[0], R_dcol], [R_M])
                S.barrier()
            with ExitStack() as st:
                Uc = [sb(st, "Uc%d" % i, [128, 8 * 1024], BF16) for i in range(1)] * 2
                R_Uc = [S.res("Uc")] * 2
                Ucr = [sb(st, "Ucr%d" % i, [128, 64, 128], BF16) for i in range(1)] * 2
                R_Ucr = [S.res("Ucr")] * 2
                Pst = sb(st, "Pst", [128, 16, 64, 8], F32)
                R_Pst = S.res("Pst")
                psP = [ps(st, "psP%d" % i, [128, 512], F32) for i in range(2)]
                R_psP = [S.res("psP") for _ in range(2)]
                ppc = 0
                Ublk = [sb(st, "Ublk%d" % i, [128, 16, 64, 8], BF16) for i in range(2)]
                R_Ublk = [S.res("Ublk") for _ in range(2)]
                psu = [ps(st, "psu%d" % i, [128, 1024], BF16) for i in range(2)]
                R_psu = [S.res("psu") for _ in range(2)]
                it = 0
                pc = 0
                for s in range(NSEG):
                    for ct in range(2):
                        b = it % 2
                        it += 1
                        S.dma("sp", Uc[b][:], u_d[s, ct * 1024:(ct + 1) * 1024, :].rearrange("(p j) f -> p (j f)", j=8),
                              [R_scr["u"]], [R_Uc[b]], R_Uc[b])
                        for hf in range(2):
                            S.op("dve" if hf else "act", (lambda e, b=b, hf=hf: e.tensor_copy(
                                out=Ucr[b][:, hf * 32:(hf + 1) * 32, :].rearrange("p g (j c) -> p g j c", j=8),
                                in_=Uc[b][:].rearrange("p (j g c) -> p g j c", j=8, g=64)[:, hf * 32:(hf + 1) * 32]))
                                if hf else (lambda e, b=b, hf=hf: e.activation(
                                    out=Ucr[b][:, hf * 32:(hf + 1) * 32, :].rearrange("p g (j c) -> p g j c", j=8),
                                    in_=Uc[b][:].rearrange("p (j g c) -> p g j c", j=8, g=64)[:, hf * 32:(hf + 1) * 32],
                                    func=AF.Copy)), [R_Uc[b]], [R_Ucr[b]])
                        for g8 in range(8):
                            pi_ = pc % 2
                            pc += 1

                            def tru(e, g8=g8, pi_=pi_, b=b):
                                ins = None
                                for gg in range(8):
                                    g = g8 * 8 + gg
                                    ins = e.transpose(out=psu[pi_][:, gg * 128:(gg + 1) * 128],
                                                      in_=Ucr[b][:, g, :], identity=identb[:])
                                return ins
                            S.op("pe", tru, [R_Ucr[b]], [R_psu[pi_]])
                            S.op("act" if g8 % 2 else "dve", lambda e, g8=g8, pi_=pi_, b=b: e.tensor_copy(
                                out=Ublk[b][:, :, g8 * 8:(g8 + 1) * 8, :].rearrange("p cb g c -> p g cb c"),
                                in_=psu[pi_][:].rearrange("p (g cb c) -> p g cb c", g=8, cb=16)) if g8 % 2 == 0 else
                                e.activation(out=Ublk[b][:, :, g8 * 8:(g8 + 1) * 8, :].rearrange("p cb g c -> p g cb c"),
                                             in_=psu[pi_][:].rearrange("p (g cb c) -> p g cb c", g=8, cb=16), func=AF.Copy),
                                [R_psu[pi_]], [R_Ublk[b]])
                        S.dma("pool", Ud[s, ct * 16:(ct + 1) * 16, :, :].rearrange("cb p f -> p cb f"),
                              Ublk[b][:].rearrange("p cb g c -> p cb (g c)"), [R_Ublk[b]], [R_Ud], R_Ublk[b])
                        for d in range(2):
                            for g4 in range(16):
                                pq = ppc % 2
                                ppc += 1

                                def mmP(e, d=d, g4=g4, pq=pq, b=b):
                                    ins = None
                                    for gi in range(4):
                                        ins = e.matmul(psP[pq][:, gi * 128:(gi + 1) * 128], lhsT=Xm[:, g4 * 4 + gi, d, :],
                                                       rhs=Ublk[b][:, :, g4 * 4 + gi, :], start=True, stop=True)
                                    return ins
                                S.op("pe", mmP, [R_Xm, R_Ublk[b]], [R_psP[pq]])
                                if g4 % 2 == 0:
                                    S.op("dve", lambda e, g4=g4, pq=pq: e.tensor_copy(
                                        out=Pst[:, :, g4 * 4:(g4 + 1) * 4, :].rearrange("p cb g c -> p g cb c"),
                                        in_=psP[pq][:].rearrange("p (g cb c) -> p g cb c", g=4, cb=16)),
                                        [R_psP[pq]], [R_Pst])
                                else:
                                    S.op("act", lambda e, g4=g4, pq=pq: e.activation(
                                        out=Pst[:, :, g4 * 4:(g4 + 1) * 4, :].rearrange("p cb g c -> p g cb c"),
                                        in_=psP[pq][:].rearrange("p (g cb c) -> p g cb c", g=4, cb=16), func=AF.Copy),
                                        [R_psP[pq]], [R_Pst])
                            S.dma("pool", Pd[s, d, ct * 16:(ct + 1) * 16, :, :].rearrange("cb p f -> p cb f"),
                                  Pst[:].rearrange("p cb g c -> p cb (g c)"), [R_Pst], [R_Pd[s][d]], R_Pst)
                S.barrier()
            def sweep_phase(st):
                Z = [[sb(st, "Zst%d%d" % (d, i), [128, 2, 64], F32) for i in range(2)] for d in range(2)]
                W = [sb(st, "Wst%d" % d, [128, 2, 64], F32) for d in range(2)]
                T1 = [sb(st, "T1st%d" % d, [128, 2, 64], F32) for d in range(2)]
                T2 = [sb(st, "T2st%d" % d, [128, 2, 64], F32) for d in range(2)]
                R_Z = [[S.res("Zst") for _ in range(2)] for _ in range(2)]
                R_W = [S.res("Wst") for _ in range(2)]
                R_T1 = [S.res("T1st") for _ in range(2)]
                R_T2 = [S.res("T2st") for _ in range(2)]
                Pb = [[sb(st, "Pb%d%d" % (d, i), [128, 2, 64, 8], F32) for i in range(2)] for d in range(2)]
                R_Pb = [[S.res("Pb") for _ in range(2)] for _ in range(2)]
                Zb = [[[sb(st, "Zb%d%d%d" % (d, k, i), [128, 64, 8], BF16) for i in range(2)] for k in range(2)] for d in range(2)]
                R_Zb = [[[S.res("Zbk") for _ in range(2)] for _ in range(2)] for _ in range(2)]
                stages = [([0, 2], [1, 2]), ([1], [0])]
                pp = 0
                for sti, (fsegs, bsegs) in enumerate(stages):
                    segs_d = [fsegs, bsegs]
                    for d in range(2):
                        if sti == 0:
                            S.op("dve", lambda e, d=d: e.memset(Z[d][pp][:], 0.0), [], [R_Z[d][pp]])
                        else:
                            S.op("dve", lambda e, d=d, pp=pp: e.tensor_scalar(
                                out=Z[d][pp][:, 0, :], in0=Z[d][pp][:, 0, :], scalar1=flags[:, 0:1], scalar2=None,
                                op0=ALU.mult), [R_Z[d][pp]], [R_Z[d][pp]])
                    for tb in range(NCB):
                        i2 = tb % 2
                        for d in range(2):
                            cb = tb if d == 0 else NCB - 1 - tb
                            for k, sg in enumerate(segs_d[d]):
                                S.dma("pool", Pb[d][i2][:, k, :, :].rearrange("p g c -> p (g c)"), Pd[sg, d, cb, :, :],
                                      [R_Pd[sg][d]], [R_Pb[d][i2]], R_Pb[d][i2])
                        for cc_ in range(8):
                            nx = 1 - pp
                            for d in range(2):
                                cc = cc_ if d == 0 else 7 - cc_
                                for k in range(len(segs_d[d])):
                                    S.op("pool", lambda e, d=d, k=k, cc=cc, pp=pp: e.tensor_copy(
                                        out=Zb[d][k][i2][:, :, cc], in_=Z[d][pp][:, k, :]),
                                        [R_Z[d][pp]], [R_Zb[d][k][i2]])
                            for d in range(2):
                                cc = cc_ if d == 0 else 7 - cc_
                                nk = len(segs_d[d])
                                S.op("dve", lambda e, d=d, cc=cc, nk=nk, pp=pp: e.tensor_tensor(
                                    out=W[d][:, 0:nk, :], in0=Z[d][pp][:, 0:nk, :], in1=Pb[d][i2][:, 0:nk, :, cc], op=ALU.add),
                                    [R_Z[d][pp], R_Pb[d][i2]], [R_W[d]], nosame=True)
                            yield
                            for d in range(2):
                                nk = len(segs_d[d])
                                S.op("dve", lambda e, d=d, nk=nk: e.tensor_tensor(
                                    out=T1[d][:, 0:nk, :], in0=W[d][:, 0:nk, :],
                                    in1=Ar[:, d * 64:(d + 1) * 64].unsqueeze(1).to_broadcast([128, nk, 64]), op=ALU.mult),
                                    [R_W[d], R_A], [R_T1[d]], nosame=True)
                            yield
                            for d in range(2):
                                nk = len(segs_d[d])
                                S.op("dve", lambda e, d=d, nk=nk: e.tensor_tensor(
                                    out=T2[d][0:64, 0:nk, :], in0=W[d][64:128, 0:nk, :],
                                    in1=Aisw[64:128, d * 64:(d + 1) * 64].unsqueeze(1).to_broadcast([64, nk, 64]), op=ALU.mult),
                                    [R_W[d], R_A], [R_T2[d]], nosame=True)
                            yield
                            for d in range(2):
                                nk = len(segs_d[d])
                                S.op("dve", lambda e, d=d, nk=nk: e.tensor_tensor(
                                    out=T2[d][64:128, 0:nk, :], in0=W[d][0:64, 0:nk, :],
                                    in1=Aisw[0:64, d * 64:(d + 1) * 64].unsqueeze(1).to_broadcast([64, nk, 64]), op=ALU.mult),
                                    [R_W[d], R_A], [R_T2[d]], nosame=True)
                            yield
                            for d in range(2):
                                nk = len(segs_d[d])
                                S.op("dve", lambda e, d=d, nk=nk, nx=nx: e.tensor_tensor(
                                    out=Z[d][nx][:, 0:nk, :], in0=T1[d][:, 0:nk, :], in1=T2[d][:, 0:nk, :], op=ALU.add),
                                    [R_T1[d], R_T2[d]], [R_Z[d][nx]], nosame=True)
                            pp = nx
                            yield
                        for d in range(2):
                            cb = tb if d == 0 else NCB - 1 - tb
                            for k, sg in enumerate(segs_d[d]):
                                S.dma("pool", Zd[sg, d, cb, :, :], Zb[d][k][i2][:].rearrange("p g c -> p (g c)"),
                                      [R_Zb[d][k][i2]], [R_Zd[sg][d]], R_Zb[d][k][i2])


            stX.close()
            with ExitStack() as stc:
                ga = att_phase(stc)
                gs = sweep_phase(stc)
                done_a = done_s = False
                while not (done_a and done_s):
                    if not done_a:
                        for _ in range(ATT_PER_SWEEP):
                            try:
                                next(ga)
                            except StopIteration:
                                done_a = True
                                break
                    if not done_s:
                        try:
                            next(gs)
                        except StopIteration:
                            done_s = True
                S.barrier()
            with ExitStack() as st:
                sel = sb(st, "sel", [128, 64, 128], BF16)
                R_sel = S.res("sel")
                S.dma("pool", sel[:], s5_sel_in[:, :, :], [], [R_sel], R_sel)
                Uo = [sb(st, "Uo%d" % i, [128, 8, 64, 8], BF16) for i in range(2)]
                Zo = [[sb(st, "Zo%d%d" % (d, i), [128, 8, 64, 8], BF16) for i in range(2)] for d in range(2)]
                R_Uo = [S.res("Uo") for _ in range(2)]
                R_Zo = [[S.res("Zo") for _ in range(2)] for _ in range(2)]
                Yg = sb(st, "Yg", [128, 64, 64], BF16)
                R_Yg = [S.res("Yg") for _ in range(8)]
                ygS = [sb(st, "ygS%d" % i, [128, 8, 512], BF16) for i in range(2)]
                R_ygS = [S.res("ygS") for _ in range(2)]
                psy = [ps(st, "psy%d" % i, [128, 512], F32) for i in range(3)]
                R_psy = [S.res("psy") for _ in range(3)]
                pss = [ps(st, "pss%d" % i, [128, 512], F32) for i in range(3)]
                R_pss = [S.res("pss") for _ in range(3)]
                it = 0
                yc = 0
                sc = 0
                for s in range(NSEG):
                    for blk in range(4):
                        b = it % 2
                        it += 1
                        S.dma("sp", Uo[b][:].rearrange("p cb g c -> p cb (g c)"),
                              Ud[s, blk * 8:(blk + 1) * 8, :, :].rearrange("cb p f -> p cb f"), [R_Ud], [R_Uo[b]], R_Uo[b])
                        for d in range(2):
                            S.dma("sp", Zo[d][b][:].rearrange("p cb g c -> p cb (g c)"),
                                  Zd[s, d, blk * 8:(blk + 1) * 8, :, :].rearrange("cb p f -> p cb f"), [R_Zd[s][d]],
                                  [R_Zo[d][b]], R_Zo[d][b])
                        for o in range(8):
                            pi_ = yc % 3
                            yc += 1

                            def mmy(e, o=o, pi_=pi_, b=b):
                                ins = None
                                for gg in range(8):
                                    g = o * 8 + gg
                                    outp = psy[pi_][:, gg * 64:(gg + 1) * 64]
                                    e.matmul(outp, lhsT=Mall[:, g, :], rhs=Uo[b][:, :, g, :], start=True, stop=False)
                                    e.matmul(outp, lhsT=Ym[:, g, :], rhs=Zo[0][b][:, :, g, :], start=False, stop=False)
                                    ins = e.matmul(outp, lhsT=Ym[:, 64 + g, :], rhs=Zo[1][b][:, :, g, :], start=False, stop=True)
                                return ins
                            S.op("pe", mmy, [R_M, R_Ym, R_Uo[b], R_Zo[0][b], R_Zo[1][b]], [R_psy[pi_]])
                            S.op("act", lambda e, o=o, pi_=pi_: e.activation(
                                out=Yg[:, o * 8:(o + 1) * 8, :].rearrange("p g c -> p (g c)"), in_=psy[pi_][:],
                                func=AF.Gelu_apprx_tanh), [R_psy[pi_]], [R_Yg[o]])
                        for o in range(8):
                            si = sc % 3
                            sc += 1

                            def mms(e, o=o, si=si):
                                ins = None
                                for j in range(8):
                                    for gg in range(8):
                                        ins = e.matmul(pss[si][:, j * 64:(j + 1) * 64], lhsT=sel[:, j * 8 + gg, :],
                                                       rhs=Yg[:, o * 8 + gg, :], start=(gg == 0), stop=(gg == 7))
                                return ins
                            S.op("pe", mms, [R_sel, R_Yg[o]], [R_pss[si]])
                            S.op("dve", lambda e, o=o, si=si, b=b: e.tensor_copy(
                                out=ygS[b][:, o, :].rearrange("p (c j) -> p j c", j=8),
                                in_=pss[si][:].rearrange("p (j c) -> p j c", j=8)), [R_pss[si]], [R_ygS[b]])
                        S.dma("pool", ygT_d[s, :, :, blk * 512:(blk + 1) * 512].rearrange("c p t -> p c t"), ygS[b][:],
                              [R_ygS[b]], [R_scr["ygT"]], R_ygS[b])
                S.barrier()
    if DBG_S5_IDENT:
        with ExitStack() as st:
            ub = sb(st, "ub", [128, 1024], BF16)
            R_ub = S.res("ub")
            ut = sb(st, "ut", [128, 8, 128], BF16)
            R_ut = S.res("ut")
            pst = ps(st, "pst", [128, 1024], BF16)
            R_pst = S.res("pst")
            for s in range(NSEG):
                for tb in range(SEG // 128):
                    S.dma("sp", ub[:], u_d[s, tb * 128:(tb + 1) * 128, :], [R_scr["u"]], [R_ub], R_ub)

                    def tr(e):
                        ins = None
                        for c in range(8):
                            ins = e.transpose(out=pst[:, c * 128:(c + 1) * 128], in_=ub[:, c * 128:(c + 1) * 128],
                                              identity=identb[:])
                        return ins
                    S.op("pe", tr, [R_ub], [R_pst])
                    S.op("act", lambda e: e.activation(out=ut[:], in_=pst[:].rearrange("p (c t) -> p c t", c=8),
                                                       func=AF.Gelu_apprx_tanh), [R_pst], [R_ut])
                    S.dma("pool", ygT_d[s, :, :, tb * 128:(tb + 1) * 128].rearrange("c p t -> p c t"), ut[:], [R_ut],
                          [R_scr["ygT"]], R_ut)
            S.barrier()

    if STOP_AFTER == '2':
        es.close()
        return nc
    if DBG_ATT_IDENT:
        with ExitStack() as st:
            vb = sb(st, "vb", [128, 1024], BF16)
            R_vb = S.res("vb")
            azb = sb(st, "azb", [128, 8, 128], BF16)
            R_azb = S.res("azb")
            vt = sb(st, "vt", [128, 8, 128], BF16)
            R_vt = S.res("vt")
            pst = ps(st, "pst", [128, 1024], BF16)
            R_pst = S.res("pst")
            for s in range(NSEG):
                for tb in range(SEG // 128):
                    S.dma("sp", vb[:], v_d[s, tb * 128:(tb + 1) * 128, :], [R_scr["v"]], [R_vb], R_vb)
                    S.dma("sp", azb[:], azT_d[s, :, :, tb * 128:(tb + 1) * 128].rearrange("c p t -> p c t"),
                          [R_scr["azT"]], [R_azb], R_azb)

                    def tr(e):
                        ins = None
                        for c in range(8):
                            ins = e.transpose(out=pst[:, c * 128:(c + 1) * 128], in_=vb[:, c * 128:(c + 1) * 128],
                                              identity=identb[:])
                        return ins
                    S.op("pe", tr, [R_vb], [R_pst])
                    S.op("dve", lambda e: e.tensor_tensor(out=vt[:], in0=pst[:].rearrange("p (c t) -> p c t", c=8),
                                                          in1=azb[:], op=ALU.mult), [R_pst, R_azb], [R_vt])
                    S.dma("pool", ozT_d[s, :, :, tb * 128:(tb + 1) * 128].rearrange("c p t -> p c t"), vt[:], [R_vt],
                          [R_scr["ozT"]], R_vt)
            S.barrier()

    if STOP_AFTER == '3':
        es.close()
        return nc
    with ExitStack() as st:
        mk_slabs(st)
        ygT = sb(st, "ygT", [128, 8, TT], BF16)
        szT = sb(st, "szT", [128, 8, TT], BF16)
        ozT = sb(st, "ozT", [128, 8, TT], BF16)
        gsT = sb(st, "gsT", [128, 16, TT], BF16)
        gaT = sb(st, "gaT", [128, 16, TT], BF16)
        R_yg, R_sz, R_oz, R_gs, R_ga = (S.res(n) for n in ("ygT", "szT", "ozT", "gsT", "gaT"))
        R_szc = [S.res("szc") for _ in range(8)]
        R_gsc = [S.res("gsc") for _ in range(16)]
        xh = sb(st, "xh", [128, 4, D], F32)
        R_xh = [S.res("xh") for _ in range(4)]
        pin = sb(st, "pin", [128, 4, 256], F32)
        pinb = sb(st, "pinb", [128, 4, 256], BF16)
        R_pin = S.res("pin")
        R_pinb = S.res("pinb")
        pT = sb(st, "pT", [128, 2, TT], BF16)
        R_pT = S.res("pT")
        fing = sb(st, "fing", [128, D], F32)
        R_fing = S.res("fing")
        S.dma("sp", fing[:], fing_in[:, :], [], [R_fing], R_fing)
        hnb = sb(st, "hnb", [128, 4, D], BF16)
        R_hnb = [S.res("hnb") for _ in range(4)]
        hn2T = sb(st, "hn2T", [128, 16, TT], BF16)
        R_hn2T = [S.res("hn2T") for _ in range(16)]
        junk = sb(st, "junk4", [128, D], BF16)
        R_junk = S.res("junk4")
        ss = sb(st, "ss4", [128, 16], F32)
        R_ss = S.res("ss4")
        sig = sb(st, "sig", [128, 512], BF16)
        R_sig = S.res("sig")
        tmpf = [sb(st, "tmpf%d" % i, [128, 512], F32) for i in range(2)]
        R_tmpf = [S.res("tmpf") for _ in range(2)]
        tmpg = [sb(st, "tmpg%d" % i, [128, 512], F32) for i in range(2)]
        R_tmpg = [S.res("tmpg") for _ in range(2)]
        psT2 = [ps(st, "psT40", [128, 1024], BF16), ps(st, "psT41", [128, 1024], BF16)]
        R_psT = [S.res("psT0"), S.res("psT1")]
        psA = [ps(st, "psB%d" % i, [128, 512], F32) for i in range(6)]
        R_psA = [S.res("psB%d" % i) for i in range(6)]
        pa_ctr = 0

        def nextps():
            nonlocal pa_ctr
            i = pa_ctr % 6
            pa_ctr += 1
            return i

        def tile_gen(s, t):
            tok0 = t * TT
            tsl = slice(tok0, tok0 + TT)
            def act_load(buf, dsrc, rr, rn):
                extra = R_szc if buf is szT else (R_gsc if buf is gsT else [])
                S.dma("sp", buf[:], dsrc[s, :, :, tsl].rearrange("c p t -> p c t"), [R_scr[rn]], [rr] + extra, rr)
            act_load(ygT, ygT_d, R_yg, "ygT")
            act_load(szT, szT_d, R_sz, "szT")
            act_load(ozT, ozT_d, R_oz, "ozT")
            for j in range(2):
                slab, rslab = load_slab(wb_glu, R_w["glu"], 8, j * 512)
                for m in range(4):
                    fo = j * 4 + m
                    pi = nextps()

                    def mm(e, m=m, pi=pi, slab=slab):
                        ins = None
                        for kc in range(8):
                            ins = e.matmul(psA[pi][:], lhsT=slab[:, kc, m * 128:(m + 1) * 128], rhs=ygT[:, kc, :],
                                           start=(kc == 0), stop=(kc == 7))
                        return ins
                    S.op("pe", mm, [rslab, R_yg], [R_psA[pi]])
                    S.op("act", lambda e, pi=pi, fo=fo: e.activation(out=sig[:], in_=psA[pi][:], func=AF.Sigmoid,
                                                                     bias=glub[:, fo:fo + 1]), [R_psA[pi]], [R_sig])
                    S.op("dve", lambda e, fo=fo: e.tensor_tensor(out=sig[:], in0=sig[:], in1=ygT[:, fo, :], op=ALU.mult),
                         [R_sig, R_yg], [R_sig])
                    S.op("dve", lambda e, fo=fo: e.tensor_tensor(out=szT[:, fo, :], in0=sig[:], in1=szT[:, fo, :],
                                                                 op=ALU.mult), [R_sig, R_sz], [R_szc[fo]])
            act_load(gaT, gaT_d, R_ga, "gaT")
            act_load(gsT, gsT_d, R_gs, "gsT")
            for j in range(4):
                slab_s, rs_s = load_slab(wb_s, R_w["s"], 8, j * 512)
                slab_a, rs_a = load_slab(wb_a, R_w["a"], 8, j * 512)
                for m in range(4):
                    dm = j * 4 + m
                    p1 = nextps()
                    p2 = nextps()

                    def mm1(e, m=m, p1=p1, slab=slab_s):
                        ins = None
                        for kc in range(8):
                            ins = e.matmul(psA[p1][:], lhsT=slab[:, kc, m * 128:(m + 1) * 128], rhs=szT[:, kc, :],
                                           start=(kc == 0), stop=(kc == 7))
                        return ins

                    def mm2(e, m=m, p2=p2, slab=slab_a):
                        ins = None
                        for kc in range(8):
                            ins = e.matmul(psA[p2][:], lhsT=slab[:, kc, m * 128:(m + 1) * 128], rhs=ozT[:, kc, :],
                                           start=(kc == 0), stop=(kc == 7))
                        return ins
                    S.op("pe", mm1, [rs_s] + R_szc, [R_psA[p1]])
                    S.op("pe", mm2, [rs_a, R_oz], [R_psA[p2]])
                    S.op("dve", lambda e, dm=dm, p1=p1: e.tensor_tensor(out=tmpf[0][:], in0=psA[p1][:], in1=gsT[:, dm, :],
                                                                        op=ALU.mult), [R_psA[p1], R_gs], [R_tmpf[0]])
                    S.op("dve", lambda e, dm=dm, p2=p2: e.tensor_tensor(out=tmpf[1][:], in0=psA[p2][:], in1=gaT[:, dm, :],
                                                                        op=ALU.mult), [R_psA[p2], R_ga], [R_tmpf[1]])
                    S.op("pool", lambda e, dm=dm: e.tensor_tensor(out=gsT[:, dm, :], in0=tmpf[0][:], in1=tmpf[1][:],
                                                                  op=ALU.add), [R_tmpf[0], R_tmpf[1], R_gs], [R_gsc[dm]])
            yield
            pre_slabs = [load_slab(wb_o, R_w["o"], 16, jj * 512) for jj in range(3)]
            S.dma("sp", pin[:], p_in[s, tsl, :].rearrange("(a p) d -> p a d", p=128), [], [R_pin], R_pin)
            for a in range(4):
                S.dma("sp", xh[:, a, :], x_in[s, tok0 + a * 128:tok0 + (a + 1) * 128, :], [], [R_xh[a]], R_xh[a])
            for j in range(4):
                if j == 1:
                    pre_slabs.append(load_slab(wb_o, R_w["o"], 16, 3 * 512))
                slab, rslab = pre_slabs[j]
                for a in range(4):
                    pi = nextps()

                    def mm(e, a=a, pi=pi, slab=slab):
                        ins = None
                        for kc in range(16):
                            ins = e.matmul(psA[pi][:], lhsT=gsT[:, kc, a * 128:(a + 1) * 128], rhs=slab[:, kc, :],
                                           start=(kc == 0), stop=(kc == 15))
                        return ins
                    S.op("pe", mm, [rslab] + R_gsc, [R_psA[pi]])
                    S.op("dve", lambda e, a=a, j=j, pi=pi: e.tensor_tensor(
                        out=xh[:, a, j * 512:(j + 1) * 512], in0=psA[pi][:], in1=xh[:, a, j * 512:(j + 1) * 512],
                        op=ALU.add), [R_psA[pi], R_xh[a]], [R_xh[a]])
            yield
            for a in range(4):
                S.op("act", lambda e, a=a: e.activation(out=junk[:], in_=xh[:, a, :], func=AF.Square,
                                                        accum_out=ss[:, a:a + 1]), [R_xh[a]], [R_junk, R_ss])
            S.op("dve", lambda e: e.tensor_scalar(out=ss[:, 4:8], in0=ss[:, 0:4], scalar1=1.0 / D, scalar2=EPS,
                                                  op0=ALU.mult, op1=ALU.add), [R_ss], [R_ss])
            S.op("act", lambda e: e.activation(out=ss[:, 4:8], in_=ss[:, 4:8], func=AF.Sqrt), [R_ss], [R_ss])
            S.op("dve", lambda e: e.reciprocal(out=ss[:, 4:8], in_=ss[:, 4:8]), [R_ss], [R_ss])
            for a in range(4):
                if a % 2 == 0:
                    S.op("dve", lambda e, a=a: e.tensor_scalar(out=hnb[:, a, :], in0=xh[:, a, :], scalar1=ss[:, 4 + a:5 + a],
                                                               scalar2=None, op0=ALU.mult), [R_xh[a], R_ss], [R_hnb[a]])
                else:
                    S.op("act", lambda e, a=a: e.activation(out=hnb[:, a, :], in_=xh[:, a, :], func=AF.Copy,
                                                            scale=ss[:, 4 + a:5 + a]), [R_xh[a], R_ss], [R_hnb[a]])
            for kc in range(16):
                h = kc % 2

                def tr(e, kc=kc, h=h):
                    ins = None
                    for a in range(4):
                        ins = e.transpose(out=psT2[h][:, a * 128:(a + 1) * 128],
                                          in_=hnb[:, a, kc * 128:(kc + 1) * 128], identity=identb[:])
                    return ins
                S.op("pe", tr, R_hnb, [R_psT[h]])
                S.op("dve", lambda e, kc=kc, h=h: e.tensor_scalar(out=hn2T[:, kc, :], in0=psT2[h][:, 0:512],
                                                                   scalar1=pleg[:, kc:kc + 1], scalar2=None, op0=ALU.mult),
                     [R_psT[h]], [R_hn2T[kc]])
            S.op("act", lambda e: e.activation(out=pinb[:], in_=pin[:], func=AF.Copy), [R_pin], [R_pinb])
            for kc in range(2):
                h = kc % 2

                def tr(e, kc=kc, h=h):
                    ins = None
                    for a in range(4):
                        ins = e.transpose(out=psT2[h][:, a * 128:(a + 1) * 128],
                                          in_=pinb[:, a, kc * 128:(kc + 1) * 128], identity=identb[:])
                    return ins
                S.op("pe", tr, [R_pinb], [R_psT[h]])
                S.op("dve", lambda e, kc=kc, h=h: e.tensor_copy(out=pT[:, kc, :], in_=psT2[h][:, 0:512]),
                     [R_psT[h]], [R_pT])
            for j in range(4):
                slab_g, rs_g = load_slab(wb_pg, R_w["pg"], 16, j * 512)
                slab_p, rs_p = load_slab(wb_pp, R_w["pp"], 2, j * 512)
                for a in range(4):
                    p1 = nextps()
                    p2 = nextps()

                    def mm1(e, a=a, p1=p1, slab=slab_g):
                        ins = None
                        for kc in range(16):
                            ins = e.matmul(psA[p1][:], lhsT=hn2T[:, kc, a * 128:(a + 1) * 128], rhs=slab[:, kc, :],
                                           start=(kc == 0), stop=(kc == 15))
                        return ins

                    def mm2(e, a=a, p2=p2, slab=slab_p):
                        ins = None
                        for kc in range(2):
                            ins = e.matmul(psA[p2][:], lhsT=pT[:, kc, a * 128:(a + 1) * 128], rhs=slab[:, kc, :],
                                           start=(kc == 0), stop=(kc == 1))
                        return ins
                    S.op("pe", mm1, [rs_g] + R_hn2T, [R_psA[p1]])
                    S.op("pe", mm2, [rs_p, R_pT], [R_psA[p2]])
                    S.op("act", lambda e, p1=p1: e.activation(out=tmpg[0][:], in_=psA[p1][:], func=AF.Sigmoid),
                         [R_psA[p1]], [R_tmpg[0]])
                    S.op("dve", lambda e, p2=p2: e.tensor_tensor(out=tmpg[1][:], in0=psA[p2][:], in1=tmpg[0][:],
                                                                 op=ALU.mult), [R_psA[p2], R_tmpg[0]], [R_tmpg[1]])
                    S.op("pool", lambda e, a=a, j=j: e.tensor_tensor(
                        out=xh[:, a, j * 512:(j + 1) * 512], in0=tmpg[1][:], in1=xh[:, a, j * 512:(j + 1) * 512],
                        op=ALU.add), [R_tmpg[1], R_xh[a]], [R_xh[a]])
            for a in range(4):
                S.op("act", lambda e, a=a: e.activation(out=junk[:], in_=xh[:, a, :], func=AF.Square,
                                                        accum_out=ss[:, 8 + a:9 + a]), [R_xh[a]], [R_junk, R_ss])
            S.op("dve", lambda e: e.tensor_scalar(out=ss[:, 12:16], in0=ss[:, 8:12], scalar1=1.0 / D, scalar2=EPS,
                                                  op0=ALU.mult, op1=ALU.add), [R_ss], [R_ss])
            S.op("act", lambda e: e.activation(out=ss[:, 12:16], in_=ss[:, 12:16], func=AF.Sqrt), [R_ss], [R_ss])
            S.op("dve", lambda e: e.reciprocal(out=ss[:, 12:16], in_=ss[:, 12:16]), [R_ss], [R_ss])
            for a in range(4):
                S.op("dve", lambda e, a=a: e.scalar_tensor_tensor(out=xh[:, a, :], in0=xh[:, a, :],
                                                                  scalar=ss[:, 12 + a:13 + a], in1=fing[:],
                                                                  op0=ALU.mult, op1=ALU.mult),
                     [R_xh[a], R_ss, R_fing], [R_xh[a]])
                S.dma("pool", y_out[s, tok0 + a * 128:tok0 + (a + 1) * 128, :], xh[:, a, :], [R_xh[a]], [], R_xh[a])

        tiles = [(s_, t_) for s_ in range(NSEG) for t_ in range(SEG // TT)]
        gens = [tile_gen(s_, t_) for (s_, t_) in tiles]
        next(gens[0])
        for i in range(len(tiles)):
            next(gens[i])
            if i + 1 < len(tiles):
                next(gens[i + 1])
            for _ in gens[i]:
                pass
        S.barrier()
    es.close()
    return nc


_NC_CACHE = {}


def _core_segments(i):
    if i < 4:
        return [("p", i, 0), ("p", i, SEG), ("s", i, 0)]
    b = 4 + 3 * (i - 4)
    return [("s", b, 0), ("s", b + 1, 0), ("s", b + 2, 0)]


def _bucket(rel):
    half = 16
    max_exact = 8
    ret = (rel > 0).astype(np.int32) * half
    n = np.abs(rel)
    nf = np.maximum(n, 1).astype(np.float32)
    large = max_exact + (np.log(nf / max_exact) / math.log(128 / max_exact) * (half - max_exact)).astype(np.int32)
    large = np.minimum(large, half - 1)
    return ret + np.where(n < max_exact, n, large)


def kernel(**inp):
    f32 = np.float32
    xp, xs_ = np.asarray(inp["x_prompt"], f32), np.asarray(inp["x_sample"], f32)
    pp, ps_ = np.asarray(inp["p_prompt"], f32)[0], np.asarray(inp["p_sample"], f32)[0]
    if "nc" not in _NC_CACHE:
        _NC_CACHE["nc"] = build_nc()
    nc = _NC_CACHE["nc"]

    def chunkcols(v, n):
        return np.ascontiguousarray(np.asarray(v, f32).reshape(n, 128).T)

    nn = np.arange(1536)
    oh = np.zeros((32, 1536), f32)
    oh[_bucket(767 - nn), nn] = 1.0
    lamv = np.stack([np.asarray(inp[k], f32)[0] for k in ("lam_q1", "lam_k1", "lam_q2", "lam_k2")])
    common = {
        "ident": np.eye(128, dtype=f32),
        "w_in": np.asarray(inp["w_in"], f32)[0],
        "glu_w": np.asarray(inp["glu_w"], f32)[0],
        "w_branch_s": np.asarray(inp["w_branch_s"], f32)[0],
        "w_branch_a": np.asarray(inp["w_branch_a"], f32)[0],
        "w_out": np.asarray(inp["w_out"], f32)[0],
        "ple_gate_w": np.asarray(inp["ple_gate_w"], f32)[0],
        "ple_proj_w": np.asarray(inp["ple_proj_w"], f32)[0],
        "norm_g": chunkcols(inp["norm_g"][0], 16),
        "ple_norm_g": chunkcols(inp["ple_norm_g"][0], 16),
        "glu_b": chunkcols(inp["glu_b"][0], 8),
        "subln_g": chunkcols(inp["subln_g"][0], 2),
        "final_g": np.ascontiguousarray(np.broadcast_to(np.asarray(inp["final_g"], f32)[None, :], (128, D))),
        "lamv": np.ascontiguousarray(np.broadcast_to(lamv[None], (128, 4, 128))),
        "rel_bias": np.asarray(inp["rel_bias"], f32),
        "onehot": oh,
    }
    def nmaj(a):
        return np.ascontiguousarray(np.asarray(a, f32)[0].transpose(2, 0, 1).reshape(64, 128))
    lr = nmaj(inp["ssm_lambda_re"]); li = nmaj(inp["ssm_lambda_im"])
    ldt = np.ascontiguousarray(np.broadcast_to(np.asarray(inp["ssm_log_dt"], f32)[0].reshape(1, 128), (64, 128)))
    br = np.asarray(inp["ssm_b_re"], f32)[0].transpose(2, 0, 1, 3).reshape(64, 128, 16)
    bim = np.asarray(inp["ssm_b_im"], f32)[0].transpose(2, 0, 1, 3).reshape(64, 128, 16)
    cr = np.asarray(inp["ssm_c_re"], f32)[0].transpose(3, 0, 1, 2).reshape(64, 128, 16)
    cim = np.asarray(inp["ssm_c_im"], f32)[0].transpose(3, 0, 1, 2).reshape(64, 128, 16)
    jj = np.arange(8, dtype=f32)
    kx = np.zeros((128, 128, 8), f32); ky = np.zeros((128, 128, 8), f32)
    kx[:, :64, :] = -jj; kx[:, 64:, :] = jj - 7.0
    ky[:, :64, :] = jj; ky[:, 64:, :] = 7.0 - jj
    pj = np.arange(128) // 16
    pc_ = np.arange(128) % 16
    dvec = np.asarray(inp["ssm_d"], f32)[0].reshape(64, 16)
    sel = np.zeros((128, 64, 128), f32)
    for j in range(8):
        for g8 in range(8):
            for co in range(16):
                sel[j * 16 + co, j * 8 + g8, g8 * 16 + co] = 1.0
    common.update({
        "s5_lr": np.concatenate([lr, lr]), "s5_li": np.concatenate([li, li]), "s5_ldt": np.concatenate([ldt, ldt]),
        "s5_ba": np.concatenate([br, bim]), "s5_bb": np.concatenate([bim, br]),
        "s5_ca": np.concatenate([cr, cim]), "s5_cb": np.concatenate([cim, cr]),
        "s5_kx": kx, "s5_ky": ky,
        "s5_dcol": np.ascontiguousarray(dvec.T[pc_, :]),
        "s5_maskl": (pj[None, :] >= pj[:, None]).astype(f32),
        "s5_masku": (pj[None, :] <= pj[:, None]).astype(f32),
        "s5_sel": sel,
    })
    in_maps = []
    for i in range(8):
        segs = _core_segments(i)
        xs = np.stack([(xp if g == "p" else xs_)[b, st:st + SEG] for (g, b, st) in segs])
        pc = np.stack([(pp if g == "p" else ps_)[b, st:st + SEG] for (g, b, st) in segs])
        fl = np.zeros((128, 2), f32)
        fl[:, 0] = 1.0 if i < 4 else 0.0
        fl[:, 1] = 0.0 if i < 4 else -30000.0
        m = dict(common)
        m.update({"x": np.ascontiguousarray(xs), "p": np.ascontiguousarray(pc), "flags": fl})
        in_maps.append(m)
    res = run_bass_kernel_spmd(nc, in_maps, core_ids=list(range(8)))
    y_p = np.zeros_like(xp)
    y_s = np.zeros_like(xs_)
    for i in range(8):
        y = res.results[i]["y"]
        for j, (g, b, st) in enumerate(_core_segments(i)):
            (y_p if g == "p" else y_s)[b, st:st + SEG] = y[j]
    return (y_p, y_s)
```
